# Optimizing a Trainium2 kernel written in Bass

```python
import jax, jax.numpy as jnp
from jax import lax
import numpy as np

D_MODEL = 1024
BATCH = 8
SEQ = 4096
DEPTH = 4

GRID_W = 64
CTX_LEN = 256
HG_HEADS = 4
HG_DK = 128
HG_DV = 128
HG_KW = HG_HEADS * HG_DK
HG_W = HG_HEADS * HG_DV
HG_CHUNK = 64
NA_HEADS = 8
NA_HD = 64
NA_W = NA_HEADS * NA_HD
NA_WIN_R = 8
NA_WIN_C = 16
ROPE_THETA = 10000.0
SC_W = 512
SC_K = 3
D_FF = 4 * D_MODEL
N_MOD = 6
EPS = 1e-6
SPLIT_SIZES = (HG_KW, HG_KW, HG_KW, HG_W, HG_W, NA_W, NA_W, NA_W, SC_W, SC_W, SC_W, D_MODEL, D_MODEL, D_MODEL)
PROJ_W = sum(SPLIT_SIZES)
SPLIT_POINTS = tuple(sum(SPLIT_SIZES[:i + 1]) for i in range(len(SPLIT_SIZES) - 1))

kernel_name = "hybrid_hgrn2_natten_shortconv_dit_block"


def _rms_norm(x, w):
    xf = x.astype(jnp.float32)
    y = xf * lax.rsqrt(jnp.mean(jnp.square(xf), axis=-1, keepdims=True) + EPS)
    return (y * w.astype(jnp.float32)).astype(x.dtype)


def _modulate(x, w, shift, scale):
    return _rms_norm(x, w) * (1.0 + scale) + shift


def _axial_rope_tables(n_tokens):
    t = jnp.arange(n_tokens)
    pos = jnp.stack([t // GRID_W, t % GRID_W], axis=-1).astype(jnp.float32)
    half = NA_HD // 2
    inv = ROPE_THETA ** (-jnp.arange(0, half, 2, dtype=jnp.float32) / half)
    ang = pos[:, :, None] * inv
    return jnp.cos(ang), jnp.sin(ang)


def _apply_axial_rope(x, cos, sin):
    B_, S_, H, d = x.shape
    xa = x.astype(jnp.float32).reshape(B_, S_, H, 2, 2, d // 4)
    x1, x2 = xa[..., 0, :], xa[..., 1, :]
    cs, sn = cos[:, None], sin[:, None]
    out = jnp.stack([x1 * cs - x2 * sn, x2 * cs + x1 * sn], axis=-2)
    return out.reshape(B_, S_, H, d).astype(x.dtype)


def _gla_chunk_scan(q, k, v, log_f, s0):
    B_, L, H, _ = q.shape
    n = L // HG_CHUNK

    def to_chunks(a):
        return a.reshape(B_, n, HG_CHUNK, H, a.shape[-1]).transpose(1, 0, 3, 2, 4)

    qc, kc, vc, gc = (to_chunks(a) for a in (q, k, v, log_f))
    incl = jnp.tril(jnp.ones((HG_CHUNK, HG_CHUNK), dtype=bool))

    def step(s, inp):
        qi, ki, vi, gi = inp
        G = jnp.cumsum(gi, axis=2)
        inter = jnp.einsum('bhtk,bhkv->bhtv', qi * jnp.exp(G), s)
        diff = G[:, :, :, None, :] - G[:, :, None, :, :]
        decay = jnp.exp(jnp.where(incl[:, :, None], diff, -jnp.inf))
        attn = jnp.einsum('bhtk,bhtsk,bhsk->bhts', qi, decay, ki)
        intra = jnp.einsum('bhts,bhsv->bhtv', attn, vi)
        G_last = G[:, :, -1:, :]
        s_new = jnp.exp(G_last[:, :, 0, :])[..., None] * s + jnp.einsum(
            'bhsk,bhsv->bhkv', ki * jnp.exp(G_last - G), vi)
        return s_new, inter + intra

    s_fin, out = lax.scan(step, s0, (qc, kc, vc, gc))
    out = out.transpose(1, 0, 3, 2, 4).reshape(B_, L, H, v.shape[-1])
    return out, s_fin


def _hgrn2_direction(q_raw, f_raw, i_raw, lb, s0, reverse):
    B_, L, _ = q_raw.shape

    def heads(a):
        return a.astype(jnp.float32).reshape(B_, L, HG_HEADS, -1)

    q = heads(jax.nn.silu(q_raw.astype(jnp.float32)))
    g = lb + (1.0 - lb) * jax.nn.sigmoid(f_raw.astype(jnp.float32))
    k = heads(1.0 - g)
    log_f = heads(jnp.log(g))
    v = heads(i_raw)
    if reverse:
        q, k, v, log_f = (jnp.flip(a, axis=1) for a in (q, k, v, log_f))
    o, s = _gla_chunk_scan(q, k, v, log_f, s0)
    if reverse:
        o = jnp.flip(o, axis=1)
    return o, s


def _hgrn2_readout(o, g_raw, w):
    B_, L, _ = g_raw.shape
    y = _rms_norm(o, w).reshape(B_, L, HG_W) * jax.nn.silu(g_raw.astype(jnp.float32))
    return y.astype(g_raw.dtype)


def _hgrn2_mixer(z_lat, z_ctx, lb_f, lb_b, onorm_w, need_ctx):
    q_l, ff_l, fb_l, i_l, g_l = z_lat
    q_c, ff_c, fb_c, i_c, g_c = z_ctx
    s0 = jnp.zeros((q_l.shape[0], HG_HEADS, HG_DK, HG_DV), jnp.float32)
    oc_f, s_f = _hgrn2_direction(q_c, ff_c, i_c, lb_f, s0, False)
    ol_f, _ = _hgrn2_direction(q_l, ff_l, i_l, lb_f, s_f, False)
    oc_b, s_b = _hgrn2_direction(q_c, fb_c, i_c, lb_b, s0, True)
    ol_b, _ = _hgrn2_direction(q_l, fb_l, i_l, lb_b, s_b, True)
    y_lat = _hgrn2_readout(ol_f + ol_b, g_l, onorm_w)
    y_ctx = _hgrn2_readout(oc_f + oc_b, g_c, onorm_w) if need_ctx else None
    return y_lat, y_ctx


def _natten_mixer(z_lat, z_ctx, qn_w, kn_w, rpb, rope, need_ctx):
    q_l, k_l, v_l = z_lat
    q_c, k_c, v_c = z_ctx
    B_, S_, _ = q_l.shape
    rows = S_ // GRID_W
    wr = min(NA_WIN_R, rows)
    scale = NA_HD ** -0.5
    cos, sin = rope

    def heads(a):
        return a.reshape(a.shape[0], a.shape[1], NA_HEADS, NA_HD)

    ql = _apply_axial_rope(_rms_norm(heads(q_l), qn_w), cos, sin)
    kl = _apply_axial_rope(_rms_norm(heads(k_l), kn_w), cos, sin)
    vl = heads(v_l)
    kc_h = _rms_norm(heads(k_c), kn_w).transpose(0, 2, 1, 3)
    vc_h = heads(v_c).transpose(0, 2, 1, 3)

    q_rows = ql.reshape(B_, rows, GRID_W, NA_HEADS, NA_HD).transpose(1, 0, 3, 2, 4)
    k_grid = kl.reshape(B_, rows, GRID_W, NA_HEADS, NA_HD).transpose(0, 3, 1, 2, 4)
    v_grid = vl.reshape(B_, rows, GRID_W, NA_HEADS, NA_HD).transpose(0, 3, 1, 2, 4)

    cols = np.arange(GRID_W)
    col_start = np.clip(cols - NA_WIN_C // 2, 0, GRID_W - NA_WIN_C)
    col_idx = col_start[:, None] + np.arange(NA_WIN_C)
    col_bias_ix = col_idx - cols[:, None] + (NA_WIN_C - 1)
    n_loc = wr * NA_WIN_C

    def row_attn(args):
        r, q_r = args
        r0 = jnp.clip(r - wr // 2, 0, rows - wr)
        k_band = lax.dynamic_slice_in_dim(k_grid, r0, wr, axis=2)
        v_band = lax.dynamic_slice_in_dim(v_grid, r0, wr, axis=2)
        k_win = k_band[:, :, :, col_idx]
        v_win = v_band[:, :, :, col_idx]
        s_loc = jnp.einsum('bhqd,bhrqcd->bhqrc', q_r, k_win) * scale
        row_bias_ix = r0 + jnp.arange(wr) - r + (NA_WIN_R - 1)
        bias = rpb[:, row_bias_ix[None, :, None], col_bias_ix[:, None, :]]
        s_loc = s_loc + bias[None]
        s_ctx = jnp.einsum('bhqd,bhkd->bhqk', q_r, kc_h) * scale
        logits = jnp.concatenate([s_loc.reshape(B_, NA_HEADS, GRID_W, n_loc), s_ctx], axis=-1)
        p = jax.nn.softmax(logits.astype(jnp.float32), axis=-1).astype(v_win.dtype)
        p_loc = p[..., :n_loc].reshape(B_, NA_HEADS, GRID_W, wr, NA_WIN_C)
        p_ctx = p[..., n_loc:]
        return (jnp.einsum('bhqrc,bhrqcd->bhqd', p_loc, v_win)
                + jnp.einsum('bhqk,bhkd->bhqd', p_ctx, vc_h))

    o = lax.map(row_attn, (jnp.arange(rows), q_rows))
    y_lat = o.transpose(1, 0, 3, 2, 4).reshape(B_, S_, NA_W)

    y_ctx = None
    if need_ctx:
        qc_h = _rms_norm(heads(q_c), qn_w).transpose(0, 2, 1, 3)
        sc = jnp.einsum('bhqd,bhkd->bhqk', qc_h, kc_h) * scale
        pc = jax.nn.softmax(sc.astype(jnp.float32), axis=-1).astype(vc_h.dtype)
        oc = jnp.einsum('bhqk,bhkd->bhqd', pc, vc_h)
        y_ctx = oc.transpose(0, 2, 1, 3).reshape(q_c.shape[0], q_c.shape[1], NA_W)
    return y_lat, y_ctx


def _short_conv(b_gate, c_gate, x_in, w):
    u = c_gate * x_in
    up = jnp.pad(u, ((0, 0), (1, 1), (0, 0)))
    y = up[:, :-2] * w[0] + up[:, 1:-1] * w[1] + up[:, 2:] * w[2]
    return b_gate * y


def _branch_merge(ya, yb, yc, gates, wa, wb, wc, wo):
    ga, gb, gc = (jax.nn.sigmoid(g) for g in gates)
    mix = ga * (ya @ wa) + gb * (yb @ wb) + gc * (yc @ wc)
    return mix @ wo


def _sq_relu_mlp(h, w1, w2):
    return jnp.square(jax.nn.relu(h @ w1)) @ w2


def setup_inputs(seed: int = 0) -> dict:
    key = jax.random.key(seed)
    ks = jax.random.split(key, 24)

    def nrm(k, shape, scale):
        return jax.random.normal(k, shape, jnp.float32) * scale

    D = D_MODEL
    return {
        "x": nrm(ks[0], (BATCH, SEQ, D), 1.0),
        "c": nrm(ks[1], (BATCH, D), 1.0),
        "ctx": nrm(ks[2], (BATCH, CTX_LEN, D), 1.0),
        "c_ctx": nrm(ks[3], (D,), 1.0),
        "ada_w": nrm(ks[4], (DEPTH, D, N_MOD * D), 0.25 * D ** -0.5),
        "ada_b": nrm(ks[5], (DEPTH, N_MOD * D), 0.02),
        "norm1_w": 1.0 + nrm(ks[6], (DEPTH, D), 0.02),
        "norm2_w": 1.0 + nrm(ks[7], (DEPTH, D), 0.02),
        "w_in": nrm(ks[8], (DEPTH, D, PROJ_W), D ** -0.5),
        "hgrn_lb_logits": nrm(ks[9], (2, DEPTH, HG_KW), 0.5),
        "hgrn_onorm_w": 1.0 + nrm(ks[10], (DEPTH, HG_DV), 0.02),
        "q_norm_w": 1.0 + nrm(ks[11], (DEPTH, NA_HD), 0.02),
        "k_norm_w": 1.0 + nrm(ks[12], (DEPTH, NA_HD), 0.02),
        "natten_rpb": nrm(ks[13], (DEPTH, NA_HEADS, 2 * NA_WIN_R - 1, 2 * NA_WIN_C - 1), 0.1),
        "conv_w": nrm(ks[14], (DEPTH, SC_K, SC_W), SC_K ** -0.5),
        "w_branch_a": nrm(ks[15], (DEPTH, HG_W, D), HG_W ** -0.5),
        "w_branch_b": nrm(ks[16], (DEPTH, NA_W, D), NA_W ** -0.5),
        "w_branch_c": nrm(ks[17], (DEPTH, SC_W, D), SC_W ** -0.5),
        "w_out": nrm(ks[18], (DEPTH, D, D), D ** -0.5),
        "mlp_w1": nrm(ks[19], (DEPTH, D, D_FF), D ** -0.5),
        "mlp_w2": nrm(ks[20], (DEPTH, D_FF, D), D_FF ** -0.5),
    }


def reference(x, c, ctx, c_ctx, ada_w, ada_b, norm1_w, norm2_w, w_in, hgrn_lb_logits,
              hgrn_onorm_w, q_norm_w, k_norm_w, natten_rpb, conv_w, w_branch_a, w_branch_b,
              w_branch_c, w_out, mlp_w1, mlp_w2):
    S_ = x.shape[1]
    rope = _axial_rope_tables(S_)
    lb_p = jax.nn.softmax(hgrn_lb_logits.astype(jnp.float32), axis=1)
    lower_bounds = jnp.cumsum(lb_p, axis=1) - lb_p[:, :1]
    silu_c = jax.nn.silu(c)
    silu_cc = jax.nn.silu(c_ctx)
    h_ctx = ctx
    for l in range(DEPTH):
        need_ctx = l < DEPTH - 1
        mod_lat = jnp.split((silu_c @ ada_w[l] + ada_b[l])[:, None, :], N_MOD, axis=-1)
        mod_ctx = jnp.split(silu_cc @ ada_w[l] + ada_b[l], N_MOD, axis=-1)
        z_lat = jnp.split(_modulate(x, norm1_w[l], mod_lat[0], mod_lat[1]) @ w_in[l], SPLIT_POINTS, axis=-1)
        z_ctx = jnp.split(_modulate(h_ctx, norm1_w[l], mod_ctx[0], mod_ctx[1]) @ w_in[l], SPLIT_POINTS, axis=-1)

        ya_lat, ya_ctx = _hgrn2_mixer(z_lat[0:5], z_ctx[0:5], lower_bounds[0, l], lower_bounds[1, l],
                                      hgrn_onorm_w[l], need_ctx)
        yb_lat, yb_ctx = _natten_mixer(z_lat[5:8], z_ctx[5:8], q_norm_w[l], k_norm_w[l],
                                       natten_rpb[l], rope, need_ctx)
        yc_lat = _short_conv(z_lat[8], z_lat[9], z_lat[10], conv_w[l])

        x = x + mod_lat[2] * _branch_merge(ya_lat, yb_lat, yc_lat, z_lat[11:14], w_branch_a[l],
                                           w_branch_b[l], w_branch_c[l], w_out[l])
        x = x + mod_lat[5] * _sq_relu_mlp(_modulate(x, norm2_w[l], mod_lat[3], mod_lat[4]),
                                          mlp_w1[l], mlp_w2[l])
        if need_ctx:
            yc_ctx = _short_conv(z_ctx[8], z_ctx[9], z_ctx[10], conv_w[l])
            h_ctx = h_ctx + mod_ctx[2] * _branch_merge(ya_ctx, yb_ctx, yc_ctx, z_ctx[11:14], w_branch_a[l],
                                                       w_branch_b[l], w_branch_c[l], w_out[l])
            h_ctx = h_ctx + mod_ctx[5] * _sq_relu_mlp(_modulate(h_ctx, norm2_w[l], mod_ctx[3], mod_ctx[4]),
                                                      mlp_w1[l], mlp_w2[l])
    return x
```

```python
import numpy as np
import ml_dtypes
from contextlib import ExitStack
import concourse.bass as bass
import concourse.mybir as mybir
from concourse.bass_utils import run_bass_kernel_spmd

F32 = mybir.dt.float32
BF16 = mybir.dt.bfloat16
ALU = mybir.AluOpType
AF = mybir.ActivationFunctionType

D = 1024
L = 4
TC = 256
TL = 4096
T = TC + TL
NG = 8
PROJ = 8704
EPS = 1e-6
OFF = dict(hq=0, hff=512, hfb=1024, hi=1536, hg=2048, nq=2560, nk=3072, nv=3584,
           cb=4096, cc=4608, cx=5120, ga=5632, gb=6656, gc=7680)
BLOCKS = [(0, 256)] + [(256 + 512 * i, 512) for i in range(8)]
BLOCKS256 = [(256 * i, 256) for i in range(17)]
NEG = -30000.0

DBG = {"units": None, "step": None}


class _Stop(Exception):
    pass


def ck(k):
    if DBG["step"] is not None and DBG["step"] <= k:
        raise _Stop()


COMPUTE = ("pe", "act", "dve")
DMAQ = ("sp", "pq")
N_DMA_SEMS = 32


class Buf:
    __slots__ = ("name", "lw", "rd")

    def __init__(self, name=""):
        self.name = name
        self.lw = None
        self.rd = []


class Op:
    __slots__ = ("eng", "idx", "sem", "val", "uid")
    _n = [0]

    def __init__(self, eng, idx, sem, val):
        self.eng, self.idx, self.sem, self.val = eng, idx, sem, val
        Op._n[0] += 1
        self.uid = Op._n[0]


class Sched:
    def __init__(self, nc, stack):
        self.nc = nc
        self.E = {"pe": nc.tensor, "act": nc.scalar, "dve": nc.vector, "sp": nc.sync, "pq": nc.gpsimd}
        self.sems = {e: stack.enter_context(nc.semaphore("s_" + e)) for e in COMPUTE}
        self.dsems = [stack.enter_context(nc.semaphore("d%d" % i)) for i in range(N_DMA_SEMS)]
        self.count = {e: 0 for e in COMPUTE + DMAQ}
        self.known = {e: {} for e in COMPUTE + DMAQ}
        self.known_dma = {e: set() for e in COMPUTE + DMAQ}
        self.dma_count = 0
        self.dma_last = [None] * N_DMA_SEMS
        self.dma_total = [0] * N_DMA_SEMS
        self.last = {e: None for e in COMPUTE}
        self.ninst = 0

    def _need(self, eng, dep, waits):
        if dep is None:
            return
        if dep.eng in DMAQ:
            if dep.uid in self.known_dma[eng]:
                return
            self.known_dma[eng].add(dep.uid)
            waits.append(dep)
            return
        if dep.eng == eng and eng == "pe":
            return
        if self.known[eng].get(dep.eng, -1) >= dep.idx:
            return
        self.known[eng][dep.eng] = dep.idx
        waits.append(dep)

    def op(self, eng, fn, reads=(), writes=()):
        waits = []
        for b in reads:
            self._need(eng, b.lw, waits)
        for b in writes:
            self._need(eng, b.lw, waits)
            for r in b.rd:
                self._need(eng, r, waits)
        e = self.E[eng]
        if eng in DMAQ:
            k = self.dma_count % N_DMA_SEMS
            self.dma_count += 1
            prev = self.dma_last[k]
            if prev is not None and prev.uid not in self.known_dma[eng]:
                self.known_dma[eng].add(prev.uid)
                waits.append(prev)
            self.dma_total[k] += 16
            o = Op(eng, self.count[eng], self.dsems[k], self.dma_total[k])
            self.dma_last[k] = o
        else:
            o = Op(eng, self.count[eng], self.sems[eng], self.count[eng] + 1)
            self.last[eng] = o
        self.count[eng] += 1
        for d in waits:
            e.wait_ge(d.sem, d.val)
        ins = fn(e)
        ins.then_inc(o.sem, 16 if eng in DMAQ else 1)
        self.ninst += 1
        for b in reads:
            b.rd.append(o)
        for b in writes:
            b.lw = o
            b.rd = []
        return o

    def barrier(self):
        deps = [self.last[e] for e in COMPUTE if self.last[e] is not None]
        deps += [d for d in self.dma_last if d is not None]
        for eng in COMPUTE + DMAQ:
            e = self.E[eng]
            for d in deps:
                if d.eng == eng and eng in COMPUTE:
                    continue
                if d.eng in DMAQ:
                    if d.uid in self.known_dma[eng]:
                        continue
                    self.known_dma[eng].add(d.uid)
                else:
                    if self.known[eng].get(d.eng, -1) >= d.idx:
                        continue
                    self.known[eng][d.eng] = d.idx
                e.wait_ge(d.sem, d.val)

    def finish(self, final_ops):
        for d in final_ops:
            self.nc.sync.wait_ge(d.sem, d.val)


def build(n_layers=L, tap=None, stop=None):
    nc = bass.Bass("TRN2", target_bir_lowering=False)

    def din(name, shape, dt=F32):
        return nc.dram_tensor(name, list(shape), dt, kind="ExternalInput").ap()

    xT_in = din("xT", [NG, 128, T])
    cT_in = din("cT", [128, NG, 2])
    ada_w = din("ada_w", [L, D, 6 * D])
    ada_bT = din("ada_bT", [128, L, 48])
    nwT = din("nwT", [128, L, 2, NG])
    w_in = din("w_in", [L, D, PROJ])
    lbl = din("lbl", [128, 2, L, 4])
    onw = din("onw", [128, L])
    qkw = din("qkw", [128, L, 2])
    bias_tab = din("bias_tab", [L, 8, 128, 3 * 6 * 256])
    cwT = din("cwT", [128, L, 3, 4])
    w_ba = din("w_ba", [L, 512, D])
    w_bb = din("w_bb", [L, 512, D])
    w_bc = din("w_bc", [L, 512, D])
    w_out = din("w_out", [L, D, D])
    w1 = din("w1", [L, D, 4 * D])
    w2 = din("w2", [L, 4 * D, D])
    cos_in = din("cosT", [128, TL])
    sin_in = din("sinT", [128, TL])
    cst_in = din("cst", [128, 128 * 5])
    cst2_in = din("cst2", [128, 128])
    outT = nc.dram_tensor("outT", [NG, 128, TL], F32, kind="ExternalOutput").ap()
    tap_out = None
    if tap is not None:
        tap_out = nc.dram_tensor("tap", list(tap[1]), F32, kind="ExternalOutput").ap()

    def dscr(name, shape, dt):
        return nc.dram_tensor(name, list(shape), dt, kind="Internal").ap()

    xs = dscr("xs", [NG, 128, T], F32)
    ya_d = dscr("ya_d", [4, 128, T], BF16)
    yb_d = dscr("yb_d", [4, 128, T], BF16)
    yc_d = dscr("yc_d", [4, 128, T], BF16)
    of_d = dscr("of_d", [4, 128, T], BF16)

    final_ops = []
    with ExitStack() as top:
        S = Sched(nc, top)

        uid = [0]

        def sb(st, name, shape, dt):
            uid[0] += 1
            return st.enter_context(nc.sbuf_tensor("%s_s%d" % (name, uid[0]), list(shape), dt))

        def pst(st, name, shape, dt=F32):
            uid[0] += 1
            return st.enter_context(nc.psum_tensor("%s_p%d" % (name, uid[0]), list(shape), dt))

        def mm(out_ap, pairs, reads, wbuf, start=True, stop=True):
            n = len(pairs)

            def fn(e):
                ins = None
                for i, (l_, r_) in enumerate(pairs):
                    ins = e.matmul(out_ap, l_, r_, start=(start and i == 0), stop=(stop and i == n - 1))
                return ins
            return S.op("pe", fn, reads=reads, writes=[wbuf])

        def act(out, in_, func, reads, wbuf, scale=1.0, bias=None):
            if bias is None:
                return S.op("act", lambda e: e.activation(out=out, in_=in_, func=func, scale=scale),
                            reads=reads, writes=[wbuf])
            return S.op("act", lambda e: e.activation(out=out, in_=in_, func=func, scale=scale, bias=bias),
                        reads=reads, writes=[wbuf])

        def tt(out, a, b, op, reads, wbuf):
            return S.op("dve", lambda e: e.tensor_tensor(out=out, in0=a, in1=b, op=op), reads=reads, writes=[wbuf])

        def ts(out, a, s1, op0, reads, wbuf, s2=None, op1=None):
            if op1 is None:
                return S.op("dve", lambda e: e.tensor_scalar(out=out, in0=a, scalar1=s1, scalar2=None, op0=op0),
                            reads=reads, writes=[wbuf])
            return S.op("dve", lambda e: e.tensor_scalar(out=out, in0=a, scalar1=s1, scalar2=s2, op0=op0, op1=op1),
                        reads=reads, writes=[wbuf])

        def stt(out, a, s, b, op0, op1, reads, wbuf):
            return S.op("dve", lambda e: e.scalar_tensor_tensor(out=out, in0=a, scalar=s, in1=b, op0=op0, op1=op1),
                        reads=reads, writes=[wbuf])

        def dma(q, out, in_, reads, wbuf):
            return S.op(q, lambda e: e.dma_start(out=out, in_=in_), reads=reads, writes=[wbuf] if wbuf else [])

        P = ExitStack()
        top.enter_context(P)
        cst = sb(P, "cst", [128, 5, 128], BF16)
        maskb = sb(P, "maskb", [128, 128], BF16)
        cf = sb(P, "cf", [128, 8], F32)
        onesf = sb(P, "onesf", [128, 512], F32)
        modT = sb(P, "modT", [128, L, 48, 2], F32)
        nw = sb(P, "nw", [128, L, 2, NG], F32)
        av = sb(P, "av", [128, 2, 2, NG], F32)
        lbt = sb(P, "lbt", [128, 2, L, 4], F32)
        oml = sb(P, "oml", [128, 2, L, 4], F32)
        onwt = sb(P, "onwt", [128, L], F32)
        qkwt = sb(P, "qkwt", [128, L, 2], F32)
        cwt = sb(P, "cwt", [128, L, 3, 4], F32)
        bc = Buf("const")
        bmod = Buf("mod")
        bav = Buf("av")
        dma("pq", cst[:].rearrange("p a b -> p (a b)"), cst_in[:, :], [], bc)
        dma("pq", maskb[:], cst2_in[:, :], [], bc)
        dma("sp", nw[:].rearrange("p a b c -> p (a b c)"), nwT.rearrange("p a b c -> p (a b c)"), [], bc)
        dma("sp", lbt[:].rearrange("p a b c -> p (a b c)"), lbl.rearrange("p a b c -> p (a b c)"), [], bc)
        dma("sp", onwt[:], onw[:, :], [], bc)
        dma("sp", qkwt[:].rearrange("p a b -> p (a b)"), qkw.rearrange("p a b -> p (a b)"), [], bc)
        dma("sp", cwt[:].rearrange("p a b c -> p (a b c)"), cwT.rearrange("p a b c -> p (a b c)"), [], bc)
        S.op("dve", lambda e: e.memset(cf[:, 0:1], EPS), writes=[bc])
        S.op("dve", lambda e: e.memset(cf[:, 1:2], 1.0), writes=[bc])
        S.op("dve", lambda e: e.memset(cf[:, 2:3], 0.0), writes=[bc])
        S.op("dve", lambda e: e.memset(onesf[:], 1.0), writes=[bc])
        ones_b = cst[:, 0, :]
        ident = cst[:, 1, :]
        blockones = cst[:, 2, :]
        perm = cst[:, 3, :]
        maskf = cst[:, 4, :]
        eps_ap = cf[:, 0:1]
        one_ap = cf[:, 1:2]

        act(lbt[:].rearrange("p a b c -> p (a b c)"), lbt[:].rearrange("p a b c -> p (a b c)"), AF.Exp, [bc], bc)
        with ExitStack() as st:
            ssum = sb(st, "ssum", [128, 2, 4], F32)
            tt(ssum[:], lbt[:, :, 0, :], lbt[:, :, 1, :], ALU.add, [bc], bc)
            tt(ssum[:], ssum[:], lbt[:, :, 2, :], ALU.add, [bc], bc)
            tt(ssum[:], ssum[:], lbt[:, :, 3, :], ALU.add, [bc], bc)
            S.op("dve", lambda e: e.reciprocal(out=ssum[:], in_=ssum[:]), reads=[bc], writes=[bc])
            for j in range(L):
                tt(lbt[:, :, j, :], lbt[:, :, j, :], ssum[:], ALU.mult, [bc], bc)
            tt(lbt[:, :, 2, :], lbt[:, :, 2, :], lbt[:, :, 1, :], ALU.add, [bc], bc)
            tt(lbt[:, :, 3, :], lbt[:, :, 3, :], lbt[:, :, 2, :], ALU.add, [bc], bc)
            S.op("dve", lambda e: e.memset(lbt[:, :, 0, :], 0.0), writes=[bc])
            ts(oml[:].rearrange("p a b c -> p (a b c)"), lbt[:].rearrange("p a b c -> p (a b c)"), -1.0, ALU.mult,
               [bc], bc, 1.0, ALU.add)
            ts(qkwt[:, :, 0], qkwt[:, :, 0], 0.125, ALU.mult, [bc], bc)
            S.barrier()

        with ExitStack() as st:
            ct = sb(st, "ct", [128, NG, 2], F32)
            sc = sb(st, "sc", [128, NG, 2], F32)
            abt = sb(st, "abt", [128, L, 48], F32)
            wst = [sb(st, "wst%d" % i, [128, NG, 768], F32) for i in range(2)]
            bw = [Buf(), Buf()]
            mps = pst(st, "mps", [128, 512])
            bps = Buf()
            b0 = Buf()
            dma("sp", ct[:].rearrange("p a b -> p (a b)"), cT_in.rearrange("p a b -> p (a b)"), [], b0)
            dma("sp", abt[:].rearrange("p a b -> p (a b)"), ada_bT.rearrange("p a b -> p (a b)"), [], b0)
            ctf = ct[:].rearrange("p a b -> p (a b)")
            scf = sc[:].rearrange("p a b -> p (a b)")
            act(scf, ctf, AF.Exp, [b0], b0, scale=-1.0)
            ts(scf, scf, 1.0, ALU.add, [b0], b0)
            S.op("dve", lambda e: e.reciprocal(out=scf, in_=scf), reads=[b0], writes=[b0])
            tt(scf, scf, ctf, ALU.mult, [b0], b0)
            k = 0
            for l in range(n_layers):
                for s_ in range(8):
                    w = wst[k % 2]
                    dma("sp", w[:], ada_w[l, :, s_ * 768:(s_ + 1) * 768].rearrange("(kc p) c -> p kc c", p=128),
                        [], bw[k % 2])
                    for j in range(6):
                        grp = s_ * 6 + j
                        mm(mps[:, grp * 2:grp * 2 + 2],
                           [(w[:, kc, j * 128:(j + 1) * 128], sc[:, kc, :]) for kc in range(NG)],
                           [bw[k % 2], b0], bps)
                    k += 1
                tt(modT[:, l, :, :], mps[:, 0:96].rearrange("p (a b) -> p a b", b=2),
                   abt[:, l, :].unsqueeze(2).to_broadcast([128, 48, 2]), ALU.add, [bps, b0], bmod)
            S.barrier()

        bxs = [Buf("xs%d" % i) for i in range(17)]

        def xs_bufs(t0, nb):
            return [bxs[i] for i in range(t0 // 256, (t0 + nb) // 256)]

        with ExitStack() as st:
            cp = [sb(st, "cp%d" % i, [128, NG, 512], F32) for i in range(2)]
            bcp = [Buf(), Buf()]
            for bi, (t0, nb) in enumerate(BLOCKS):
                c_ = cp[bi % 2]
                dma("sp", c_[:, :, :nb], xT_in[:, :, t0:t0 + nb].rearrange("g p t -> p g t"), [], bcp[bi % 2])
                S.op("sp", lambda e: e.dma_start(out=xs[:, :, t0:t0 + nb].rearrange("g p t -> p g t"), in_=c_[:, :, :nb]),
                     reads=[bcp[bi % 2]], writes=xs_bufs(t0, nb))
            S.barrier()

        def rms_rstd(st_tmp, x_ap_g, ngrp, nb, sq, ss_ps, bsq, bss, rstd, brstd, lhs_ones, inv_n, xreads):
            for g in range(ngrp):
                act(sq[:, g, :nb], x_ap_g(g), AF.Square, xreads, bsq)
            mm(ss_ps[:, :nb], [(lhs_ones, sq[:, g, :nb]) for g in range(ngrp)], [bsq, bc], bss)
            act(rstd[:, :nb], ss_ps[:, :nb], AF.Ln, [bss, bc], brstd, scale=inv_n, bias=eps_ap)
            act(rstd[:, :nb], rstd[:, :nb], AF.Exp, [brstd], brstd, scale=-0.5)

        for l in range(n_layers):
            if stop == "P0":
                break
            for ni, (sci) in enumerate((1, 4)):
                for r in range(2):
                    S.op("dve", lambda e: e.scalar_tensor_tensor(
                        out=av[:, ni, r, :], in0=modT[:, l, sci * 8:(sci + 1) * 8, r], scalar=1.0,
                        in1=nw[:, l, ni, :], op0=ALU.add, op1=ALU.mult), reads=[bmod, bc], writes=[bav])

            def mod(idx, g, r):
                return modT[:, l, idx * 8 + g, r:r + 1]

            LS = ExitStack()
            hT = sb(LS, "hT", [128, NG, T], BF16)
            bh = [Buf("h%d" % i) for i in range(9)]

            with ExitStack() as st:
                xb = [sb(st, "xb%d" % i, [128, NG, 512], F32) for i in range(2)]
                bxb = [Buf(), Buf()]
                sq = sb(st, "sq", [128, NG, 512], BF16)
                rstd = sb(st, "rstd", [128, 512], F32)
                tmp = [sb(st, "tmp%d" % i, [128, 512], F32) for i in range(2)]
                btmp = [Buf(), Buf()]
                ss_ps = pst(st, "ss_ps", [128, 512])
                bsq, bss, brstd = Buf(), Buf(), Buf()
                for bi, (t0, nb) in enumerate(BLOCKS):
                    r = 1 if bi == 0 else 0
                    x_ = xb[bi % 2]
                    dma("sp", x_[:, :, :nb], xs[:, :, t0:t0 + nb].rearrange("g p t -> p g t"),
                        xs_bufs(t0, nb), bxb[bi % 2])
                    rms_rstd(st, lambda g: x_[:, g, :nb], NG, nb, sq, ss_ps, bsq, bss, rstd, brstd,
                             ones_b, 1.0 / D, [bxb[bi % 2]])
                    for g in range(NG):
                        tp_ = tmp[g % 2]
                        tt(tp_[:, :nb], x_[:, g, :nb], rstd[:, :nb], ALU.mult, [bxb[bi % 2], brstd], btmp[g % 2])
                        act(hT[:, g, t0:t0 + nb], tp_[:, :nb], AF.Identity, [btmp[g % 2], bav, bmod], bh[bi],
                            scale=av[:, 0, r, g:g + 1], bias=mod(0, g, r))
                S.barrier()

            if tap is not None and tap[0] == "hT" and l == tap[2]:
                with ExitStack() as st:
                    tb = sb(st, "tb", [128, T], F32)
                    btb = Buf()
                    for g in range(NG):
                        act(tb[:], hT[:, g, :], AF.Copy, bh, btb)
                        final_ops.append(dma("sp", tap_out[g], tb[:], [btb], None))
                    S.barrier()

            if stop == "A":
                LS.close()
                S.barrier()
                break
            def make_vtok(st, vt, bvt, col0):
                with ExitStack() as s2:
                    wv = sb(s2, "wv", [128, NG, 512], BF16)
                    bwv = Buf()
                    vps = [pst(s2, "vps%d" % i, [128, 512]) for i in range(2)]
                    bvp = [Buf(), Buf()]
                    dma("pq", wv[:], w_in[l, :, col0:col0 + 512].rearrange("(kc p) c -> p kc c", p=128), [], bwv)
                    for ti in range(T // 128):
                        bi = 0 if ti < 2 else 1 + (ti - 2) // 4
                        mm(vps[ti % 2][:], [(hT[:, kc, ti * 128:(ti + 1) * 128], wv[:, kc, :]) for kc in range(NG)],
                           [bh[bi], bwv], bvp[ti % 2])
                        if ti % 2 == 0:
                            act(vt[:, ti, :], vps[ti % 2][:], AF.Copy, [bvp[ti % 2]], bvt)
                        else:
                            S.op("dve", lambda e: e.tensor_copy(out=vt[:, ti, :], in_=vps[ti % 2][:]),
                                 reads=[bvp[ti % 2]], writes=[bvt])
                    S.barrier()

            with ExitStack() as st:
                vt = sb(st, "vt", [128, T // 128, 512], BF16)
                bvt = Buf()
                make_vtok(st, vt, bvt, OFF["hi"])
                wq = sb(st, "wq", [128, NG, 512], BF16)
                wf = sb(st, "wf", [128, NG, 512], BF16)
                wg = sb(st, "wg", [128, NG, 512], BF16)
                bwq, bwf, bwg = Buf(), Buf(), Buf()
                dma("pq", wq[:], w_in[l, :, OFF["hq"]:OFF["hq"] + 512].rearrange("(kc p) c -> p kc c", p=128), [], bwq)
                dma("pq", wg[:], w_in[l, :, OFF["hg"]:OFF["hg"] + 512].rearrange("(kc p) c -> p kc c", p=128), [], bwg)
                NT = 8
                tf = [sb(st, "tf%d" % i, [128, 512], F32) for i in range(NT)]
                btf = [Buf() for _ in range(NT)]
                gbuf = sb(st, "gbuf", [128, 640], F32)
                bgb = Buf()
                S.op("dve", lambda e: e.memset(gbuf[:], 0.0), writes=[bgb])
                csc = sb(st, "csc", [128, 4, 8], F32)
                bcs = Buf()
                qp = sb(st, "qp", [128, 512], BF16)
                kp = sb(st, "kp", [128, 512], BF16)
                bqp, bkp = Buf(), Buf()
                kt = sb(st, "kt", [128, 4, 128], BF16)
                bkt = Buf()
                atb = sb(st, "atb", [128, 4, 128], BF16)
                batb = Buf()
                Sst = sb(st, "Sst", [128, 4, 128], F32)
                bS = [Buf() for _ in range(4)]
                sbf = [sb(st, "sbf%d" % i, [128, 128], BF16) for i in range(2)]
                bsbf = [Buf(), Buf()]
                tu = [sb(st, "tu%d" % i, [128, 128], F32) for i in range(2)]
                btu = [Buf(), Buf()]
                ofb = sb(st, "ofb", [128, 512], BF16)
                bofb = Buf()
                ps_f = pst(st, "ps_f", [128, 512])
                ps_q = pst(st, "ps_q", [128, 512])
                ps_t = pst(st, "ps_t", [128, 4, 128], BF16)
                ps_a = pst(st, "ps_a", [128, 4, 128])
                ps_u = pst(st, "ps_u", [128, 8, 128])
                ps_o = pst(st, "ps_o", [128, 512])
                ps_s = pst(st, "ps_s", [128, 512])
                bpf, bpq, bpt, bpa, bpu, bpo, bpss = [Buf() for _ in range(7)]
                bof = [[Buf() for _ in range(9)] for _ in range(4)]
                bya = [Buf() for _ in range(17)]

                nunits = [0]
                for dr in range(2):
                    off_f = OFF["hff"] if dr == 0 else OFF["hfb"]
                    dma("pq", wf[:], w_in[l, :, off_f:off_f + 512].rearrange("(kc p) c -> p kc c", p=128), [], bwf)
                    for hh in range(4):
                        S.op("dve", lambda e: e.memset(Sst[:, hh, :], 0.0), writes=[bS[hh]])
                    order = list(range(9)) if dr == 0 else [0] + list(range(8, 0, -1))
                    mask = maskf if dr == 0 else maskb[:]
                    for bi in order:
                        t0, nb = BLOCKS[bi]
                        nch = nb // 64
                        npair = nb // 128
                        r = 1 if bi == 0 else 0
                        for hd in range(4):
                          nunits[0] += 1
                          if DBG["units"] is not None and nunits[0] > DBG["units"]:
                              continue
                          try:
                              ck(0)
                              cs = slice(hd * 128, (hd + 1) * 128)
                              lb_ap = lbt[:, dr, l, hd:hd + 1]
                              oml_ap = oml[:, dr, l, hd:hd + 1]
                              mm(ps_f[:, :nb], [(wf[:, kc, cs], hT[:, kc, t0:t0 + nb]) for kc in range(NG)],
                                 [bwf, bh[bi]], bpf)
                              mm(ps_q[:, :nb], [(wq[:, kc, cs], hT[:, kc, t0:t0 + nb]) for kc in range(NG)],
                                 [bwq, bh[bi]], bpq)
                              ck(1)
                              E_, L1, L2, Q4, A5, E6 = tf[0], tf[1], tf[2], tf[3], tf[4], tf[5]
                              act(E_[:, :nb], ps_f[:, :nb], AF.Exp, [bpf], btf[0], scale=-1.0)
                              act(L1[:, :nb], E_[:, :nb], AF.Ln, [btf[0], bc], btf[1], scale=1.0, bias=one_ap)
                              act(L2[:, :nb], E_[:, :nb], AF.Ln, [btf[0], bc], btf[2], scale=lb_ap, bias=one_ap)
                              tt(L2[:, :nb], L2[:, :nb], L1[:, :nb], ALU.subtract, [btf[1], btf[2]], btf[2])
                              S.op("dve", lambda e: e.tensor_tensor_scan(
                                  out=gbuf[:, 1:nb + 1], data0=onesf[:, :nb], data1=L2[:, :nb], initial=0.0,
                                  op0=ALU.mult, op1=ALU.add), reads=[btf[2], bc], writes=[bgb])
                              ck(2)
                              act(L1[:, :nb], L1[:, :nb], AF.Exp, [btf[1]], btf[1], scale=-1.0)
                              stt(E_[:, :nb], E_[:, :nb], oml_ap, L1[:, :nb], ALU.mult, ALU.mult, [btf[0], btf[1], bc], btf[0])
                              act(Q4[:, :nb], ps_q[:, :nb], AF.Exp, [bpq], btf[3], scale=-1.0)
                              ts(Q4[:, :nb], Q4[:, :nb], 1.0, ALU.add, [btf[3]], btf[3])
                              S.op("dve", lambda e: e.reciprocal(out=Q4[:, :nb], in_=Q4[:, :nb]), reads=[btf[3]], writes=[btf[3]])
                              tt(Q4[:, :nb], Q4[:, :nb], ps_q[:, :nb], ALU.mult, [btf[3], bpq], btf[3])
                              ck(3)
                              goff = 1 if dr == 0 else 0
                              ref = gbuf[:, 32:32 + nb].rearrange("p (c t) -> p c t", t=64)[:, :, 0:1].to_broadcast([128, nch, 64])
                              tt(A5[:, :nb].rearrange("p (c t) -> p c t", t=64),
                                 gbuf[:, goff:goff + nb].rearrange("p (c t) -> p c t", t=64), ref, ALU.subtract,
                                 [bgb], btf[4])
                              sgn = 1.0 if dr == 0 else -1.0
                              act(E6[:, :nb], A5[:, :nb], AF.Exp, [btf[4]], btf[5], scale=sgn)
                              act(A5[:, :nb], A5[:, :nb], AF.Exp, [btf[4]], btf[4], scale=-sgn)
                              tt(qp[:, :nb], Q4[:, :nb], E6[:, :nb], ALU.mult, [btf[3], btf[5]], bqp)
                              tt(kp[:, :nb], E_[:, :nb], A5[:, :nb], ALU.mult, [btf[0], btf[4]], bkp)
                              ck(4)
                              g0 = gbuf[:, 0:nb].rearrange("p (c t) -> p c t", t=64)[:, :, 0]
                              g32 = gbuf[:, 32:32 + nb].rearrange("p (c t) -> p c t", t=64)[:, :, 0]
                              g64 = gbuf[:, 64:64 + nb].rearrange("p (c t) -> p c t", t=64)[:, :, 0]
                              tt(csc[:, 0, :nch], g32, g0, ALU.subtract, [bgb], bcs)
                              tt(csc[:, 1, :nch], g64, g32, ALU.subtract, [bgb], bcs)
                              act(csc[:, 0:2, :nch], csc[:, 0:2, :nch], AF.Exp, [bcs], bcs)
                              tt(csc[:, 2, :nch], csc[:, 0, :nch], csc[:, 1, :nch], ALU.mult, [bcs], bcs)
                              wi, ei = (0, 1) if dr == 0 else (1, 0)
                              ck(5)
                              for j in range(npair):
                                  S.op("pe", lambda e: e.transpose(out=ps_t[:, j, :], in_=kp[:, j * 128:(j + 1) * 128], identity=ident),
                                       reads=[bkp, bc], writes=[bpt])
                              act(kt[:, :npair, :], ps_t[:, :npair, :], AF.Copy, [bpt], bkt)
                              ck(6)
                              for j in range(npair):
                                  mm(ps_a[:, j, :], [(kp[:, j * 128:(j + 1) * 128], qp[:, j * 128:(j + 1) * 128])],
                                     [bkp, bqp], bpa)
                              for j in range(npair):
                                  tt(atb[:, j, :], ps_a[:, j, :], mask, ALU.mult, [bpa, bc], batb)
                              ck(7)
                              for c in range(nch):
                                  if DBG.get("ucount") is not None and c >= DBG["ucount"]:
                                      break
                                  j, par = c // 2, c % 2
                                  ti = t0 // 128 + j
                                  mm(ps_u[:, par * 4 + j, :], [(kt[par * 64:(par + 1) * 64, j, :], vt[par * 64:(par + 1) * 64, ti, cs])],
                                     [bkt, bvt], bpu)
                              ck(7.5)
                              for j in range(npair):
                                  ti = t0 // 128 + j
                                  mm(ps_o[:, j * 128:(j + 1) * 128], [(vt[:, ti, cs], atb[:, j, :])], [bvt, batb], bpo,
                                     start=(j == 0), stop=False)
                              ck(8)
                              corder = list(range(nch)) if dr == 0 else list(range(nch - 1, -1, -1))
                              for ci, c in enumerate(corder):
                                  sb_ = sbf[ci % 2]
                                  tu_ = tu[ci % 2]
                                  act(sb_[:], Sst[:, hd, :], AF.Identity, [bS[hd], bcs], bsbf[ci % 2], scale=csc[:, wi, c:c + 1])
                                  mm(ps_o[:, c * 64:(c + 1) * 64], [(sb_[:], qp[:, c * 64:(c + 1) * 64])],
                                     [bsbf[ci % 2], bqp], bpo, start=False, stop=True)
                                  act(tu_[:], ps_u[:, (c % 2) * 4 + c // 2, :], AF.Identity, [bpu, bcs], btu[ci % 2], scale=csc[:, ei, c:c + 1])
                                  stt(Sst[:, hd, :], Sst[:, hd, :], csc[:, 2, c:c + 1], tu_[:], ALU.mult, ALU.add,
                                      [bS[hd], bcs, btu[ci % 2]], bS[hd])
                              ck(9)
                              if dr == 0:
                                  act(ofb[:, :nb], ps_o[:, :nb], AF.Copy, [bpo], bofb)
                                  dma("sp", of_d[hd, :, t0:t0 + nb], ofb[:, :nb], [bofb], bof[hd][bi])
                              else:
                                  O_, G6, G7 = tf[1], tf[6], tf[7]
                                  dma("sp", ofb[:, :nb], of_d[hd, :, t0:t0 + nb], [bof[hd][bi]], bofb)
                                  tt(O_[:, :nb], ps_o[:, :nb], ofb[:, :nb], ALU.add, [bpo, bofb], btf[1])
                                  act(kp[:, :nb], O_[:, :nb], AF.Square, [btf[1]], bkp)
                                  mm(ps_s[:, :nb], [(ones_b, kp[:, :nb])], [bkp, bc], bpss)
                                  act(G6[:, :nb], ps_s[:, :nb], AF.Ln, [bpss, bc], btf[6], scale=1.0 / 128, bias=eps_ap)
                                  act(G6[:, :nb], G6[:, :nb], AF.Exp, [btf[6]], btf[6], scale=-0.5)
                                  stt(O_[:, :nb], O_[:, :nb], onwt[:, l:l + 1], G6[:, :nb], ALU.mult, ALU.mult,
                                      [btf[1], btf[6], bc], btf[1])
                                  mm(ps_f[:, :nb], [(wg[:, kc, cs], hT[:, kc, t0:t0 + nb]) for kc in range(NG)],
                                     [bwg, bh[bi]], bpf)
                                  act(G7[:, :nb], ps_f[:, :nb], AF.Exp, [bpf], btf[7], scale=-1.0)
                                  ts(G7[:, :nb], G7[:, :nb], 1.0, ALU.add, [btf[7]], btf[7])
                                  S.op("dve", lambda e: e.reciprocal(out=G7[:, :nb], in_=G7[:, :nb]), reads=[btf[7]], writes=[btf[7]])
                                  tt(G7[:, :nb], G7[:, :nb], ps_f[:, :nb], ALU.mult, [btf[7], bpf], btf[7])
                                  tt(qp[:, :nb], O_[:, :nb], G7[:, :nb], ALU.mult, [btf[1], btf[7]], bqp)
                                  dma("sp", ya_d[hd, :, t0:t0 + nb], qp[:, :nb], [bqp], bya[t0 // 256])
                                  if nb == 512:
                                      bya[t0 // 256 + 1] = bya[t0 // 256]
                          except _Stop:
                            pass
                S.barrier()

            def tap_dram(name, src, ngrp, dt, rbufs):
                if tap is None or tap[0] != name or l != tap[2]:
                    return
                with ExitStack() as st:
                    ta = sb(st, "tap_a", [128, T], dt)
                    tb = sb(st, "tap_b", [128, T], F32)
                    bta, btb = Buf(), Buf()
                    for g in range(ngrp):
                        dma("sp", ta[:], src[g], rbufs, bta)
                        act(tb[:], ta[:], AF.Copy, [bta], btb)
                        final_ops.append(dma("sp", tap_out[g], tb[:], [btb], None))
                    S.barrier()

            tap_dram("ya", ya_d, 4, BF16, bya)
            tap_dram("of", of_d, 4, BF16, [])
            if stop == "B":
                LS.close()
                S.barrier()
                break

            byb = [Buf() for _ in range(17)]
            with ExitStack() as st:
                vt = sb(st, "vn", [128, T // 128, 512], BF16)
                bvt = Buf()
                make_vtok(st, vt, bvt, OFF["nv"])
                wqk = sb(st, "wqk", [128, 2, NG, 128], BF16)
                bwqk = Buf()
                qT = sb(st, "qT", [128, T], BF16)
                kT = sb(st, "kT", [128, T], BF16)
                bq_ = [Buf() for _ in range(9)]
                bk_ = [Buf() for _ in range(9)]
                cs_t = [sb(st, "cst%d" % i, [128, 2, 512], F32) for i in range(2)]
                bcs_t = [Buf(), Buf()]
                sqn = sb(st, "sqn", [128, 512], BF16)
                rs_ = sb(st, "rsn", [128, 512], F32)
                qn_ = sb(st, "qn", [128, 512], BF16)
                t1 = sb(st, "t1n", [128, 512], F32)
                t2 = sb(st, "t2n", [128, 512], F32)
                bsqn, brs, bqn, bt1, bt2 = [Buf() for _ in range(5)]
                bt = sb(st, "bt", [128, 3, 6, 256], F32)
                bbt = Buf()
                tmpS = [sb(st, "tmpS%d" % i, [128, 2, 256], F32) for i in range(2)]
                btmpS = [Buf(), Buf()]
                Pt = [sb(st, "Pt%d" % i, [128, 2, 256], BF16) for i in range(4)]
                bPt = [Buf() for _ in range(4)]
                rsum = sb(st, "rsum", [64, 256], F32)
                brsum = Buf()
                yo = [sb(st, "yo%d" % i, [64, 256], BF16) for i in range(2)]
                byo = [Buf(), Buf()]
                ps_p = pst(st, "ps_p", [128, 512])
                ps_n = pst(st, "ps_n", [128, 512])
                ps_r = pst(st, "ps_r", [128, 512])
                ps_s2 = [pst(st, "ps_s2%d" % i, [128, 2, 256]) for i in range(2)]
                ps_o2 = pst(st, "ps_o2", [64, 512])
                ps_m = pst(st, "ps_m", [64, 512])
                bpp, bpn, bpr, bpo2, bpm = [Buf() for _ in range(5)]
                bpss = [Buf(), Buf()]
                for gq in range(4):
                    dma("pq", wqk[:, 0], w_in[l, :, OFF["nq"] + gq * 128:OFF["nq"] + (gq + 1) * 128]
                        .rearrange("(kc p) c -> p kc c", p=128), [], bwqk)
                    dma("pq", wqk[:, 1], w_in[l, :, OFF["nk"] + gq * 128:OFF["nk"] + (gq + 1) * 128]
                        .rearrange("(kc p) c -> p kc c", p=128), [], bwqk)
                    for bi, (t0, nb) in enumerate(BLOCKS):
                        ct_ = cs_t[bi % 2]
                        if bi > 0:
                            dma("sp", ct_[:, 0, :], cos_in[:, t0 - 256:t0 + 256], [], bcs_t[bi % 2])
                            dma("sp", ct_[:, 1, :], sin_in[:, t0 - 256:t0 + 256], [], bcs_t[bi % 2])
                        for qi, (dst, bdst) in enumerate(((qT, bq_), (kT, bk_))):
                            mm(ps_p[:, :nb], [(wqk[:, qi, kc, :], hT[:, kc, t0:t0 + nb]) for kc in range(NG)],
                               [bwqk, bh[bi]], bpp)
                            act(sqn[:, :nb], ps_p[:, :nb], AF.Square, [bpp], bsqn)
                            mm(ps_n[:, :nb], [(blockones, sqn[:, :nb])], [bsqn, bc], bpn)
                            act(rs_[:, :nb], ps_n[:, :nb], AF.Ln, [bpn, bc], brs, scale=1.0 / 64, bias=eps_ap)
                            act(rs_[:, :nb], rs_[:, :nb], AF.Exp, [brs], brs, scale=-0.5)
                            if bi == 0:
                                stt(dst[:, t0:t0 + nb], ps_p[:, :nb], qkwt[:, l, qi:qi + 1], rs_[:, :nb], ALU.mult, ALU.mult,
                                    [bpp, brs, bc], bdst[bi])
                            else:
                                stt(qn_[:, :nb], ps_p[:, :nb], qkwt[:, l, qi:qi + 1], rs_[:, :nb], ALU.mult, ALU.mult,
                                    [bpp, brs, bc], bqn)
                                mm(ps_r[:, :nb], [(perm, qn_[:, :nb])], [bqn, bc], bpr)
                                tt(t1[:, :nb], qn_[:, :nb], ct_[:, 0, :nb], ALU.mult, [bqn, bcs_t[bi % 2]], bt1)
                                tt(t2[:, :nb], ps_r[:, :nb], ct_[:, 1, :nb], ALU.mult, [bpr, bcs_t[bi % 2]], bt2)
                                tt(dst[:, t0:t0 + nb], t1[:, :nb], t2[:, :nb], ALU.add, [bt1, bt2], bdst[bi])
                    for par in range(2):
                        h8 = gq * 2 + par
                        prt = slice(par * 64, par * 64 + 64)
                        dma("sp", bt[:].rearrange("p a b c -> p (a b c)"), bias_tab[l, h8], [], bbt)
                        for qb in range(17):
                            if qb < 16:
                                q0 = 256 + qb * 256
                                ts_ = min(max(4 * qb - 4, 0), 52)
                                pat = 0 if qb == 0 else (2 if qb == 15 else 1)
                                ktiles = [256 + (ts_ + 2 * j) * 64 for j in range(6)] + [0, 128]
                            else:
                                q0 = 0
                                pat = 0
                                ktiles = [0, 128]
                            npairs = len(ktiles) // 2
                            for jj in range(npairs):
                                pss = ps_s2[jj % 2]
                                for u in range(2):
                                    k0 = ktiles[2 * jj + u]
                                    mm(pss[:, u, :], [(kT[prt, k0:k0 + 128], qT[prt, q0:q0 + 256])], bk_ + bq_, bpss[jj % 2])
                                Pj = Pt[jj]
                                if qb < 16 and jj < 3:
                                    tS = tmpS[jj % 2]
                                    tt(tS[:], pss[:], bt[:, pat, 2 * jj:2 * jj + 2, :], ALU.add, [bpss[jj % 2], bbt], btmpS[jj % 2])
                                    act(Pj[:], tS[:], AF.Exp, [btmpS[jj % 2]], bPt[jj])
                                else:
                                    act(Pj[:], pss[:], AF.Exp, [bpss[jj % 2]], bPt[jj])
                            pairs_o, pairs_m = [], []
                            for i, k0 in enumerate(ktiles):
                                Pj = Pt[i // 2][:, i % 2, :]
                                pairs_o.append((vt[:, k0 // 128, h8 * 64:(h8 + 1) * 64], Pj))
                                pairs_m.append((cst[:, 0, 0:64], Pj))
                            mm(ps_o2[:, :256], pairs_o, [bvt] + bPt[:npairs], bpo2)
                            mm(ps_m[:, :256], pairs_m, [bc] + bPt[:npairs], bpm)
                            S.op("dve", lambda e: e.reciprocal(out=rsum[:], in_=ps_m[:, :256]), reads=[bpm], writes=[brsum])
                            y_ = yo[qb % 2]
                            tt(y_[:], ps_o2[:, :256], rsum[:], ALU.mult, [bpo2, brsum], byo[qb % 2])
                            dma("sp", yb_d[gq, prt, q0:q0 + 256], y_[:], [byo[qb % 2]], byb[q0 // 256])
                S.barrier()
            tap_dram("yb", yb_d, 4, BF16, byb)
            if stop == "C":
                LS.close()
                S.barrier()
                break

            byc = Buf()
            with ExitStack() as st:
                wc = sb(st, "wc", [128, 3, NG, 128], BF16)
                bwc = Buf()
                ul = sb(st, "ul", [128, TL + 2], F32)
                uc = sb(st, "uc", [128, TC + 2], F32)
                bu = Buf()
                Bs = sb(st, "Bs", [128, T], F32)
                bBs = Buf()
                acc = sb(st, "acc", [128, TL], F32)
                bacc = Buf()
                yout = sb(st, "yout", [128, T], BF16)
                byout = Buf()
                csb = sb(st, "csb", [128, 512], F32)
                bcsb = Buf()
                ps_c = [pst(st, "ps_c%d" % i, [128, 512]) for i in range(3)]
                bpc = [Buf() for _ in range(3)]
                for (u_, n) in ((uc, TC), (ul, TL)):
                    S.op("dve", lambda e: e.memset(u_[:, 0:1], 0.0), writes=[bu])
                    S.op("dve", lambda e: e.memset(u_[:, n + 1:n + 2], 0.0), writes=[bu])
                for gc in range(4):
                    for i, key in enumerate(("cb", "cc", "cx")):
                        dma("pq", wc[:, i], w_in[l, :, OFF[key] + gc * 128:OFF[key] + (gc + 1) * 128]
                            .rearrange("(kc p) c -> p kc c", p=128), [], bwc)
                    for bi, (t0, nb) in enumerate(BLOCKS):
                        for i in range(3):
                            mm(ps_c[i][:, :nb], [(wc[:, i, kc, :], hT[:, kc, t0:t0 + nb]) for kc in range(NG)],
                               [bwc, bh[bi]], bpc[i])
                        act(Bs[:, t0:t0 + nb], ps_c[0][:, :nb], AF.Copy, [bpc[0]], bBs)
                        act(csb[:, :nb], ps_c[1][:, :nb], AF.Copy, [bpc[1]], bcsb)
                        udst = uc[:, 1:1 + nb] if bi == 0 else ul[:, 1 + t0 - 256:1 + t0 - 256 + nb]
                        tt(udst, ps_c[2][:, :nb], csb[:, :nb], ALU.mult, [bpc[2], bcsb], bu)
                    for (u_, n, o0) in ((uc, TC, 0), (ul, TL, 256)):
                        ts(acc[:, :n], u_[:, 1:1 + n], cwt[:, l, 1, gc:gc + 1], ALU.mult, [bu, bc], bacc)
                        stt(acc[:, :n], u_[:, 0:n], cwt[:, l, 0, gc:gc + 1], acc[:, :n], ALU.mult, ALU.add, [bu, bc], bacc)
                        stt(acc[:, :n], u_[:, 2:2 + n], cwt[:, l, 2, gc:gc + 1], acc[:, :n], ALU.mult, ALU.add, [bu, bc], bacc)
                        tt(yout[:, o0:o0 + n], acc[:, :n], Bs[:, o0:o0 + n], ALU.mult, [bacc, bBs], byout)
                    dma("sp", yc_d[gc], yout[:], [byout], byc)
                S.barrier()
            tap_dram("yc", yc_d, 4, BF16, [byc])
            if stop == "D":
                LS.close()
                S.barrier()
                break

            with ExitStack() as st:
                wgt = sb(st, "wgt", [128, NG, 3072], BF16)
                wbr = sb(st, "wbr", [128, 3, 4, 1024], BF16)
                wo = sb(st, "wo", [128, NG, 1024], BF16)
                bwm = Buf()
                for br in range(3):
                    dma("pq", wgt[:, :, br * 1024:(br + 1) * 1024],
                        w_in[l, :, OFF["ga"] + br * 1024:OFF["ga"] + (br + 1) * 1024].rearrange("(kc p) c -> p kc c", p=128), [], bwm)
                for br, wsrc in enumerate((w_ba, w_bb, w_bc)):
                    dma("pq", wbr[:, br], wsrc[l].rearrange("(kc p) c -> p kc c", p=128), [], bwm)
                dma("pq", wo[:], w_out[l].rearrange("(kc p) c -> p kc c", p=128), [], bwm)
                yblk = [sb(st, "yblk%d" % i, [128, 3, 4, 256], BF16) for i in range(2)]
                byblk = [Buf(), Buf()]
                xb2 = [sb(st, "xb2%d" % i, [128, NG, 256], F32) for i in range(2)]
                bxb2 = [Buf(), Buf()]
                mixT = sb(st, "mixT", [128, NG, 256], BF16)
                bmix = Buf()
                ea = [sb(st, "ea%d" % i, [128, 256], F32) for i in range(3)]
                bea = [Buf() for _ in range(3)]
                macc = sb(st, "macc", [128, 256], F32)
                ctb = sb(st, "ctb", [128, 256], F32)
                bmacc, bctb = Buf(), Buf()
                ps_g = [pst(st, "ps_g%d" % i, [128, 512]) for i in range(3)]
                ps_j = [pst(st, "ps_j%d" % i, [128, 512]) for i in range(3)]
                ps_w = pst(st, "ps_w", [128, 512])
                bpg = [Buf() for _ in range(3)]
                bpj = [Buf() for _ in range(3)]
                bpw = Buf()
                for bi, (t0, nb) in enumerate(BLOCKS256):
                    r = 1 if bi == 0 else 0
                    hbi = 0 if bi == 0 else 1 + (bi - 1) // 2
                    yb_ = yblk[bi % 2]
                    x_ = xb2[bi % 2]
                    for i, (src, rb) in enumerate(((ya_d, [bya[bi]]), (yb_d, [byb[bi]]), (yc_d, [byc]))):
                        dma("sp", yb_[:, i], src[:, :, t0:t0 + nb].rearrange("g p t -> p g t"), rb, byblk[bi % 2])
                    dma("sp", x_[:], xs[:, :, t0:t0 + nb].rearrange("g p t -> p g t"), [bxs[bi]], bxb2[bi % 2])
                    for m in range(NG):
                        for br in range(3):
                            mm(ps_g[br][:, :nb], [(wgt[:, kc, br * 1024 + m * 128:br * 1024 + (m + 1) * 128], hT[:, kc, t0:t0 + nb])
                                                  for kc in range(NG)], [bwm, bh[hbi]], bpg[br])
                            mm(ps_j[br][:, :nb], [(wbr[:, br, kc, m * 128:(m + 1) * 128], yb_[:, br, kc, :]) for kc in range(4)],
                               [bwm, byblk[bi % 2]], bpj[br])
                            act(ea[br][:], ps_g[br][:, :nb], AF.Exp, [bpg[br]], bea[br], scale=-1.0)
                            act(ea[br][:], ea[br][:], AF.Ln, [bea[br], bc], bea[br], scale=1.0, bias=one_ap)
                            act(ea[br][:], ea[br][:], AF.Exp, [bea[br]], bea[br], scale=-1.0)
                            if br == 0:
                                tt(macc[:], ps_j[br][:, :nb], ea[br][:], ALU.mult, [bpj[br], bea[br]], bmacc)
                            else:
                                tt(ctb[:], ps_j[br][:, :nb], ea[br][:], ALU.mult, [bpj[br], bea[br]], bctb)
                                if br == 1:
                                    tt(macc[:], macc[:], ctb[:], ALU.add, [bmacc, bctb], bmacc)
                                else:
                                    tt(mixT[:, m, :], macc[:], ctb[:], ALU.add, [bmacc, bctb], bmix)
                    for mo in range(NG):
                        mm(ps_w[:, :nb], [(wo[:, kc, mo * 128:(mo + 1) * 128], mixT[:, kc, :]) for kc in range(NG)],
                           [bwm, bmix], bpw)
                        stt(x_[:, mo, :], ps_w[:, :nb], mod(2, mo, r), x_[:, mo, :], ALU.mult, ALU.add,
                            [bpw, bmod, bxb2[bi % 2]], bxb2[bi % 2])
                    dma("sp", xs[:, :, t0:t0 + nb].rearrange("g p t -> p g t"), x_[:], [bxb2[bi % 2]], bxs[bi])
                S.barrier()

            LS.close()
            S.barrier()
            tap_dram("x1", xs, NG, F32, bxs)
            if stop == "M":
                break

            with ExitStack() as st:
                w1t = sb(st, "w1t", [128, NG, 4096], BF16)
                w2t = sb(st, "w2t", [128, 32, 1024], BF16)
                bw12 = Buf()
                for i in range(4):
                    dma("pq", w1t[:, :, i * 1024:(i + 1) * 1024],
                        w1[l, :, i * 1024:(i + 1) * 1024].rearrange("(kc p) c -> p kc c", p=128), [], bw12)
                for i in range(4):
                    dma("pq", w2t[:, i * 8:(i + 1) * 8, :],
                        w2[l, i * 1024:(i + 1) * 1024, :].rearrange("(kc p) c -> p kc c", p=128), [], bw12)
                xe = [sb(st, "xe%d" % i, [128, NG, 256], F32) for i in range(2)]
                bxe = [Buf(), Buf()]
                sq2 = sb(st, "sq2", [128, NG, 256], BF16)
                rstd2 = sb(st, "rstd2", [128, 256], F32)
                tmp2 = [sb(st, "tmp2%d" % i, [128, 256], F32) for i in range(2)]
                btmp2 = [Buf(), Buf()]
                h2 = sb(st, "h2", [128, NG, 256], BF16)
                bh2 = Buf()
                uT = sb(st, "uT", [128, 32, 256], BF16)
                buT = Buf()
                rl = [sb(st, "rl%d" % i, [128, 256], F32) for i in range(2)]
                brl = [Buf(), Buf()]
                ss2 = pst(st, "ss2", [128, 512])
                ps_u2 = [pst(st, "ps_u2%d" % i, [128, 512]) for i in range(2)]
                ps_w2 = [pst(st, "ps_w2%d" % i, [128, 512]) for i in range(2)]
                bsq2, bss2, brstd2 = Buf(), Buf(), Buf()
                bpu2 = [Buf(), Buf()]
                bpw2 = [Buf(), Buf()]
                for bi, (t0, nb) in enumerate(BLOCKS256):
                    r = 1 if bi == 0 else 0
                    x_ = xe[bi % 2]
                    dma("sp", x_[:], xs[:, :, t0:t0 + nb].rearrange("g p t -> p g t"), [bxs[bi]], bxe[bi % 2])
                    rms_rstd(st, lambda g: x_[:, g, :], NG, nb, sq2, ss2, bsq2, bss2, rstd2, brstd2, ones_b, 1.0 / D,
                             [bxe[bi % 2]])
                    for g in range(NG):
                        tp_ = tmp2[g % 2]
                        tt(tp_[:], x_[:, g, :], rstd2[:], ALU.mult, [bxe[bi % 2], brstd2], btmp2[g % 2])
                        act(h2[:, g, :], tp_[:], AF.Identity, [btmp2[g % 2], bav, bmod], bh2,
                            scale=av[:, 1, r, g:g + 1], bias=mod(3, g, r))
                    for j in range(32):
                        pu = ps_u2[j % 2]
                        mm(pu[:, :nb], [(w1t[:, kc, j * 128:(j + 1) * 128], h2[:, kc, :]) for kc in range(NG)],
                           [bw12, bh2], bpu2[j % 2])
                        act(rl[j % 2][:], pu[:, :nb], AF.Relu, [bpu2[j % 2]], brl[j % 2])
                        tt(uT[:, j, :], rl[j % 2][:], rl[j % 2][:], ALU.mult, [brl[j % 2]], buT)
                    for mo in range(NG):
                        pw = ps_w2[mo % 2]
                        mm(pw[:, :nb], [(w2t[:, j, mo * 128:(mo + 1) * 128], uT[:, j, :]) for j in range(32)],
                           [bw12, buT], bpw2[mo % 2])
                        stt(x_[:, mo, :], pw[:, :nb], mod(5, mo, r), x_[:, mo, :], ALU.mult, ALU.add,
                            [bpw2[mo % 2], bmod, bxe[bi % 2]], bxe[bi % 2])
                    dma("sp", xs[:, :, t0:t0 + nb].rearrange("g p t -> p g t"), x_[:], [bxe[bi % 2]], bxs[bi])
                S.barrier()
            tap_dram("x2", xs, NG, F32, bxs)

        with ExitStack() as st:
            cp = [sb(st, "fcp%d" % i, [128, NG, 512], F32) for i in range(2)]
            bcp = [Buf(), Buf()]
            for bi in range(8):
                t0 = 256 + 512 * bi
                c_ = cp[bi % 2]
                dma("sp", c_[:], xs[:, :, t0:t0 + 512].rearrange("g p t -> p g t"), xs_bufs(t0, 512), bcp[bi % 2])
                final_ops.append(S.op("sp", lambda e: e.dma_start(
                    out=outT[:, :, t0 - 256:t0 + 256].rearrange("g p t -> p g t"), in_=c_[:]), reads=[bcp[bi % 2]]))
        S.finish(final_ops)
    print("instructions:", S.ninst)
    return nc


def _consts():
    c = np.zeros((128, 5, 128), np.float32)
    c[:, 0, :] = 1.0
    c[:, 1, :] = np.eye(128)
    c[:64, 2, :64] = 1.0
    c[64:, 2, 64:] = 1.0
    for m in range(128):
        blk, i = divmod(m, 32)
        pm = blk * 32 + (i + 16) % 32
        c[pm, 3, m] = 1.0
    s = np.arange(128)[:, None]
    t = np.arange(128)[None, :]
    same = (s // 64) == (t // 64)
    c[:, 4, :] = (same & (s <= t)).astype(np.float32)
    mb = (same & (s >= t)).astype(np.float32)
    return c.reshape(128, 640), mb


def _rope_tables():
    tpos = np.arange(TL)
    pos = np.stack([tpos // 64, tpos % 64], -1).astype(np.float32)
    half = 32
    inv = (10000.0 ** (-np.arange(0, half, 2, dtype=np.float32) / half)).astype(np.float32)
    ang = pos[:, :, None] * inv
    cos, sin = np.cos(ang), np.sin(ang)
    cT = np.zeros((128, TL), np.float32)
    sT = np.zeros((128, TL), np.float32)
    for p in range(128):
        d = p % 64
        axis, rem = divmod(d, 32)
        hf, j = divmod(rem, 16)
        cT[p] = cos[:, axis, j]
        sT[p] = sin[:, axis, j] * (-1.0 if hf == 0 else 1.0)
    return cT, sT


def _bias_table(rpb):
    Lr = rpb.shape[0]
    out = np.full((Lr, 8, 128, 3, 6, 256), NEG, np.float32)
    cols = np.arange(64)
    cstart = np.clip(cols - 8, 0, 48)
    for pat, qb in enumerate((0, 1, 15)):
        ts_ = int(np.clip(4 * qb - 4, 0, 52))
        for j in range(6):
            for kr in range(2):
                rk = ts_ + 2 * j + kr
                for qr in range(4):
                    rq = 4 * qb + qr
                    r0 = int(np.clip(rq - 4, 0, 56))
                    if not (r0 <= rk < r0 + 8):
                        continue
                    rix = rk - rq + 7
                    ck = np.arange(64)[:, None]
                    cq = np.arange(64)[None, :]
                    ok = (ck >= cstart[None, :]) & (ck < cstart[None, :] + 16)
                    cix = np.clip(ck - cq + 15, 0, 30)
                    vals = rpb[:, :, rix, :][:, :, cix]
                    blk = np.where(ok[None, None], vals, NEG)
                    out[:, :, kr * 64:(kr + 1) * 64, pat, j, qr * 64:(qr + 1) * 64] = blk
    return out.reshape(Lr, 8, 128, 3 * 6 * 256)


def prep_inputs(x, c, ctx, c_ctx, ada_w, ada_b, norm1_w, norm2_w, w_in, hgrn_lb_logits, hgrn_onorm_w,
                q_norm_w, k_norm_w, natten_rpb, conv_w, w_branch_a, w_branch_b, w_branch_c, w_out,
                mlp_w1, mlp_w2):
    f = lambda a: np.ascontiguousarray(np.asarray(a, np.float32))
    cst, mb = _consts()
    cT_, sT_ = _rope_tables()
    shared = {
        "ada_w": f(ada_w),
        "ada_bT": f(np.asarray(ada_b).reshape(L, 48, 128).transpose(2, 0, 1)),
        "nwT": f(np.stack([np.asarray(norm1_w).reshape(L, NG, 128), np.asarray(norm2_w).reshape(L, NG, 128)], 1)
                 .transpose(3, 0, 1, 2)),
        "w_in": f(w_in),
        "lbl": f(np.asarray(hgrn_lb_logits).reshape(2, L, 4, 128).transpose(3, 0, 1, 2)),
        "onw": f(np.asarray(hgrn_onorm_w).T),
        "qkw": f(np.stack([np.tile(np.asarray(q_norm_w), (1, 2)), np.tile(np.asarray(k_norm_w), (1, 2))], -1)
                 .transpose(1, 0, 2)),
        "bias_tab": f(_bias_table(np.asarray(natten_rpb, np.float32))),
        "cwT": f(np.asarray(conv_w).reshape(L, 3, 4, 128).transpose(3, 0, 1, 2)),
        "w_ba": f(w_branch_a), "w_bb": f(w_branch_b), "w_bc": f(w_branch_c), "w_out": f(w_out),
        "w1": f(mlp_w1), "w2": f(mlp_w2),
        "cosT": f(cT_), "sinT": f(sT_), "cst": f(cst), "cst2": f(mb),
    }
    maps = []
    xn, cn, ctxn, ccn = (np.asarray(a, np.float32) for a in (x, c, ctx, c_ctx))
    for b in range(xn.shape[0]):
        seq = np.concatenate([ctxn[b], xn[b]], 0)
        m = dict(shared)
        m["xT"] = f(seq.T.reshape(NG, 128, T))
        m["cT"] = f(np.stack([cn[b].reshape(NG, 128).T, ccn.reshape(NG, 128).T], -1))
        maps.append(m)
    return maps


def kernel(**inputs):
    maps = prep_inputs(**inputs)
    nc = build()
    res = run_bass_kernel_spmd(nc, maps, core_ids=list(range(8)))
    outs = [r["outT"].reshape(D, TL).T for r in res.results]
    return np.ascontiguousarray(np.stack(outs, 0).astype(np.float32))
```

```python
import numpy as np
import ml_dtypes
from contextlib import ExitStack
import concourse.bass as bass
import concourse.mybir as mybir
from concourse.bass_utils import run_bass_kernel_spmd

F32 = mybir.dt.float32
BF16 = mybir.dt.bfloat16
ALU = mybir.AluOpType
AF = mybir.ActivationFunctionType

D = 1024
L = 4
TC = 256
TL = 4096
T = TC + TL
NG = 8
PROJ = 8704
EPS = 1e-6
OFF = dict(hq=0, hff=512, hfb=1024, hi=1536, hg=2048, nq=2560, nk=3072, nv=3584,
           cb=4096, cc=4608, cx=5120, ga=5632, gb=6656, gc=7680)
BLOCKS = [(0, 256)] + [(256 + 512 * i, 512) for i in range(8)]
BLOCKS256 = [(256 * i, 256) for i in range(17)]
NEG = -30000.0

DBG = {"units": None, "step": None}


class _Stop(Exception):
    pass


def ck(k):
    if DBG["step"] is not None and DBG["step"] <= k:
        raise _Stop()


COMPUTE = ("pe", "act", "dve")
DMAQ = ("sp", "pq")
N_DMA_SEMS = 32


class Buf:
    __slots__ = ("name", "lw", "rd")

    def __init__(self, name=""):
        self.name = name
        self.lw = None
        self.rd = []


class Op:
    __slots__ = ("eng", "idx", "sem", "val", "uid")
    _n = [0]

    def __init__(self, eng, idx, sem, val):
        self.eng, self.idx, self.sem, self.val = eng, idx, sem, val
        Op._n[0] += 1
        self.uid = Op._n[0]


class Sched:
    def __init__(self, nc, stack):
        self.nc = nc
        self.E = {"pe": nc.tensor, "act": nc.scalar, "dve": nc.vector, "sp": nc.sync, "pq": nc.gpsimd}
        self.sems = {e: stack.enter_context(nc.semaphore("s_" + e)) for e in COMPUTE}
        self.dsems = [stack.enter_context(nc.semaphore("d%d" % i)) for i in range(N_DMA_SEMS)]
        self.count = {e: 0 for e in COMPUTE + DMAQ}
        self.known = {e: {} for e in COMPUTE + DMAQ}
        self.known_dma = {e: set() for e in COMPUTE + DMAQ}
        self.dma_count = 0
        self.dma_last = [None] * N_DMA_SEMS
        self.dma_total = [0] * N_DMA_SEMS
        self.last = {e: None for e in COMPUTE}
        self.ninst = 0

    def _need(self, eng, dep, waits):
        if dep is None:
            return
        if dep.eng in DMAQ:
            if dep.uid in self.known_dma[eng]:
                return
            self.known_dma[eng].add(dep.uid)
            waits.append(dep)
            return
        if dep.eng == eng and eng == "pe":
            return
        if self.known[eng].get(dep.eng, -1) >= dep.idx:
            return
        self.known[eng][dep.eng] = dep.idx
        waits.append(dep)

    def op(self, eng, fn, reads=(), writes=()):
        waits = []
        for b in reads:
            self._need(eng, b.lw, waits)
        for b in writes:
            self._need(eng, b.lw, waits)
            for r in b.rd:
                self._need(eng, r, waits)
        e = self.E[eng]
        if eng in DMAQ:
            k = self.dma_count % N_DMA_SEMS
            self.dma_count += 1
            prev = self.dma_last[k]
            if prev is not None and prev.uid not in self.known_dma[eng]:
                self.known_dma[eng].add(prev.uid)
                waits.append(prev)
            self.dma_total[k] += 16
            o = Op(eng, self.count[eng], self.dsems[k], self.dma_total[k])
            self.dma_last[k] = o
        else:
            o = Op(eng, self.count[eng], self.sems[eng], self.count[eng] + 1)
            self.last[eng] = o
        self.count[eng] += 1
        for d in waits:
            e.wait_ge(d.sem, d.val)
        ins = fn(e)
        ins.then_inc(o.sem, 16 if eng in DMAQ else 1)
        self.ninst += 1
        for b in reads:
            b.rd.append(o)
        for b in writes:
            b.lw = o
            b.rd = []
        return o

    def barrier(self):
        deps = [self.last[e] for e in COMPUTE if self.last[e] is not None]
        deps += [d for d in self.dma_last if d is not None]
        for eng in COMPUTE + DMAQ:
            e = self.E[eng]
            for d in deps:
                if d.eng == eng and eng in COMPUTE:
                    continue
                if d.eng in DMAQ:
                    if d.uid in self.known_dma[eng]:
                        continue
                    self.known_dma[eng].add(d.uid)
                else:
                    if self.known[eng].get(d.eng, -1) >= d.idx:
                        continue
                    self.known[eng][d.eng] = d.idx
                e.wait_ge(d.sem, d.val)

    def finish(self, final_ops):
        for d in final_ops:
            self.nc.sync.wait_ge(d.sem, d.val)


def build(n_layers=L, tap=None, stop=None):
    nc = bass.Bass("TRN2", target_bir_lowering=False)

    def din(name, shape, dt=F32):
        return nc.dram_tensor(name, list(shape), dt, kind="ExternalInput").ap()

    xT_in = din("xT", [NG, 128, T])
    cT_in = din("cT", [128, NG, 2])
    ada_w = din("ada_w", [L, D, 6 * D])
    ada_bT = din("ada_bT", [128, L, 48])
    nwT = din("nwT", [128, L, 2, NG])
    w_in = din("w_in", [L, D, PROJ])
    lbl = din("lbl", [128, 2, L, 4])
    onw = din("onw", [128, L])
    qkw = din("qkw", [128, L, 2])
    bias_tab = din("bias_tab", [L, 8, 128, 3 * 6 * 256])
    cwT = din("cwT", [128, L, 3, 4])
    w_ba = din("w_ba", [L, 512, D])
    w_bb = din("w_bb", [L, 512, D])
    w_bc = din("w_bc", [L, 512, D])
    w_out = din("w_out", [L, D, D])
    w1 = din("w1", [L, D, 4 * D])
    w2 = din("w2", [L, 4 * D, D])
    cos_in = din("cosT", [128, TL])
    sin_in = din("sinT", [128, TL])
    cst_in = din("cst", [128, 128 * 5])
    cst2_in = din("cst2", [128, 128])
    outT = nc.dram_tensor("outT", [NG, 128, TL], F32, kind="ExternalOutput").ap()
    tap_out = None
    if tap is not None:
        tap_out = nc.dram_tensor("tap", list(tap[1]), F32, kind="ExternalOutput").ap()

    def dscr(name, shape, dt):
        return nc.dram_tensor(name, list(shape), dt, kind="Internal").ap()

    xs = dscr("xs", [NG, 128, T], F32)
    ya_d = dscr("ya_d", [4, 128, T], BF16)
    yb_d = dscr("yb_d", [4, 128, T], BF16)
    yc_d = dscr("yc_d", [4, 128, T], BF16)
    of_d = dscr("of_d", [4, 128, T], BF16)

    final_ops = []
    with ExitStack() as top:
        S = Sched(nc, top)

        uid = [0]

        def sb(st, name, shape, dt):
            uid[0] += 1
            return st.enter_context(nc.sbuf_tensor("%s_s%d" % (name, uid[0]), list(shape), dt))

        def pst(st, name, shape, dt=F32):
            uid[0] += 1
            return st.enter_context(nc.psum_tensor("%s_p%d" % (name, uid[0]), list(shape), dt))

        def mm(out_ap, pairs, reads, wbuf, start=True, stop=True):
            n = len(pairs)

            def fn(e):
                ins = None
                for i, (l_, r_) in enumerate(pairs):
                    ins = e.matmul(out_ap, l_, r_, start=(start and i == 0), stop=(stop and i == n - 1))
                return ins
            return S.op("pe", fn, reads=reads, writes=[wbuf])

        def act(out, in_, func, reads, wbuf, scale=1.0, bias=None):
            if bias is None:
                return S.op("act", lambda e: e.activation(out=out, in_=in_, func=func, scale=scale),
                            reads=reads, writes=[wbuf])
            return S.op("act", lambda e: e.activation(out=out, in_=in_, func=func, scale=scale, bias=bias),
                        reads=reads, writes=[wbuf])

        def tt(out, a, b, op, reads, wbuf):
            return S.op("dve", lambda e: e.tensor_tensor(out=out, in0=a, in1=b, op=op), reads=reads, writes=[wbuf])

        def ts(out, a, s1, op0, reads, wbuf, s2=None, op1=None):
            if op1 is None:
                return S.op("dve", lambda e: e.tensor_scalar(out=out, in0=a, scalar1=s1, scalar2=None, op0=op0),
                            reads=reads, writes=[wbuf])
            return S.op("dve", lambda e: e.tensor_scalar(out=out, in0=a, scalar1=s1, scalar2=s2, op0=op0, op1=op1),
                        reads=reads, writes=[wbuf])

        def stt(out, a, s, b, op0, op1, reads, wbuf):
            return S.op("dve", lambda e: e.scalar_tensor_tensor(out=out, in0=a, scalar=s, in1=b, op0=op0, op1=op1),
                        reads=reads, writes=[wbuf])

        def dma(q, out, in_, reads, wbuf):
            return S.op(q, lambda e: e.dma_start(out=out, in_=in_), reads=reads, writes=[wbuf] if wbuf else [])

        P = ExitStack()
        top.enter_context(P)
        cst = sb(P, "cst", [128, 5, 128], BF16)
        maskb = sb(P, "maskb", [128, 128], BF16)
        cf = sb(P, "cf", [128, 8], F32)
        onesf = sb(P, "onesf", [128, 512], F32)
        modT = sb(P, "modT", [128, L, 48, 2], F32)
        nw = sb(P, "nw", [128, L, 2, NG], F32)
        av = sb(P, "av", [128, 2, 2, NG], F32)
        lbt = sb(P, "lbt", [128, 2, L, 4], F32)
        oml = sb(P, "oml", [128, 2, L, 4], F32)
        onwt = sb(P, "onwt", [128, L], F32)
        qkwt = sb(P, "qkwt", [128, L, 2], F32)
        cwt = sb(P, "cwt", [128, L, 3, 4], F32)
        bc = Buf("const")
        bmod = Buf("mod")
        bav = Buf("av")
        dma("pq", cst[:].rearrange("p a b -> p (a b)"), cst_in[:, :], [], bc)
        dma("pq", maskb[:], cst2_in[:, :], [], bc)
        dma("sp", nw[:].rearrange("p a b c -> p (a b c)"), nwT.rearrange("p a b c -> p (a b c)"), [], bc)
        dma("sp", lbt[:].rearrange("p a b c -> p (a b c)"), lbl.rearrange("p a b c -> p (a b c)"), [], bc)
        dma("sp", onwt[:], onw[:, :], [], bc)
        dma("sp", qkwt[:].rearrange("p a b -> p (a b)"), qkw.rearrange("p a b -> p (a b)"), [], bc)
        dma("sp", cwt[:].rearrange("p a b c -> p (a b c)"), cwT.rearrange("p a b c -> p (a b c)"), [], bc)
        S.op("dve", lambda e: e.memset(cf[:, 0:1], EPS), writes=[bc])
        S.op("dve", lambda e: e.memset(cf[:, 1:2], 1.0), writes=[bc])
        S.op("dve", lambda e: e.memset(cf[:, 2:3], 0.0), writes=[bc])
        S.op("dve", lambda e: e.memset(onesf[:], 1.0), writes=[bc])
        ones_b = cst[:, 0, :]
        ident = cst[:, 1, :]
        blockones = cst[:, 2, :]
        perm = cst[:, 3, :]
        maskf = cst[:, 4, :]
        eps_ap = cf[:, 0:1]
        one_ap = cf[:, 1:2]

        act(lbt[:].rearrange("p a b c -> p (a b c)"), lbt[:].rearrange("p a b c -> p (a b c)"), AF.Exp, [bc], bc)
        with ExitStack() as st:
            ssum = sb(st, "ssum", [128, 2, 4], F32)
            tt(ssum[:], lbt[:, :, 0, :], lbt[:, :, 1, :], ALU.add, [bc], bc)
            tt(ssum[:], ssum[:], lbt[:, :, 2, :], ALU.add, [bc], bc)
            tt(ssum[:], ssum[:], lbt[:, :, 3, :], ALU.add, [bc], bc)
            S.op("dve", lambda e: e.reciprocal(out=ssum[:], in_=ssum[:]), reads=[bc], writes=[bc])
            for j in range(L):
                tt(lbt[:, :, j, :], lbt[:, :, j, :], ssum[:], ALU.mult, [bc], bc)
            tt(lbt[:, :, 2, :], lbt[:, :, 2, :], lbt[:, :, 1, :], ALU.add, [bc], bc)
            tt(lbt[:, :, 3, :], lbt[:, :, 3, :], lbt[:, :, 2, :], ALU.add, [bc], bc)
            S.op("dve", lambda e: e.memset(lbt[:, :, 0, :], 0.0), writes=[bc])
            ts(oml[:].rearrange("p a b c -> p (a b c)"), lbt[:].rearrange("p a b c -> p (a b c)"), -1.0, ALU.mult,
               [bc], bc, 1.0, ALU.add)
            ts(qkwt[:, :, 0], qkwt[:, :, 0], 0.125, ALU.mult, [bc], bc)
            S.barrier()

        with ExitStack() as st:
            ct = sb(st, "ct", [128, NG, 2], F32)
            sc = sb(st, "sc", [128, NG, 2], F32)
            abt = sb(st, "abt", [128, L, 48], F32)
            wst = [sb(st, "wst%d" % i, [128, NG, 768], F32) for i in range(2)]
            bw = [Buf(), Buf()]
            mps = pst(st, "mps", [128, 512])
            bps = Buf()
            b0 = Buf()
            dma("sp", ct[:].rearrange("p a b -> p (a b)"), cT_in.rearrange("p a b -> p (a b)"), [], b0)
            dma("sp", abt[:].rearrange("p a b -> p (a b)"), ada_bT.rearrange("p a b -> p (a b)"), [], b0)
            ctf = ct[:].rearrange("p a b -> p (a b)")
            scf = sc[:].rearrange("p a b -> p (a b)")
            act(scf, ctf, AF.Exp, [b0], b0, scale=-1.0)
            ts(scf, scf, 1.0, ALU.add, [b0], b0)
            S.op("dve", lambda e: e.reciprocal(out=scf, in_=scf), reads=[b0], writes=[b0])
            tt(scf, scf, ctf, ALU.mult, [b0], b0)
            k = 0
            for l in range(n_layers):
                for s_ in range(8):
                    w = wst[k % 2]
                    dma("sp", w[:], ada_w[l, :, s_ * 768:(s_ + 1) * 768].rearrange("(kc p) c -> p kc c", p=128),
                        [], bw[k % 2])
                    for j in range(6):
                        grp = s_ * 6 + j
                        mm(mps[:, grp * 2:grp * 2 + 2],
                           [(w[:, kc, j * 128:(j + 1) * 128], sc[:, kc, :]) for kc in range(NG)],
                           [bw[k % 2], b0], bps)
                    k += 1
                tt(modT[:, l, :, :], mps[:, 0:96].rearrange("p (a b) -> p a b", b=2),
                   abt[:, l, :].unsqueeze(2).to_broadcast([128, 48, 2]), ALU.add, [bps, b0], bmod)
            S.barrier()

        bxs = [Buf("xs%d" % i) for i in range(17)]

        def xs_bufs(t0, nb):
            return [bxs[i] for i in range(t0 // 256, (t0 + nb) // 256)]

        with ExitStack() as st:
            cp = [sb(st, "cp%d" % i, [128, NG, 512], F32) for i in range(2)]
            bcp = [Buf(), Buf()]
            for bi, (t0, nb) in enumerate(BLOCKS):
                c_ = cp[bi % 2]
                dma("sp", c_[:, :, :nb], xT_in[:, :, t0:t0 + nb].rearrange("g p t -> p g t"), [], bcp[bi % 2])
                S.op("sp", lambda e: e.dma_start(out=xs[:, :, t0:t0 + nb].rearrange("g p t -> p g t"), in_=c_[:, :, :nb]),
                     reads=[bcp[bi % 2]], writes=xs_bufs(t0, nb))
            S.barrier()

        def rms_rstd(st_tmp, x_ap_g, ngrp, nb, sq, ss_ps, bsq, bss, rstd, brstd, lhs_ones, inv_n, xreads):
            for g in range(ngrp):
                act(sq[:, g, :nb], x_ap_g(g), AF.Square, xreads, bsq)
            mm(ss_ps[:, :nb], [(lhs_ones, sq[:, g, :nb]) for g in range(ngrp)], [bsq, bc], bss)
            act(rstd[:, :nb], ss_ps[:, :nb], AF.Ln, [bss, bc], brstd, scale=inv_n, bias=eps_ap)
            act(rstd[:, :nb], rstd[:, :nb], AF.Exp, [brstd], brstd, scale=-0.5)

        for l in range(n_layers):
            if stop == "P0":
                break
            for ni, (sci) in enumerate((1, 4)):
                for r in range(2):
                    S.op("dve", lambda e: e.scalar_tensor_tensor(
                        out=av[:, ni, r, :], in0=modT[:, l, sci * 8:(sci + 1) * 8, r], scalar=1.0,
                        in1=nw[:, l, ni, :], op0=ALU.add, op1=ALU.mult), reads=[bmod, bc], writes=[bav])

            def mod(idx, g, r):
                return modT[:, l, idx * 8 + g, r:r + 1]

            LS = ExitStack()
            hT = sb(LS, "hT", [128, NG, T], BF16)
            bh = [Buf("h%d" % i) for i in range(9)]

            with ExitStack() as st:
                xb = [sb(st, "xb%d" % i, [128, NG, 512], F32) for i in range(2)]
                bxb = [Buf(), Buf()]
                sq = sb(st, "sq", [128, NG, 512], BF16)
                rstd = sb(st, "rstd", [128, 512], F32)
                tmp = [sb(st, "tmp%d" % i, [128, 512], F32) for i in range(2)]
                btmp = [Buf(), Buf()]
                ss_ps = pst(st, "ss_ps", [128, 512])
                bsq, bss, brstd = Buf(), Buf(), Buf()
                for bi, (t0, nb) in enumerate(BLOCKS):
                    r = 1 if bi == 0 else 0
                    x_ = xb[bi % 2]
                    dma("sp", x_[:, :, :nb], xs[:, :, t0:t0 + nb].rearrange("g p t -> p g t"),
                        xs_bufs(t0, nb), bxb[bi % 2])
                    rms_rstd(st, lambda g: x_[:, g, :nb], NG, nb, sq, ss_ps, bsq, bss, rstd, brstd,
                             ones_b, 1.0 / D, [bxb[bi % 2]])
                    for g in range(NG):
                        tp_ = tmp[g % 2]
                        tt(tp_[:, :nb], x_[:, g, :nb], rstd[:, :nb], ALU.mult, [bxb[bi % 2], brstd], btmp[g % 2])
                        act(hT[:, g, t0:t0 + nb], tp_[:, :nb], AF.Identity, [btmp[g % 2], bav, bmod], bh[bi],
                            scale=av[:, 0, r, g:g + 1], bias=mod(0, g, r))
                S.barrier()

            if tap is not None and tap[0] == "hT" and l == tap[2]:
                with ExitStack() as st:
                    tb = sb(st, "tb", [128, T], F32)
                    btb = Buf()
                    for g in range(NG):
                        act(tb[:], hT[:, g, :], AF.Copy, bh, btb)
                        final_ops.append(dma("sp", tap_out[g], tb[:], [btb], None))
                    S.barrier()

            if stop == "A":
                LS.close()
                S.barrier()
                break
            def make_vtok(st, vt, bvt, col0):
                with ExitStack() as s2:
                    wv = sb(s2, "wv", [128, NG, 512], BF16)
                    bwv = Buf()
                    vps = [pst(s2, "vps%d" % i, [128, 512]) for i in range(2)]
                    bvp = [Buf(), Buf()]
                    dma("pq", wv[:], w_in[l, :, col0:col0 + 512].rearrange("(kc p) c -> p kc c", p=128), [], bwv)
                    for ti in range(T // 128):
                        bi = 0 if ti < 2 else 1 + (ti - 2) // 4
                        mm(vps[ti % 2][:], [(hT[:, kc, ti * 128:(ti + 1) * 128], wv[:, kc, :]) for kc in range(NG)],
                           [bh[bi], bwv], bvp[ti % 2])
                        if ti % 2 == 0:
                            act(vt[:, ti, :], vps[ti % 2][:], AF.Copy, [bvp[ti % 2]], bvt)
                        else:
                            S.op("dve", lambda e: e.tensor_copy(out=vt[:, ti, :], in_=vps[ti % 2][:]),
                                 reads=[bvp[ti % 2]], writes=[bvt])
                    S.barrier()

            with ExitStack() as st:
                vt = sb(st, "vt", [128, T // 128, 512], BF16)
                bvt = Buf()
                make_vtok(st, vt, bvt, OFF["hi"])
                wq = sb(st, "wq", [128, NG, 512], BF16)
                wf = sb(st, "wf", [128, NG, 512], BF16)
                wg = sb(st, "wg", [128, NG, 512], BF16)
                bwq, bwf, bwg = Buf(), Buf(), Buf()
                dma("pq", wq[:], w_in[l, :, OFF["hq"]:OFF["hq"] + 512].rearrange("(kc p) c -> p kc c", p=128), [], bwq)
                dma("pq", wg[:], w_in[l, :, OFF["hg"]:OFF["hg"] + 512].rearrange("(kc p) c -> p kc c", p=128), [], bwg)
                NT = 6
                tf = [sb(st, "tf%d" % i, [128, 512], F32) for i in range(NT)]
                btf = [Buf() for _ in range(NT)]
                to_ = [[sb(st, "to%d_%d" % (i, k), [128, 512], F32) for i in range(3)] for k in range(2)]
                bto = [[Buf() for _ in range(3)] for _ in range(2)]
                gbuf = sb(st, "gbuf", [128, 640], F32)
                bgb = Buf()
                S.op("dve", lambda e: e.memset(gbuf[:], 0.0), writes=[bgb])
                csc = [sb(st, "csc%d" % k, [128, 4, 8], F32) for k in range(2)]
                bcs = [Buf(), Buf()]
                qp = [sb(st, "qp%d" % k, [128, 512], BF16) for k in range(2)]
                bqp = [Buf(), Buf()]
                yq = [sb(st, "yq%d" % k, [128, 512], BF16) for k in range(2)]
                byq = [Buf(), Buf()]
                sqo = [sb(st, "sqo%d" % k, [128, 512], BF16) for k in range(2)]
                bsqo = [Buf(), Buf()]
                kp2 = [sb(st, "kp%d" % k, [128, 512], BF16) for k in range(2)]
                bkp2 = [Buf(), Buf()]
                kt = sb(st, "kt", [128, 4, 128], BF16)
                bkt = Buf()
                atb = sb(st, "atb", [128, 4, 128], BF16)
                batb = Buf()
                Sst = sb(st, "Sst", [128, 4, 128], F32)
                bS = [Buf() for _ in range(4)]
                sbf = [[sb(st, "sbf%d_%d" % (i, k), [128, 128], BF16) for i in range(2)] for k in range(2)]
                bsbf = [[Buf(), Buf()], [Buf(), Buf()]]
                tuS = [sb(st, "tuS%d" % k, [128, 8, 128], F32) for k in range(2)]
                btuS = [Buf(), Buf()]
                ofb = [sb(st, "ofb%d" % k, [128, 512], BF16) for k in range(2)]
                bofb = [Buf(), Buf()]
                ps_f = pst(st, "ps_f", [128, 512])
                ps_q = pst(st, "ps_q", [128, 512])
                ps_t = pst(st, "ps_t", [128, 4, 128], BF16)
                ps_a = pst(st, "ps_a", [128, 4, 128])
                ps_u = pst(st, "ps_u", [128, 8, 128])
                ps_o = [pst(st, "ps_o%d" % k, [128, 512]) for k in range(2)]
                bpf, bpq, bpt, bpa, bpu = [Buf() for _ in range(5)]
                bpo = [Buf(), Buf()]
                bof = [[Buf() for _ in range(9)] for _ in range(4)]
                bya = [Buf() for _ in range(17)]

                for dr in range(2):
                    off_f = OFF["hff"] if dr == 0 else OFF["hfb"]
                    dma("pq", wf[:], w_in[l, :, off_f:off_f + 512].rearrange("(kc p) c -> p kc c", p=128), [], bwf)
                    for hh in range(4):
                        S.op("dve", lambda e: e.memset(Sst[:, hh, :], 0.0), writes=[bS[hh]])
                    order = list(range(9)) if dr == 0 else [0] + list(range(8, 0, -1))
                    mask = maskf if dr == 0 else maskb[:]
                    wi, ei = (0, 1) if dr == 0 else (1, 0)
                    sgn = 1.0 if dr == 0 else -1.0
                    goff = 1 if dr == 0 else 0
                    for bi in order:
                        t0, nb = BLOCKS[bi]
                        nch = nb // 64
                        npair = nb // 128
                        corder = list(range(nch)) if dr == 0 else list(range(nch - 1, -1, -1))

                        def front(hd, k):
                            kp, bkp = kp2[k], bkp2[k]
                            cs = slice(hd * 128, (hd + 1) * 128)
                            lb_ap = lbt[:, dr, l, hd:hd + 1]
                            oml_ap = oml[:, dr, l, hd:hd + 1]
                            cs_ = csc[k]
                            mm(ps_f[:, :nb], [(wf[:, kc, cs], hT[:, kc, t0:t0 + nb]) for kc in range(NG)],
                               [bwf, bh[bi]], bpf)
                            mm(ps_q[:, :nb], [(wq[:, kc, cs], hT[:, kc, t0:t0 + nb]) for kc in range(NG)],
                               [bwq, bh[bi]], bpq)
                            E_, L1, L2, Q4, A5, E6 = tf[0], tf[1], tf[2], tf[3], tf[4], tf[5]
                            act(E_[:, :nb], ps_f[:, :nb], AF.Exp, [bpf], btf[0], scale=-1.0)
                            act(L1[:, :nb], E_[:, :nb], AF.Ln, [btf[0], bc], btf[1], scale=1.0, bias=one_ap)
                            act(L2[:, :nb], E_[:, :nb], AF.Ln, [btf[0], bc], btf[2], scale=lb_ap, bias=one_ap)
                            tt(L2[:, :nb], L2[:, :nb], L1[:, :nb], ALU.subtract, [btf[1], btf[2]], btf[2])
                            S.op("dve", lambda e: e.tensor_tensor_scan(
                                out=gbuf[:, 1:nb + 1], data0=onesf[:, :nb], data1=L2[:, :nb], initial=0.0,
                                op0=ALU.mult, op1=ALU.add), reads=[btf[2], bc], writes=[bgb])
                            act(L1[:, :nb], L1[:, :nb], AF.Exp, [btf[1]], btf[1], scale=-1.0)
                            stt(E_[:, :nb], E_[:, :nb], oml_ap, L1[:, :nb], ALU.mult, ALU.mult, [btf[0], btf[1], bc], btf[0])
                            act(Q4[:, :nb], ps_q[:, :nb], AF.Exp, [bpq], btf[3], scale=-1.0)
                            ts(Q4[:, :nb], Q4[:, :nb], 1.0, ALU.add, [btf[3]], btf[3])
                            S.op("dve", lambda e: e.reciprocal(out=Q4[:, :nb], in_=Q4[:, :nb]), reads=[btf[3]], writes=[btf[3]])
                            tt(Q4[:, :nb], Q4[:, :nb], ps_q[:, :nb], ALU.mult, [btf[3], bpq], btf[3])
                            ref = gbuf[:, 32:32 + nb].rearrange("p (c t) -> p c t", t=64)[:, :, 0:1].to_broadcast([128, nch, 64])
                            tt(A5[:, :nb].rearrange("p (c t) -> p c t", t=64),
                               gbuf[:, goff:goff + nb].rearrange("p (c t) -> p c t", t=64), ref, ALU.subtract,
                               [bgb], btf[4])
                            act(E6[:, :nb], A5[:, :nb], AF.Exp, [btf[4]], btf[5], scale=sgn)
                            act(A5[:, :nb], A5[:, :nb], AF.Exp, [btf[4]], btf[4], scale=-sgn)
                            tt(qp[k][:, :nb], Q4[:, :nb], E6[:, :nb], ALU.mult, [btf[3], btf[5]], bqp[k])
                            tt(kp[:, :nb], E_[:, :nb], A5[:, :nb], ALU.mult, [btf[0], btf[4]], bkp)
                            g0 = gbuf[:, 0:nb].rearrange("p (c t) -> p c t", t=64)[:, :, 0]
                            g32 = gbuf[:, 32:32 + nb].rearrange("p (c t) -> p c t", t=64)[:, :, 0]
                            g64 = gbuf[:, 64:64 + nb].rearrange("p (c t) -> p c t", t=64)[:, :, 0]
                            tt(cs_[:, 0, :nch], g32, g0, ALU.subtract, [bgb], bcs[k])
                            tt(cs_[:, 1, :nch], g64, g32, ALU.subtract, [bgb], bcs[k])
                            act(cs_[:, 0:2, :nch], cs_[:, 0:2, :nch], AF.Exp, [bcs[k]], bcs[k])
                            tt(cs_[:, 2, :nch], cs_[:, 0, :nch], cs_[:, 1, :nch], ALU.mult, [bcs[k]], bcs[k])

                        def front_b(hd, k):
                            kp, bkp = kp2[k], bkp2[k]
                            cs = slice(hd * 128, (hd + 1) * 128)
                            cs_ = csc[k]
                            for j in range(npair):
                                S.op("pe", lambda e: e.transpose(out=ps_t[:, j, :], in_=kp[:, j * 128:(j + 1) * 128], identity=ident),
                                     reads=[bkp, bc], writes=[bpt])
                            act(kt[:, :npair, :], ps_t[:, :npair, :], AF.Copy, [bpt], bkt)
                            for j in range(npair):
                                mm(ps_a[:, j, :], [(kp[:, j * 128:(j + 1) * 128], qp[k][:, j * 128:(j + 1) * 128])],
                                   [bkp, bqp[k]], bpa)
                            for j in range(npair):
                                tt(atb[:, j, :], ps_a[:, j, :], mask, ALU.mult, [bpa, bc], batb)
                            for c in range(nch):
                                j, par = c // 2, c % 2
                                ti = t0 // 128 + j
                                mm(ps_u[:, par * 4 + j, :], [(kt[par * 64:(par + 1) * 64, j, :], vt[par * 64:(par + 1) * 64, ti, cs])],
                                   [bkt, bvt], bpu)
                            for par in range(2):
                                e_b = cs_[:, ei, 0:nch].rearrange("p (j r) -> p j r", r=2)[:, :, par:par + 1].to_broadcast([128, npair, 128])
                                tt(tuS[k][:, par * 4:par * 4 + npair, :], ps_u[:, par * 4:par * 4 + npair, :], e_b, ALU.mult,
                                   [bpu, bcs[k]], btuS[k])
                            for j in range(npair):
                                ti = t0 // 128 + j
                                mm(ps_o[k][:, j * 128:(j + 1) * 128], [(vt[:, ti, cs], atb[:, j, :])], [bvt, batb], bpo[k],
                                   start=(j == 0), stop=False)

                        def chain_step(hd, k, ci, c):
                            cs_ = csc[k]
                            sb_ = sbf[k][ci % 2]
                            act(sb_[:], Sst[:, hd, :], AF.Identity, [bS[hd], bcs[k]], bsbf[k][ci % 2], scale=cs_[:, wi, c:c + 1])
                            mm(ps_o[k][:, c * 64:(c + 1) * 64], [(sb_[:], qp[k][:, c * 64:(c + 1) * 64])],
                               [bsbf[k][ci % 2], bqp[k]], bpo[k], start=False, stop=True)
                            stt(Sst[:, hd, :], Sst[:, hd, :], cs_[:, 2, c:c + 1], tuS[k][:, (c % 2) * 4 + c // 2, :], ALU.mult, ALU.add,
                                [bS[hd], bcs[k], btuS[k]], bS[hd])

                        def output(hd, k):
                            cs = slice(hd * 128, (hd + 1) * 128)
                            if dr == 0:
                                act(ofb[k][:, :nb], ps_o[k][:, :nb], AF.Copy, [bpo[k]], bofb[k])
                                dma("sp", of_d[hd, :, t0:t0 + nb], ofb[k][:, :nb], [bofb[k]], bof[hd][bi])
                                return
                            O_, G6, G7 = to_[k]
                            bO, bG6, bG7 = bto[k]
                            dma("sp", ofb[k][:, :nb], of_d[hd, :, t0:t0 + nb], [bof[hd][bi]], bofb[k])
                            tt(O_[:, :nb], ps_o[k][:, :nb], ofb[k][:, :nb], ALU.add, [bpo[k], bofb[k]], bO)
                            act(sqo[k][:, :nb], O_[:, :nb], AF.Square, [bO], bsqo[k])
                            mm(ps_q[:, :nb], [(ones_b, sqo[k][:, :nb])], [bsqo[k], bc], bpq)
                            act(G6[:, :nb], ps_q[:, :nb], AF.Ln, [bpq, bc], bG6, scale=1.0 / 128, bias=eps_ap)
                            act(G6[:, :nb], G6[:, :nb], AF.Exp, [bG6], bG6, scale=-0.5)
                            stt(O_[:, :nb], O_[:, :nb], onwt[:, l:l + 1], G6[:, :nb], ALU.mult, ALU.mult, [bO, bG6, bc], bO)
                            mm(ps_f[:, :nb], [(wg[:, kc, cs], hT[:, kc, t0:t0 + nb]) for kc in range(NG)],
                               [bwg, bh[bi]], bpf)
                            act(G7[:, :nb], ps_f[:, :nb], AF.Exp, [bpf], bG7, scale=-1.0)
                            ts(G7[:, :nb], G7[:, :nb], 1.0, ALU.add, [bG7], bG7)
                            S.op("dve", lambda e: e.reciprocal(out=G7[:, :nb], in_=G7[:, :nb]), reads=[bG7], writes=[bG7])
                            tt(G7[:, :nb], G7[:, :nb], ps_f[:, :nb], ALU.mult, [bG7, bpf], bG7)
                            tt(yq[k][:, :nb], O_[:, :nb], G7[:, :nb], ALU.mult, [bO, bG7], byq[k])
                            dma("sp", ya_d[hd, :, t0:t0 + nb], yq[k][:, :nb], [byq[k]], bya[t0 // 256])
                            if nb == 512:
                                bya[t0 // 256 + 1] = bya[t0 // 256]

                        for hp in range(2):
                            front(2 * hp, 0)
                            front(2 * hp + 1, 1)
                            front_b(2 * hp, 0)
                            front_b(2 * hp + 1, 1)
                            for ci, c in enumerate(corder):
                                chain_step(2 * hp, 0, ci, c)
                                chain_step(2 * hp + 1, 1, ci, c)
                            output(2 * hp, 0)
                            output(2 * hp + 1, 1)
                S.barrier()

            def tap_dram(name, src, ngrp, dt, rbufs):
                if tap is None or tap[0] != name or l != tap[2]:
                    return
                with ExitStack() as st:
                    ta = sb(st, "tap_a", [128, T], dt)
                    tb = sb(st, "tap_b", [128, T], F32)
                    bta, btb = Buf(), Buf()
                    for g in range(ngrp):
                        dma("sp", ta[:], src[g], rbufs, bta)
                        act(tb[:], ta[:], AF.Copy, [bta], btb)
                        final_ops.append(dma("sp", tap_out[g], tb[:], [btb], None))
                    S.barrier()

            tap_dram("ya", ya_d, 4, BF16, bya)
            tap_dram("of", of_d, 4, BF16, [])
            if stop == "B":
                LS.close()
                S.barrier()
                break

            byb = [Buf() for _ in range(17)]
            with ExitStack() as st:
                vt = sb(st, "vn", [128, T // 128, 512], BF16)
                bvt = Buf()
                make_vtok(st, vt, bvt, OFF["nv"])
                wqk = sb(st, "wqk", [128, 2, NG, 128], BF16)
                bwqk = Buf()
                qT = sb(st, "qT", [128, T], BF16)
                kT = sb(st, "kT", [128, T], BF16)
                bq_ = [Buf() for _ in range(9)]
                bk_ = [Buf() for _ in range(9)]
                cs_t = [sb(st, "cst%d" % i, [128, 2, 512], F32) for i in range(2)]
                bcs_t = [Buf(), Buf()]
                sqn2 = [sb(st, "sqn%d" % i, [128, 512], BF16) for i in range(2)]
                rs2 = [sb(st, "rsn%d" % i, [128, 512], F32) for i in range(2)]
                qn2 = [sb(st, "qn%d" % i, [128, 512], BF16) for i in range(2)]
                t12 = [sb(st, "t1n%d" % i, [128, 512], F32) for i in range(2)]
                t22 = [sb(st, "t2n%d" % i, [128, 512], F32) for i in range(2)]
                bsqn2, brs2, bqn2, bt12, bt22 = [[Buf(), Buf()] for _ in range(5)]
                bt = sb(st, "bt", [128, 3, 6, 256], F32)
                bbt = Buf()
                tmpS = [sb(st, "tmpS%d" % i, [128, 2, 256], F32) for i in range(2)]
                btmpS = [Buf(), Buf()]
                Pt = [sb(st, "Pt%d" % i, [128, 2, 256], BF16) for i in range(8)]
                bPt = [Buf() for _ in range(8)]
                rsum2 = [sb(st, "rsum%d" % i, [64, 256], F32) for i in range(2)]
                brsum2 = [Buf(), Buf()]
                yo = [sb(st, "yo%d" % i, [64, 256], BF16) for i in range(2)]
                byo = [Buf(), Buf()]
                qbn = [0]
                for gq in range(4):
                    dma("pq", wqk[:, 0], w_in[l, :, OFF["nq"] + gq * 128:OFF["nq"] + (gq + 1) * 128]
                        .rearrange("(kc p) c -> p kc c", p=128), [], bwqk)
                    dma("pq", wqk[:, 1], w_in[l, :, OFF["nk"] + gq * 128:OFF["nk"] + (gq + 1) * 128]
                        .rearrange("(kc p) c -> p kc c", p=128), [], bwqk)
                    SP_ = ExitStack()
                    ps_p2 = [pst(SP_, "ps_p%d" % i, [128, 512]) for i in range(2)]
                    ps_n2 = [pst(SP_, "ps_n%d" % i, [128, 512]) for i in range(2)]
                    ps_r2 = [pst(SP_, "ps_r%d" % i, [128, 512]) for i in range(2)]
                    bpp2, bpn2, bpr2 = [[Buf(), Buf()] for _ in range(3)]
                    for bi, (t0, nb) in enumerate(BLOCKS):
                        ct_ = cs_t[bi % 2]
                        if bi > 0:
                            dma("sp", ct_[:, 0, :], cos_in[:, t0 - 256:t0 + 256], [], bcs_t[bi % 2])
                            dma("sp", ct_[:, 1, :], sin_in[:, t0 - 256:t0 + 256], [], bcs_t[bi % 2])
                        QK = ((qT, bq_), (kT, bk_))
                        for qi in range(2):
                            mm(ps_p2[qi][:, :nb], [(wqk[:, qi, kc, :], hT[:, kc, t0:t0 + nb]) for kc in range(NG)],
                               [bwqk, bh[bi]], bpp2[qi])
                        for qi in range(2):
                            act(sqn2[qi][:, :nb], ps_p2[qi][:, :nb], AF.Square, [bpp2[qi]], bsqn2[qi])
                        for qi in range(2):
                            mm(ps_n2[qi][:, :nb], [(blockones, sqn2[qi][:, :nb])], [bsqn2[qi], bc], bpn2[qi])
                        for qi in range(2):
                            act(rs2[qi][:, :nb], ps_n2[qi][:, :nb], AF.Ln, [bpn2[qi], bc], brs2[qi], scale=1.0 / 64, bias=eps_ap)
                        for qi in range(2):
                            act(rs2[qi][:, :nb], rs2[qi][:, :nb], AF.Exp, [brs2[qi]], brs2[qi], scale=-0.5)
                        if bi == 0:
                            for qi in range(2):
                                dst, bdst = QK[qi]
                                stt(dst[:, t0:t0 + nb], ps_p2[qi][:, :nb], qkwt[:, l, qi:qi + 1], rs2[qi][:, :nb], ALU.mult, ALU.mult,
                                    [bpp2[qi], brs2[qi], bc], bdst[bi])
                        else:
                            for qi in range(2):
                                stt(qn2[qi][:, :nb], ps_p2[qi][:, :nb], qkwt[:, l, qi:qi + 1], rs2[qi][:, :nb], ALU.mult, ALU.mult,
                                    [bpp2[qi], brs2[qi], bc], bqn2[qi])
                            for qi in range(2):
                                mm(ps_r2[qi][:, :nb], [(perm, qn2[qi][:, :nb])], [bqn2[qi], bc], bpr2[qi])
                            for qi in range(2):
                                tt(t12[qi][:, :nb], qn2[qi][:, :nb], ct_[:, 0, :nb], ALU.mult, [bqn2[qi], bcs_t[bi % 2]], bt12[qi])
                            for qi in range(2):
                                dst, bdst = QK[qi]
                                tt(t22[qi][:, :nb], ps_r2[qi][:, :nb], ct_[:, 1, :nb], ALU.mult, [bpr2[qi], bcs_t[bi % 2]], bt22[qi])
                                tt(dst[:, t0:t0 + nb], t12[qi][:, :nb], t22[qi][:, :nb], ALU.add, [bt12[qi], bt22[qi]], bdst[bi])
                    S.barrier()
                    SP_.close()
                    SA_ = ExitStack()
                    ps_s2 = [pst(SA_, "ps_s2%d" % i, [128, 2, 256]) for i in range(4)]
                    bpss = [Buf() for _ in range(4)]
                    pso = [pst(SA_, "pso%d" % i, [64, 2, 256]) for i in range(2)]
                    bpso = [Buf(), Buf()]
                    for par in range(2):
                        h8 = gq * 2 + par
                        prt = slice(par * 64, par * 64 + 64)
                        dma("sp", bt[:].rearrange("p a b c -> p (a b c)"), bias_tab[l, h8], [], bbt)
                        def qk_part(qb):
                            nonlocal_q = {}
                            if qb < 16:
                                q0 = 256 + qb * 256
                                ts_ = min(max(4 * qb - 4, 0), 52)
                                pat = 0 if qb == 0 else (2 if qb == 15 else 1)
                                ktiles = [256 + (ts_ + 2 * j) * 64 for j in range(6)] + [0, 128]
                            else:
                                q0 = 0
                                pat = 0
                                ktiles = [0, 128]
                            npairs = len(ktiles) // 2
                            pb0 = (qbn[0] % 2) * 4
                            ok_ = qbn[0] % 2
                            qbn[0] += 1
                            for jj in range(npairs):
                                pss = ps_s2[jj]
                                for u in range(2):
                                    k0 = ktiles[2 * jj + u]
                                    mm(pss[:, u, :], [(kT[prt, k0:k0 + 128], qT[prt, q0:q0 + 256])], bk_ + bq_, bpss[jj])
                                Pj = Pt[pb0 + jj]
                                if qb < 16 and jj < 3:
                                    tS = tmpS[jj % 2]
                                    tt(tS[:], pss[:], bt[:, pat, 2 * jj:2 * jj + 2, :], ALU.add, [bpss[jj], bbt], btmpS[jj % 2])
                                    act(Pj[:], tS[:], AF.Exp, [btmpS[jj % 2]], bPt[pb0 + jj])
                                else:
                                    act(Pj[:], pss[:], AF.Exp, [bpss[jj]], bPt[pb0 + jj])
                            return (qb, q0, ktiles, npairs, pb0, ok_)

                        def pv_part(info):
                            qb, q0, ktiles, npairs, pb0, ok_ = info
                            pairs_o, pairs_m = [], []
                            for i, k0 in enumerate(ktiles):
                                Pj = Pt[pb0 + i // 2][:, i % 2, :]
                                pairs_o.append((vt[:, k0 // 128, h8 * 64:(h8 + 1) * 64], Pj))
                                pairs_m.append((cst[:, 0, 0:64], Pj))
                            po = pso[ok_]
                            mm(po[:, 0, :], pairs_o, [bvt] + bPt[pb0:pb0 + npairs], bpso[ok_])
                            mm(po[:, 1, :], pairs_m, [bc] + bPt[pb0:pb0 + npairs], bpso[ok_])
                            rsum = rsum2[ok_]
                            S.op("dve", lambda e: e.reciprocal(out=rsum[:], in_=po[:, 1, :]), reads=[bpso[ok_]], writes=[brsum2[ok_]])
                            y_ = yo[qb % 2]
                            tt(y_[:], po[:, 0, :], rsum[:], ALU.mult, [bpso[ok_], brsum2[ok_]], byo[qb % 2])
                            dma("sp", yb_d[gq, prt, q0:q0 + 256], y_[:], [byo[qb % 2]], byb[q0 // 256])

                        prev_info = qk_part(0)
                        for qb in range(1, 17):
                            info = qk_part(qb)
                            pv_part(prev_info)
                            prev_info = info
                        pv_part(prev_info)
                    S.barrier()
                    SA_.close()
                S.barrier()
            tap_dram("yb", yb_d, 4, BF16, byb)
            if stop == "C":
                LS.close()
                S.barrier()
                break

            byc = Buf()
            with ExitStack() as st:
                wc = sb(st, "wc", [128, 3, NG, 128], BF16)
                bwc = Buf()
                ul = sb(st, "ul", [128, TL + 2], F32)
                uc = sb(st, "uc", [128, TC + 2], F32)
                bu = Buf()
                Bs = sb(st, "Bs", [128, T], F32)
                bBs = Buf()
                acc = sb(st, "acc", [128, TL], F32)
                bacc = Buf()
                yout = sb(st, "yout", [128, T], BF16)
                byout = Buf()
                csb = sb(st, "csb", [128, 512], F32)
                bcsb = Buf()
                ps_c = [pst(st, "ps_c%d" % i, [128, 512]) for i in range(3)]
                bpc = [Buf() for _ in range(3)]
                for (u_, n) in ((uc, TC), (ul, TL)):
                    S.op("dve", lambda e: e.memset(u_[:, 0:1], 0.0), writes=[bu])
                    S.op("dve", lambda e: e.memset(u_[:, n + 1:n + 2], 0.0), writes=[bu])
                for gc in range(4):
                    for i, key in enumerate(("cb", "cc", "cx")):
                        dma("pq", wc[:, i], w_in[l, :, OFF[key] + gc * 128:OFF[key] + (gc + 1) * 128]
                            .rearrange("(kc p) c -> p kc c", p=128), [], bwc)
                    for bi, (t0, nb) in enumerate(BLOCKS):
                        for i in range(3):
                            mm(ps_c[i][:, :nb], [(wc[:, i, kc, :], hT[:, kc, t0:t0 + nb]) for kc in range(NG)],
                               [bwc, bh[bi]], bpc[i])
                        act(Bs[:, t0:t0 + nb], ps_c[0][:, :nb], AF.Copy, [bpc[0]], bBs)
                        act(csb[:, :nb], ps_c[1][:, :nb], AF.Copy, [bpc[1]], bcsb)
                        udst = uc[:, 1:1 + nb] if bi == 0 else ul[:, 1 + t0 - 256:1 + t0 - 256 + nb]
                        tt(udst, ps_c[2][:, :nb], csb[:, :nb], ALU.mult, [bpc[2], bcsb], bu)
                    for (u_, n, o0) in ((uc, TC, 0), (ul, TL, 256)):
                        ts(acc[:, :n], u_[:, 1:1 + n], cwt[:, l, 1, gc:gc + 1], ALU.mult, [bu, bc], bacc)
                        stt(acc[:, :n], u_[:, 0:n], cwt[:, l, 0, gc:gc + 1], acc[:, :n], ALU.mult, ALU.add, [bu, bc], bacc)
                        stt(acc[:, :n], u_[:, 2:2 + n], cwt[:, l, 2, gc:gc + 1], acc[:, :n], ALU.mult, ALU.add, [bu, bc], bacc)
                        tt(yout[:, o0:o0 + n], acc[:, :n], Bs[:, o0:o0 + n], ALU.mult, [bacc, bBs], byout)
                    dma("sp", yc_d[gc], yout[:], [byout], byc)
                S.barrier()
            tap_dram("yc", yc_d, 4, BF16, [byc])
            if stop == "D":
                LS.close()
                S.barrier()
                break

            with ExitStack() as st:
                wgt = sb(st, "wgt", [128, NG, 3072], BF16)
                wbr = sb(st, "wbr", [128, 3, 4, 1024], BF16)
                wo = sb(st, "wo", [128, NG, 1024], BF16)
                bwm = Buf()
                for br in range(3):
                    dma("pq", wgt[:, :, br * 1024:(br + 1) * 1024],
                        w_in[l, :, OFF["ga"] + br * 1024:OFF["ga"] + (br + 1) * 1024].rearrange("(kc p) c -> p kc c", p=128), [], bwm)
                for br, wsrc in enumerate((w_ba, w_bb, w_bc)):
                    dma("pq", wbr[:, br], wsrc[l].rearrange("(kc p) c -> p kc c", p=128), [], bwm)
                dma("pq", wo[:], w_out[l].rearrange("(kc p) c -> p kc c", p=128), [], bwm)
                yblk = [sb(st, "yblk%d" % i, [128, 3, 4, 256], BF16) for i in range(2)]
                byblk = [Buf(), Buf()]
                xb2 = [sb(st, "xb2%d" % i, [128, NG, 256], F32) for i in range(2)]
                bxb2 = [Buf(), Buf()]
                mixT = sb(st, "mixT", [128, NG, 256], BF16)
                bmix = Buf()
                ea = [sb(st, "ea%d" % i, [128, 256], F32) for i in range(6)]
                bea = [Buf() for _ in range(6)]
                macc = [sb(st, "macc%d" % i, [128, 256], F32) for i in range(2)]
                ctb = [sb(st, "ctb%d" % i, [128, 256], F32) for i in range(2)]
                bmacc = [Buf(), Buf()]
                bctb = [Buf(), Buf()]
                psb = [pst(st, "psb%d" % i, [128, 2, 256]) for i in range(6)]
                bpsb = [Buf() for _ in range(6)]
                ps_w = [pst(st, "ps_w%d" % i, [128, 512]) for i in range(2)]
                bpw = [Buf(), Buf()]
                item = 0
                print("M1 sbuf remaining", nc.sbuf_bytes_remaining)
                for bi, (t0, nb) in enumerate(BLOCKS256):
                    r = 1 if bi == 0 else 0
                    hbi = 0 if bi == 0 else 1 + (bi - 1) // 2
                    yb_ = yblk[bi % 2]
                    x_ = xb2[bi % 2]
                    for i, (src, rb) in enumerate(((ya_d, [bya[bi]]), (yb_d, [byb[bi]]), (yc_d, [byc]))):
                        dma("sp", yb_[:, i], src[:, :, t0:t0 + nb].rearrange("g p t -> p g t"), rb, byblk[bi % 2])
                    dma("sp", x_[:], xs[:, :, t0:t0 + nb].rearrange("g p t -> p g t"), [bxs[bi]], bxb2[bi % 2])
                    for m in range(NG):
                        ma = macc[m % 2]
                        bma = bmacc[m % 2]
                        for br in range(3):
                            k = item % 6
                            item += 1
                            pg = psb[k]
                            e_ = ea[k]
                            mm(pg[:, 0, :], [(wgt[:, kc, br * 1024 + m * 128:br * 1024 + (m + 1) * 128], hT[:, kc, t0:t0 + nb])
                                             for kc in range(NG)], [bwm, bh[hbi]], bpsb[k])
                            mm(pg[:, 1, :], [(wbr[:, br, kc, m * 128:(m + 1) * 128], yb_[:, br, kc, :]) for kc in range(4)],
                               [bwm, byblk[bi % 2]], bpsb[k])
                            act(e_[:], pg[:, 0, :], AF.Exp, [bpsb[k]], bea[k], scale=-1.0)
                            act(e_[:], e_[:], AF.Ln, [bea[k], bc], bea[k], scale=1.0, bias=one_ap)
                            act(e_[:], e_[:], AF.Exp, [bea[k]], bea[k], scale=-1.0)
                            if br == 0:
                                tt(ma[:], pg[:, 1, :], e_[:], ALU.mult, [bpsb[k], bea[k]], bma)
                            else:
                                cb_ = ctb[br % 2]
                                bcb = bctb[br % 2]
                                tt(cb_[:], pg[:, 1, :], e_[:], ALU.mult, [bpsb[k], bea[k]], bcb)
                                if br == 1:
                                    tt(ma[:], ma[:], cb_[:], ALU.add, [bma, bcb], bma)
                                else:
                                    tt(mixT[:, m, :], ma[:], cb_[:], ALU.add, [bma, bcb], bmix)
                    for mo in range(NG):
                        pw = ps_w[mo % 2]
                        mm(pw[:, :nb], [(wo[:, kc, mo * 128:(mo + 1) * 128], mixT[:, kc, :]) for kc in range(NG)],
                           [bwm, bmix], bpw[mo % 2])
                        stt(x_[:, mo, :], pw[:, :nb], mod(2, mo, r), x_[:, mo, :], ALU.mult, ALU.add,
                            [bpw[mo % 2], bmod, bxb2[bi % 2]], bxb2[bi % 2])
                    dma("sp", xs[:, :, t0:t0 + nb].rearrange("g p t -> p g t"), x_[:], [bxb2[bi % 2]], bxs[bi])
                S.barrier()

            LS.close()
            S.barrier()
            tap_dram("x1", xs, NG, F32, bxs)
            if stop == "M":
                break

            with ExitStack() as st:
                w1t = sb(st, "w1t", [128, NG, 4096], BF16)
                w2t = sb(st, "w2t", [128, 32, 1024], BF16)
                bw12 = Buf()
                for i in range(4):
                    dma("pq", w1t[:, :, i * 1024:(i + 1) * 1024],
                        w1[l, :, i * 1024:(i + 1) * 1024].rearrange("(kc p) c -> p kc c", p=128), [], bw12)
                for i in range(4):
                    dma("pq", w2t[:, i * 8:(i + 1) * 8, :],
                        w2[l, i * 1024:(i + 1) * 1024, :].rearrange("(kc p) c -> p kc c", p=128), [], bw12)
                xe = [sb(st, "xe%d" % i, [128, NG, 256], F32) for i in range(2)]
                bxe = [Buf(), Buf()]
                sq2 = sb(st, "sq2", [128, NG, 256], BF16)
                rstd2 = sb(st, "rstd2", [128, 256], F32)
                tmp2 = [sb(st, "tmp2%d" % i, [128, 256], F32) for i in range(2)]
                btmp2 = [Buf(), Buf()]
                h2 = sb(st, "h2", [128, NG, 256], BF16)
                bh2 = Buf()
                uT = sb(st, "uT", [128, 32, 256], BF16)
                buT = Buf()
                rl = [sb(st, "rl%d" % i, [128, 256], F32) for i in range(2)]
                brl = [Buf(), Buf()]
                ss2 = pst(st, "ss2", [128, 512])
                ps_u2 = [pst(st, "ps_u2%d" % i, [128, 512]) for i in range(2)]
                ps_w2 = [pst(st, "ps_w2%d" % i, [128, 512]) for i in range(2)]
                bsq2, bss2, brstd2 = Buf(), Buf(), Buf()
                bpu2 = [Buf(), Buf()]
                bpw2 = [Buf(), Buf()]
                for bi, (t0, nb) in enumerate(BLOCKS256):
                    r = 1 if bi == 0 else 0
                    x_ = xe[bi % 2]
                    dma("sp", x_[:], xs[:, :, t0:t0 + nb].rearrange("g p t -> p g t"), [bxs[bi]], bxe[bi % 2])
                    rms_rstd(st, lambda g: x_[:, g, :], NG, nb, sq2, ss2, bsq2, bss2, rstd2, brstd2, ones_b, 1.0 / D,
                             [bxe[bi % 2]])
                    for g in range(NG):
                        tp_ = tmp2[g % 2]
                        tt(tp_[:], x_[:, g, :], rstd2[:], ALU.mult, [bxe[bi % 2], brstd2], btmp2[g % 2])
                        act(h2[:, g, :], tp_[:], AF.Identity, [btmp2[g % 2], bav, bmod], bh2,
                            scale=av[:, 1, r, g:g + 1], bias=mod(3, g, r))
                    for j in range(32):
                        pu = ps_u2[j % 2]
                        mm(pu[:, :nb], [(w1t[:, kc, j * 128:(j + 1) * 128], h2[:, kc, :]) for kc in range(NG)],
                           [bw12, bh2], bpu2[j % 2])
                        act(rl[j % 2][:], pu[:, :nb], AF.Relu, [bpu2[j % 2]], brl[j % 2])
                        tt(uT[:, j, :], rl[j % 2][:], rl[j % 2][:], ALU.mult, [brl[j % 2]], buT)
                    for mo in range(NG):
                        pw = ps_w2[mo % 2]
                        mm(pw[:, :nb], [(w2t[:, j, mo * 128:(mo + 1) * 128], uT[:, j, :]) for j in range(32)],
                           [bw12, buT], bpw2[mo % 2])
                        stt(x_[:, mo, :], pw[:, :nb], mod(5, mo, r), x_[:, mo, :], ALU.mult, ALU.add,
                            [bpw2[mo % 2], bmod, bxe[bi % 2]], bxe[bi % 2])
                    dma("sp", xs[:, :, t0:t0 + nb].rearrange("g p t -> p g t"), x_[:], [bxe[bi % 2]], bxs[bi])
                S.barrier()
            tap_dram("x2", xs, NG, F32, bxs)

        with ExitStack() as st:
            cp = [sb(st, "fcp%d" % i, [128, NG, 512], F32) for i in range(2)]
            bcp = [Buf(), Buf()]
            for bi in range(8):
                t0 = 256 + 512 * bi
                c_ = cp[bi % 2]
                dma("sp", c_[:], xs[:, :, t0:t0 + 512].rearrange("g p t -> p g t"), xs_bufs(t0, 512), bcp[bi % 2])
                final_ops.append(S.op("sp", lambda e: e.dma_start(
                    out=outT[:, :, t0 - 256:t0 + 256].rearrange("g p t -> p g t"), in_=c_[:]), reads=[bcp[bi % 2]]))
        S.finish(final_ops)
    print("instructions:", S.ninst)
    return nc


def _consts():
    c = np.zeros((128, 5, 128), np.float32)
    c[:, 0, :] = 1.0
    c[:, 1, :] = np.eye(128)
    c[:64, 2, :64] = 1.0
    c[64:, 2, 64:] = 1.0
    for m in range(128):
        blk, i = divmod(m, 32)
        pm = blk * 32 + (i + 16) % 32
        c[pm, 3, m] = 1.0
    s = np.arange(128)[:, None]
    t = np.arange(128)[None, :]
    same = (s // 64) == (t // 64)
    c[:, 4, :] = (same & (s <= t)).astype(np.float32)
    mb = (same & (s >= t)).astype(np.float32)
    return c.reshape(128, 640), mb


def _rope_tables():
    tpos = np.arange(TL)
    pos = np.stack([tpos // 64, tpos % 64], -1).astype(np.float32)
    half = 32
    inv = (10000.0 ** (-np.arange(0, half, 2, dtype=np.float32) / half)).astype(np.float32)
    ang = pos[:, :, None] * inv
    cos, sin = np.cos(ang), np.sin(ang)
    cT = np.zeros((128, TL), np.float32)
    sT = np.zeros((128, TL), np.float32)
    for p in range(128):
        d = p % 64
        axis, rem = divmod(d, 32)
        hf, j = divmod(rem, 16)
        cT[p] = cos[:, axis, j]
        sT[p] = sin[:, axis, j] * (-1.0 if hf == 0 else 1.0)
    return cT, sT


def _bias_table(rpb):
    Lr = rpb.shape[0]
    out = np.full((Lr, 8, 128, 3, 6, 256), NEG, np.float32)
    cols = np.arange(64)
    cstart = np.clip(cols - 8, 0, 48)
    for pat, qb in enumerate((0, 1, 15)):
        ts_ = int(np.clip(4 * qb - 4, 0, 52))
        for j in range(6):
            for kr in range(2):
                rk = ts_ + 2 * j + kr
                for qr in range(4):
                    rq = 4 * qb + qr
                    r0 = int(np.clip(rq - 4, 0, 56))
                    if not (r0 <= rk < r0 + 8):
                        continue
                    rix = rk - rq + 7
                    ck = np.arange(64)[:, None]
                    cq = np.arange(64)[None, :]
                    ok = (ck >= cstart[None, :]) & (ck < cstart[None, :] + 16)
                    cix = np.clip(ck - cq + 15, 0, 30)
                    vals = rpb[:, :, rix, :][:, :, cix]
                    blk = np.where(ok[None, None], vals, NEG)
                    out[:, :, kr * 64:(kr + 1) * 64, pat, j, qr * 64:(qr + 1) * 64] = blk
    return out.reshape(Lr, 8, 128, 3 * 6 * 256)


def prep_inputs(x, c, ctx, c_ctx, ada_w, ada_b, norm1_w, norm2_w, w_in, hgrn_lb_logits, hgrn_onorm_w,
                q_norm_w, k_norm_w, natten_rpb, conv_w, w_branch_a, w_branch_b, w_branch_c, w_out,
                mlp_w1, mlp_w2):
    f = lambda a: np.ascontiguousarray(np.asarray(a, np.float32))
    cst, mb = _consts()
    cT_, sT_ = _rope_tables()
    shared = {
        "ada_w": f(ada_w),
        "ada_bT": f(np.asarray(ada_b).reshape(L, 48, 128).transpose(2, 0, 1)),
        "nwT": f(np.stack([np.asarray(norm1_w).reshape(L, NG, 128), np.asarray(norm2_w).reshape(L, NG, 128)], 1)
                 .transpose(3, 0, 1, 2)),
        "w_in": f(w_in),
        "lbl": f(np.asarray(hgrn_lb_logits).reshape(2, L, 4, 128).transpose(3, 0, 1, 2)),
        "onw": f(np.asarray(hgrn_onorm_w).T),
        "qkw": f(np.stack([np.tile(np.asarray(q_norm_w), (1, 2)), np.tile(np.asarray(k_norm_w), (1, 2))], -1)
                 .transpose(1, 0, 2)),
        "bias_tab": f(_bias_table(np.asarray(natten_rpb, np.float32))),
        "cwT": f(np.asarray(conv_w).reshape(L, 3, 4, 128).transpose(3, 0, 1, 2)),
        "w_ba": f(w_branch_a), "w_bb": f(w_branch_b), "w_bc": f(w_branch_c), "w_out": f(w_out),
        "w1": f(mlp_w1), "w2": f(mlp_w2),
        "cosT": f(cT_), "sinT": f(sT_), "cst": f(cst), "cst2": f(mb),
    }
    maps = []
    xn, cn, ctxn, ccn = (np.asarray(a, np.float32) for a in (x, c, ctx, c_ctx))
    for b in range(xn.shape[0]):
        seq = np.concatenate([ctxn[b], xn[b]], 0)
        m = dict(shared)
        m["xT"] = f(seq.T.reshape(NG, 128, T))
        m["cT"] = f(np.stack([cn[b].reshape(NG, 128).T, ccn.reshape(NG, 128).T], -1))
        maps.append(m)
    return maps


def kernel(**inputs):
    maps = prep_inputs(**inputs)
    nc = build()
    res = run_bass_kernel_spmd(nc, maps, core_ids=list(range(8)))
    outs = [r["outT"].reshape(D, TL).T for r in res.results]
    return np.ascontiguousarray(np.stack(outs, 0).astype(np.float32))
```

```python
import numpy as np
import ml_dtypes
from contextlib import ExitStack
import concourse.bass as bass
import concourse.mybir as mybir
from concourse.bass_utils import run_bass_kernel_spmd

F32 = mybir.dt.float32
BF16 = mybir.dt.bfloat16
ALU = mybir.AluOpType
AF = mybir.ActivationFunctionType

D = 1024
L = 4
TC = 256
TL = 4096
T = TC + TL
NG = 8
PROJ = 8704
EPS = 1e-6
OFF = dict(hq=0, hff=512, hfb=1024, hi=1536, hg=2048, nq=2560, nk=3072, nv=3584,
           cb=4096, cc=4608, cx=5120, ga=5632, gb=6656, gc=7680)
BLOCKS = [(0, 256)] + [(256 + 512 * i, 512) for i in range(8)]
BLOCKS256 = [(256 * i, 256) for i in range(17)]
NEG = -30000.0

DBG = {"units": None, "step": None}


class _Stop(Exception):
    pass


def ck(k):
    if DBG["step"] is not None and DBG["step"] <= k:
        raise _Stop()


COMPUTE = ("pe", "act", "dve")
DMAQ = ("sp", "pq")
N_DMA_SEMS = 32


class Buf:
    __slots__ = ("name", "lw", "rd")

    def __init__(self, name=""):
        self.name = name
        self.lw = None
        self.rd = []


class Op:
    __slots__ = ("eng", "idx", "sem", "val", "uid")
    _n = [0]

    def __init__(self, eng, idx, sem, val):
        self.eng, self.idx, self.sem, self.val = eng, idx, sem, val
        Op._n[0] += 1
        self.uid = Op._n[0]


class Sched:
    def __init__(self, nc, stack):
        self.nc = nc
        self.E = {"pe": nc.tensor, "act": nc.scalar, "dve": nc.vector, "sp": nc.sync, "pq": nc.gpsimd}
        self.sems = {e: stack.enter_context(nc.semaphore("s_" + e)) for e in COMPUTE}
        self.dsems = [stack.enter_context(nc.semaphore("d%d" % i)) for i in range(N_DMA_SEMS)]
        self.count = {e: 0 for e in COMPUTE + DMAQ}
        self.known = {e: {} for e in COMPUTE + DMAQ}
        self.known_dma = {e: set() for e in COMPUTE + DMAQ}
        self.dma_count = 0
        self.dma_last = [None] * N_DMA_SEMS
        self.dma_total = [0] * N_DMA_SEMS
        self.last = {e: None for e in COMPUTE}
        self.ninst = 0

    def _need(self, eng, dep, waits):
        if dep is None:
            return
        if dep.eng in DMAQ:
            if dep.uid in self.known_dma[eng]:
                return
            self.known_dma[eng].add(dep.uid)
            waits.append(dep)
            return
        if dep.eng == eng and eng == "pe":
            return
        if self.known[eng].get(dep.eng, -1) >= dep.idx:
            return
        self.known[eng][dep.eng] = dep.idx
        waits.append(dep)

    def op(self, eng, fn, reads=(), writes=()):
        waits = []
        for b in reads:
            self._need(eng, b.lw, waits)
        for b in writes:
            self._need(eng, b.lw, waits)
            for r in b.rd:
                self._need(eng, r, waits)
        e = self.E[eng]
        if eng in DMAQ:
            k = self.dma_count % N_DMA_SEMS
            self.dma_count += 1
            prev = self.dma_last[k]
            if prev is not None and prev.uid not in self.known_dma[eng]:
                self.known_dma[eng].add(prev.uid)
                waits.append(prev)
            self.dma_total[k] += 16
            o = Op(eng, self.count[eng], self.dsems[k], self.dma_total[k])
            self.dma_last[k] = o
        else:
            o = Op(eng, self.count[eng], self.sems[eng], self.count[eng] + 1)
            self.last[eng] = o
        self.count[eng] += 1
        for d in waits:
            e.wait_ge(d.sem, d.val)
        ins = fn(e)
        ins.then_inc(o.sem, 16 if eng in DMAQ else 1)
        self.ninst += 1
        for b in reads:
            b.rd.append(o)
        for b in writes:
            b.lw = o
            b.rd = []
        return o

    def barrier(self):
        deps = [self.last[e] for e in COMPUTE if self.last[e] is not None]
        deps += [d for d in self.dma_last if d is not None]
        for eng in COMPUTE + DMAQ:
            e = self.E[eng]
            for d in deps:
                if d.eng == eng and eng in COMPUTE:
                    continue
                if d.eng in DMAQ:
                    if d.uid in self.known_dma[eng]:
                        continue
                    self.known_dma[eng].add(d.uid)
                else:
                    if self.known[eng].get(d.eng, -1) >= d.idx:
                        continue
                    self.known[eng][d.eng] = d.idx
                e.wait_ge(d.sem, d.val)

    def finish(self, final_ops):
        for d in final_ops:
            self.nc.sync.wait_ge(d.sem, d.val)


def build(n_layers=L, tap=None, stop=None):
    nc = bass.Bass("TRN2", target_bir_lowering=False)

    def din(name, shape, dt=F32):
        return nc.dram_tensor(name, list(shape), dt, kind="ExternalInput").ap()

    xT_in = din("xT", [NG, 128, T])
    cT_in = din("cT", [128, NG, 2])
    ada_w = din("ada_w", [L, D, 6 * D])
    ada_bT = din("ada_bT", [128, L, 48])
    nwT = din("nwT", [128, L, 2, NG])
    w_in = din("w_in", [L, D, PROJ])
    lbl = din("lbl", [128, 2, L, 4])
    onw = din("onw", [128, L])
    qkw = din("qkw", [128, L, 2])
    bias_tab = din("bias_tab", [L, 8, 128, 3 * 6 * 256])
    cwT = din("cwT", [128, L, 3, 4])
    w_ba = din("w_ba", [L, 512, D])
    w_bb = din("w_bb", [L, 512, D])
    w_bc = din("w_bc", [L, 512, D])
    w_out = din("w_out", [L, D, D])
    w1 = din("w1", [L, D, 4 * D])
    w2 = din("w2", [L, 4 * D, D])
    cos_in = din("cosT", [128, TL])
    sin_in = din("sinT", [128, TL])
    cst_in = din("cst", [128, 128 * 5])
    cst2_in = din("cst2", [128, 128])
    outT = nc.dram_tensor("outT", [NG, 128, TL], F32, kind="ExternalOutput").ap()
    tap_out = None
    if tap is not None:
        tap_out = nc.dram_tensor("tap", list(tap[1]), F32, kind="ExternalOutput").ap()

    def dscr(name, shape, dt):
        return nc.dram_tensor(name, list(shape), dt, kind="Internal").ap()

    xs = dscr("xs", [NG, 128, T], F32)
    ya_d = dscr("ya_d", [4, 128, T], BF16)
    yb_d = dscr("yb_d", [4, 128, T], BF16)
    yc_d = dscr("yc_d", [4, 128, T], BF16)
    of_d = dscr("of_d", [4, 128, T], BF16)

    final_ops = []
    with ExitStack() as top:
        S = Sched(nc, top)

        uid = [0]

        def sb(st, name, shape, dt):
            uid[0] += 1
            return st.enter_context(nc.sbuf_tensor("%s_s%d" % (name, uid[0]), list(shape), dt))

        def pst(st, name, shape, dt=F32):
            uid[0] += 1
            return st.enter_context(nc.psum_tensor("%s_p%d" % (name, uid[0]), list(shape), dt))

        def mm(out_ap, pairs, reads, wbuf, start=True, stop=True):
            n = len(pairs)

            def fn(e):
                ins = None
                for i, (l_, r_) in enumerate(pairs):
                    ins = e.matmul(out_ap, l_, r_, start=(start and i == 0), stop=(stop and i == n - 1))
                return ins
            return S.op("pe", fn, reads=reads, writes=[wbuf])

        def act(out, in_, func, reads, wbuf, scale=1.0, bias=None):
            if bias is None:
                return S.op("act", lambda e: e.activation(out=out, in_=in_, func=func, scale=scale),
                            reads=reads, writes=[wbuf])
            return S.op("act", lambda e: e.activation(out=out, in_=in_, func=func, scale=scale, bias=bias),
                        reads=reads, writes=[wbuf])

        def tt(out, a, b, op, reads, wbuf):
            return S.op("dve", lambda e: e.tensor_tensor(out=out, in0=a, in1=b, op=op), reads=reads, writes=[wbuf])

        def ts(out, a, s1, op0, reads, wbuf, s2=None, op1=None):
            if op1 is None:
                return S.op("dve", lambda e: e.tensor_scalar(out=out, in0=a, scalar1=s1, scalar2=None, op0=op0),
                            reads=reads, writes=[wbuf])
            return S.op("dve", lambda e: e.tensor_scalar(out=out, in0=a, scalar1=s1, scalar2=s2, op0=op0, op1=op1),
                        reads=reads, writes=[wbuf])

        def stt(out, a, s, b, op0, op1, reads, wbuf):
            return S.op("dve", lambda e: e.scalar_tensor_tensor(out=out, in0=a, scalar=s, in1=b, op0=op0, op1=op1),
                        reads=reads, writes=[wbuf])

        def dma(q, out, in_, reads, wbuf):
            return S.op(q, lambda e: e.dma_start(out=out, in_=in_), reads=reads, writes=[wbuf] if wbuf else [])

        P = ExitStack()
        top.enter_context(P)
        cst = sb(P, "cst", [128, 5, 128], BF16)
        maskb = sb(P, "maskb", [128, 128], BF16)
        cf = sb(P, "cf", [128, 8], F32)
        onesf = sb(P, "onesf", [128, 512], F32)
        modT = sb(P, "modT", [128, L, 48, 2], F32)
        nw = sb(P, "nw", [128, L, 2, NG], F32)
        av = sb(P, "av", [128, 2, 2, NG], F32)
        lbt = sb(P, "lbt", [128, 2, L, 4], F32)
        oml = sb(P, "oml", [128, 2, L, 4], F32)
        onwt = sb(P, "onwt", [128, L], F32)
        qkwt = sb(P, "qkwt", [128, L, 2], F32)
        cwt = sb(P, "cwt", [128, L, 3, 4], F32)
        bc = Buf("const")
        bmod = Buf("mod")
        bav = Buf("av")
        dma("pq", cst[:].rearrange("p a b -> p (a b)"), cst_in[:, :], [], bc)
        dma("pq", maskb[:], cst2_in[:, :], [], bc)
        dma("sp", nw[:].rearrange("p a b c -> p (a b c)"), nwT.rearrange("p a b c -> p (a b c)"), [], bc)
        dma("sp", lbt[:].rearrange("p a b c -> p (a b c)"), lbl.rearrange("p a b c -> p (a b c)"), [], bc)
        dma("sp", onwt[:], onw[:, :], [], bc)
        dma("sp", qkwt[:].rearrange("p a b -> p (a b)"), qkw.rearrange("p a b -> p (a b)"), [], bc)
        dma("sp", cwt[:].rearrange("p a b c -> p (a b c)"), cwT.rearrange("p a b c -> p (a b c)"), [], bc)
        S.op("dve", lambda e: e.memset(cf[:, 0:1], EPS), writes=[bc])
        S.op("dve", lambda e: e.memset(cf[:, 1:2], 1.0), writes=[bc])
        S.op("dve", lambda e: e.memset(cf[:, 2:3], 0.0), writes=[bc])
        S.op("dve", lambda e: e.memset(onesf[:], 1.0), writes=[bc])
        ones_b = cst[:, 0, :]
        ident = cst[:, 1, :]
        blockones = cst[:, 2, :]
        perm = cst[:, 3, :]
        maskf = cst[:, 4, :]
        eps_ap = cf[:, 0:1]
        one_ap = cf[:, 1:2]

        act(lbt[:].rearrange("p a b c -> p (a b c)"), lbt[:].rearrange("p a b c -> p (a b c)"), AF.Exp, [bc], bc)
        with ExitStack() as st:
            ssum = sb(st, "ssum", [128, 2, 4], F32)
            tt(ssum[:], lbt[:, :, 0, :], lbt[:, :, 1, :], ALU.add, [bc], bc)
            tt(ssum[:], ssum[:], lbt[:, :, 2, :], ALU.add, [bc], bc)
            tt(ssum[:], ssum[:], lbt[:, :, 3, :], ALU.add, [bc], bc)
            S.op("dve", lambda e: e.reciprocal(out=ssum[:], in_=ssum[:]), reads=[bc], writes=[bc])
            for j in range(L):
                tt(lbt[:, :, j, :], lbt[:, :, j, :], ssum[:], ALU.mult, [bc], bc)
            tt(lbt[:, :, 2, :], lbt[:, :, 2, :], lbt[:, :, 1, :], ALU.add, [bc], bc)
            tt(lbt[:, :, 3, :], lbt[:, :, 3, :], lbt[:, :, 2, :], ALU.add, [bc], bc)
            S.op("dve", lambda e: e.memset(lbt[:, :, 0, :], 0.0), writes=[bc])
            ts(oml[:].rearrange("p a b c -> p (a b c)"), lbt[:].rearrange("p a b c -> p (a b c)"), -1.0, ALU.mult,
               [bc], bc, 1.0, ALU.add)
            ts(qkwt[:, :, 0], qkwt[:, :, 0], 0.125, ALU.mult, [bc], bc)
            S.barrier()

        with ExitStack() as st:
            ct = sb(st, "ct", [128, NG, 2], F32)
            sc = sb(st, "sc", [128, NG, 2], F32)
            abt = sb(st, "abt", [128, L, 48], F32)
            wst = [sb(st, "wst%d" % i, [128, NG, 768], F32) for i in range(2)]
            bw = [Buf(), Buf()]
            mps = pst(st, "mps", [128, 512])
            bps = Buf()
            b0 = Buf()
            dma("sp", ct[:].rearrange("p a b -> p (a b)"), cT_in.rearrange("p a b -> p (a b)"), [], b0)
            dma("sp", abt[:].rearrange("p a b -> p (a b)"), ada_bT.rearrange("p a b -> p (a b)"), [], b0)
            ctf = ct[:].rearrange("p a b -> p (a b)")
            scf = sc[:].rearrange("p a b -> p (a b)")
            act(scf, ctf, AF.Exp, [b0], b0, scale=-1.0)
            ts(scf, scf, 1.0, ALU.add, [b0], b0)
            S.op("dve", lambda e: e.reciprocal(out=scf, in_=scf), reads=[b0], writes=[b0])
            tt(scf, scf, ctf, ALU.mult, [b0], b0)
            k = 0
            for l in range(n_layers):
                for s_ in range(8):
                    w = wst[k % 2]
                    dma("sp", w[:], ada_w[l, :, s_ * 768:(s_ + 1) * 768].rearrange("(kc p) c -> p kc c", p=128),
                        [], bw[k % 2])
                    for j in range(6):
                        grp = s_ * 6 + j
                        mm(mps[:, grp * 2:grp * 2 + 2],
                           [(w[:, kc, j * 128:(j + 1) * 128], sc[:, kc, :]) for kc in range(NG)],
                           [bw[k % 2], b0], bps)
                    k += 1
                tt(modT[:, l, :, :], mps[:, 0:96].rearrange("p (a b) -> p a b", b=2),
                   abt[:, l, :].unsqueeze(2).to_broadcast([128, 48, 2]), ALU.add, [bps, b0], bmod)
            S.barrier()

        bxs = [Buf("xs%d" % i) for i in range(17)]

        def xs_bufs(t0, nb):
            return [bxs[i] for i in range(t0 // 256, (t0 + nb) // 256)]

        with ExitStack() as st:
            cp = [sb(st, "cp%d" % i, [128, NG, 512], F32) for i in range(2)]
            bcp = [Buf(), Buf()]
            for bi, (t0, nb) in enumerate(BLOCKS):
                c_ = cp[bi % 2]
                dma("sp", c_[:, :, :nb], xT_in[:, :, t0:t0 + nb].rearrange("g p t -> p g t"), [], bcp[bi % 2])
                S.op("sp", lambda e: e.dma_start(out=xs[:, :, t0:t0 + nb].rearrange("g p t -> p g t"), in_=c_[:, :, :nb]),
                     reads=[bcp[bi % 2]], writes=xs_bufs(t0, nb))
            S.barrier()

        def rms_rstd(st_tmp, x_ap_g, ngrp, nb, sq, ss_ps, bsq, bss, rstd, brstd, lhs_ones, inv_n, xreads):
            for g in range(ngrp):
                act(sq[:, g, :nb], x_ap_g(g), AF.Square, xreads, bsq)
            mm(ss_ps[:, :nb], [(lhs_ones, sq[:, g, :nb]) for g in range(ngrp)], [bsq, bc], bss)
            act(rstd[:, :nb], ss_ps[:, :nb], AF.Ln, [bss, bc], brstd, scale=inv_n, bias=eps_ap)
            act(rstd[:, :nb], rstd[:, :nb], AF.Exp, [brstd], brstd, scale=-0.5)

        for l in range(n_layers):
            if stop == "P0":
                break
            for ni, (sci) in enumerate((1, 4)):
                for r in range(2):
                    S.op("dve", lambda e: e.scalar_tensor_tensor(
                        out=av[:, ni, r, :], in0=modT[:, l, sci * 8:(sci + 1) * 8, r], scalar=1.0,
                        in1=nw[:, l, ni, :], op0=ALU.add, op1=ALU.mult), reads=[bmod, bc], writes=[bav])

            def mod(idx, g, r):
                return modT[:, l, idx * 8 + g, r:r + 1]

            LS = ExitStack()
            hT = sb(LS, "hT", [128, NG, T], BF16)
            bh = [Buf("h%d" % i) for i in range(9)]

            with ExitStack() as st:
                xb = [sb(st, "xb%d" % i, [128, NG, 512], F32) for i in range(2)]
                bxb = [Buf(), Buf()]
                sq = sb(st, "sq", [128, NG, 512], BF16)
                rstd = sb(st, "rstd", [128, 512], F32)
                tmp = [sb(st, "tmp%d" % i, [128, 512], F32) for i in range(2)]
                btmp = [Buf(), Buf()]
                ss_ps = pst(st, "ss_ps", [128, 512])
                bsq, bss, brstd = Buf(), Buf(), Buf()
                for bi, (t0, nb) in enumerate(BLOCKS):
                    r = 1 if bi == 0 else 0
                    x_ = xb[bi % 2]
                    dma("sp", x_[:, :, :nb], xs[:, :, t0:t0 + nb].rearrange("g p t -> p g t"),
                        xs_bufs(t0, nb), bxb[bi % 2])
                    rms_rstd(st, lambda g: x_[:, g, :nb], NG, nb, sq, ss_ps, bsq, bss, rstd, brstd,
                             ones_b, 1.0 / D, [bxb[bi % 2]])
                    for g in range(NG):
                        tp_ = tmp[g % 2]
                        tt(tp_[:, :nb], x_[:, g, :nb], rstd[:, :nb], ALU.mult, [bxb[bi % 2], brstd], btmp[g % 2])
                        act(hT[:, g, t0:t0 + nb], tp_[:, :nb], AF.Identity, [btmp[g % 2], bav, bmod], bh[bi],
                            scale=av[:, 0, r, g:g + 1], bias=mod(0, g, r))
                S.barrier()

            if tap is not None and tap[0] == "hT" and l == tap[2]:
                with ExitStack() as st:
                    tb = sb(st, "tb", [128, T], F32)
                    btb = Buf()
                    for g in range(NG):
                        act(tb[:], hT[:, g, :], AF.Copy, bh, btb)
                        final_ops.append(dma("sp", tap_out[g], tb[:], [btb], None))
                    S.barrier()

            if stop == "A":
                LS.close()
                S.barrier()
                break
            def make_vtok(st, vt, bvt, col0):
                with ExitStack() as s2:
                    wv = sb(s2, "wv", [128, NG, 512], BF16)
                    bwv = Buf()
                    vps = [pst(s2, "vps%d" % i, [128, 512]) for i in range(2)]
                    bvp = [Buf(), Buf()]
                    dma("pq", wv[:], w_in[l, :, col0:col0 + 512].rearrange("(kc p) c -> p kc c", p=128), [], bwv)
                    for ti in range(T // 128):
                        bi = 0 if ti < 2 else 1 + (ti - 2) // 4
                        mm(vps[ti % 2][:], [(hT[:, kc, ti * 128:(ti + 1) * 128], wv[:, kc, :]) for kc in range(NG)],
                           [bh[bi], bwv], bvp[ti % 2])
                        if ti % 2 == 0:
                            act(vt[:, ti, :], vps[ti % 2][:], AF.Copy, [bvp[ti % 2]], bvt)
                        else:
                            S.op("dve", lambda e: e.tensor_copy(out=vt[:, ti, :], in_=vps[ti % 2][:]),
                                 reads=[bvp[ti % 2]], writes=[bvt])
                    S.barrier()

            with ExitStack() as st:
                vt = sb(st, "vt", [128, T // 128, 512], BF16)
                bvt = Buf()
                make_vtok(st, vt, bvt, OFF["hi"])
                wq = sb(st, "wq", [128, NG, 512], BF16)
                wf = sb(st, "wf", [128, NG, 512], BF16)
                wg = sb(st, "wg", [128, NG, 512], BF16)
                bwq, bwf, bwg = Buf(), Buf(), Buf()
                dma("pq", wq[:], w_in[l, :, OFF["hq"]:OFF["hq"] + 512].rearrange("(kc p) c -> p kc c", p=128), [], bwq)
                dma("pq", wg[:], w_in[l, :, OFF["hg"]:OFF["hg"] + 512].rearrange("(kc p) c -> p kc c", p=128), [], bwg)
                NT = 6
                tf = [sb(st, "tf%d" % i, [128, 512], F32) for i in range(NT)]
                btf = [Buf() for _ in range(NT)]
                to_ = [[sb(st, "to%d_%d" % (i, k), [128, 512], F32) for i in range(3)] for k in range(2)]
                bto = [[Buf() for _ in range(3)] for _ in range(2)]
                gbuf = sb(st, "gbuf", [128, 640], F32)
                bgb = Buf()
                S.op("dve", lambda e: e.memset(gbuf[:], 0.0), writes=[bgb])
                csc = [sb(st, "csc%d" % k, [128, 4, 8], F32) for k in range(2)]
                bcs = [Buf(), Buf()]
                qp = [sb(st, "qp%d" % k, [128, 512], BF16) for k in range(2)]
                bqp = [Buf(), Buf()]
                yq = [sb(st, "yq%d" % k, [128, 512], BF16) for k in range(2)]
                byq = [Buf(), Buf()]
                sqo = [sb(st, "sqo%d" % k, [128, 512], BF16) for k in range(2)]
                bsqo = [Buf(), Buf()]
                kp2 = [sb(st, "kp%d" % k, [128, 512], BF16) for k in range(2)]
                bkp2 = [Buf(), Buf()]
                kt = sb(st, "kt", [128, 4, 128], BF16)
                bkt = Buf()
                atb = sb(st, "atb", [128, 4, 128], BF16)
                batb = Buf()
                Sst = sb(st, "Sst", [128, 4, 128], F32)
                bS = [Buf() for _ in range(4)]
                sbf = [[sb(st, "sbf%d_%d" % (i, k), [128, 128], BF16) for i in range(2)] for k in range(2)]
                bsbf = [[Buf(), Buf()], [Buf(), Buf()]]
                tuS = [sb(st, "tuS%d" % k, [128, 8, 128], F32) for k in range(2)]
                btuS = [Buf(), Buf()]
                ofb = [sb(st, "ofb%d" % k, [128, 512], BF16) for k in range(2)]
                bofb = [Buf(), Buf()]
                ps_f = pst(st, "ps_f", [128, 512])
                ps_q = pst(st, "ps_q", [128, 512])
                ps_t = pst(st, "ps_t", [128, 4, 128], BF16)
                ps_a = pst(st, "ps_a", [128, 4, 128])
                ps_u = pst(st, "ps_u", [128, 8, 128])
                ps_o = [pst(st, "ps_o%d" % k, [128, 512]) for k in range(2)]
                bpf, bpq, bpt, bpa, bpu = [Buf() for _ in range(5)]
                bpo = [Buf(), Buf()]
                bof = [[Buf() for _ in range(9)] for _ in range(4)]
                bya = [Buf() for _ in range(17)]

                for dr in range(2):
                    off_f = OFF["hff"] if dr == 0 else OFF["hfb"]
                    dma("pq", wf[:], w_in[l, :, off_f:off_f + 512].rearrange("(kc p) c -> p kc c", p=128), [], bwf)
                    for hh in range(4):
                        S.op("dve", lambda e: e.memset(Sst[:, hh, :], 0.0), writes=[bS[hh]])
                    order = list(range(9)) if dr == 0 else [0] + list(range(8, 0, -1))
                    mask = maskf if dr == 0 else maskb[:]
                    wi, ei = (0, 1) if dr == 0 else (1, 0)
                    sgn = 1.0 if dr == 0 else -1.0
                    goff = 1 if dr == 0 else 0
                    for bi in order:
                        t0, nb = BLOCKS[bi]
                        nch = nb // 64
                        npair = nb // 128
                        corder = list(range(nch)) if dr == 0 else list(range(nch - 1, -1, -1))

                        def front(hd, k):
                            kp, bkp = kp2[k], bkp2[k]
                            cs = slice(hd * 128, (hd + 1) * 128)
                            lb_ap = lbt[:, dr, l, hd:hd + 1]
                            oml_ap = oml[:, dr, l, hd:hd + 1]
                            cs_ = csc[k]
                            mm(ps_f[:, :nb], [(wf[:, kc, cs], hT[:, kc, t0:t0 + nb]) for kc in range(NG)],
                               [bwf, bh[bi]], bpf)
                            mm(ps_q[:, :nb], [(wq[:, kc, cs], hT[:, kc, t0:t0 + nb]) for kc in range(NG)],
                               [bwq, bh[bi]], bpq)
                            E_, L1, L2, Q4, A5, E6 = tf[0], tf[1], tf[2], tf[3], tf[4], tf[5]
                            act(E_[:, :nb], ps_f[:, :nb], AF.Exp, [bpf], btf[0], scale=-1.0)
                            act(L1[:, :nb], E_[:, :nb], AF.Ln, [btf[0], bc], btf[1], scale=1.0, bias=one_ap)
                            act(L2[:, :nb], E_[:, :nb], AF.Ln, [btf[0], bc], btf[2], scale=lb_ap, bias=one_ap)
                            tt(L2[:, :nb], L2[:, :nb], L1[:, :nb], ALU.subtract, [btf[1], btf[2]], btf[2])
                            S.op("dve", lambda e: e.tensor_tensor_scan(
                                out=gbuf[:, 1:nb + 1], data0=onesf[:, :nb], data1=L2[:, :nb], initial=0.0,
                                op0=ALU.mult, op1=ALU.add), reads=[btf[2], bc], writes=[bgb])
                            act(L1[:, :nb], L1[:, :nb], AF.Exp, [btf[1]], btf[1], scale=-1.0)
                            stt(E_[:, :nb], E_[:, :nb], oml_ap, L1[:, :nb], ALU.mult, ALU.mult, [btf[0], btf[1], bc], btf[0])
                            act(Q4[:, :nb], ps_q[:, :nb], AF.Exp, [bpq], btf[3], scale=-1.0)
                            ts(Q4[:, :nb], Q4[:, :nb], 1.0, ALU.add, [btf[3]], btf[3])
                            S.op("dve", lambda e: e.reciprocal(out=Q4[:, :nb], in_=Q4[:, :nb]), reads=[btf[3]], writes=[btf[3]])
                            tt(Q4[:, :nb], Q4[:, :nb], ps_q[:, :nb], ALU.mult, [btf[3], bpq], btf[3])
                            ref = gbuf[:, 32:32 + nb].rearrange("p (c t) -> p c t", t=64)[:, :, 0:1].to_broadcast([128, nch, 64])
                            tt(A5[:, :nb].rearrange("p (c t) -> p c t", t=64),
                               gbuf[:, goff:goff + nb].rearrange("p (c t) -> p c t", t=64), ref, ALU.subtract,
                               [bgb], btf[4])
                            act(E6[:, :nb], A5[:, :nb], AF.Exp, [btf[4]], btf[5], scale=sgn)
                            act(A5[:, :nb], A5[:, :nb], AF.Exp, [btf[4]], btf[4], scale=-sgn)
                            tt(qp[k][:, :nb], Q4[:, :nb], E6[:, :nb], ALU.mult, [btf[3], btf[5]], bqp[k])
                            tt(kp[:, :nb], E_[:, :nb], A5[:, :nb], ALU.mult, [btf[0], btf[4]], bkp)
                            g0 = gbuf[:, 0:nb].rearrange("p (c t) -> p c t", t=64)[:, :, 0]
                            g32 = gbuf[:, 32:32 + nb].rearrange("p (c t) -> p c t", t=64)[:, :, 0]
                            g64 = gbuf[:, 64:64 + nb].rearrange("p (c t) -> p c t", t=64)[:, :, 0]
                            tt(cs_[:, 0, :nch], g32, g0, ALU.subtract, [bgb], bcs[k])
                            tt(cs_[:, 1, :nch], g64, g32, ALU.subtract, [bgb], bcs[k])
                            act(cs_[:, 0:2, :nch], cs_[:, 0:2, :nch], AF.Exp, [bcs[k]], bcs[k])
                            tt(cs_[:, 2, :nch], cs_[:, 0, :nch], cs_[:, 1, :nch], ALU.mult, [bcs[k]], bcs[k])

                        def front_b(hd, k):
                            kp, bkp = kp2[k], bkp2[k]
                            cs = slice(hd * 128, (hd + 1) * 128)
                            cs_ = csc[k]
                            for j in range(npair):
                                S.op("pe", lambda e: e.transpose(out=ps_t[:, j, :], in_=kp[:, j * 128:(j + 1) * 128], identity=ident),
                                     reads=[bkp, bc], writes=[bpt])
                            act(kt[:, :npair, :], ps_t[:, :npair, :], AF.Copy, [bpt], bkt)
                            for j in range(npair):
                                mm(ps_a[:, j, :], [(kp[:, j * 128:(j + 1) * 128], qp[k][:, j * 128:(j + 1) * 128])],
                                   [bkp, bqp[k]], bpa)
                            for j in range(npair):
                                tt(atb[:, j, :], ps_a[:, j, :], mask, ALU.mult, [bpa, bc], batb)
                            for c in range(nch):
                                j, par = c // 2, c % 2
                                ti = t0 // 128 + j
                                mm(ps_u[:, par * 4 + j, :], [(kt[par * 64:(par + 1) * 64, j, :], vt[par * 64:(par + 1) * 64, ti, cs])],
                                   [bkt, bvt], bpu)
                            for par in range(2):
                                e_b = cs_[:, ei, 0:nch].rearrange("p (j r) -> p j r", r=2)[:, :, par:par + 1].to_broadcast([128, npair, 128])
                                tt(tuS[k][:, par * 4:par * 4 + npair, :], ps_u[:, par * 4:par * 4 + npair, :], e_b, ALU.mult,
                                   [bpu, bcs[k]], btuS[k])
                            for j in range(npair):
                                ti = t0 // 128 + j
                                mm(ps_o[k][:, j * 128:(j + 1) * 128], [(vt[:, ti, cs], atb[:, j, :])], [bvt, batb], bpo[k],
                                   start=(j == 0), stop=False)

                        def chain_step(hd, k, ci, c):
                            cs_ = csc[k]
                            sb_ = sbf[k][ci % 2]
                            act(sb_[:], Sst[:, hd, :], AF.Identity, [bS[hd], bcs[k]], bsbf[k][ci % 2], scale=cs_[:, wi, c:c + 1])
                            mm(ps_o[k][:, c * 64:(c + 1) * 64], [(sb_[:], qp[k][:, c * 64:(c + 1) * 64])],
                               [bsbf[k][ci % 2], bqp[k]], bpo[k], start=False, stop=True)
                            stt(Sst[:, hd, :], Sst[:, hd, :], cs_[:, 2, c:c + 1], tuS[k][:, (c % 2) * 4 + c // 2, :], ALU.mult, ALU.add,
                                [bS[hd], bcs[k], btuS[k]], bS[hd])

                        def output(hd, k):
                            cs = slice(hd * 128, (hd + 1) * 128)
                            if dr == 0:
                                act(ofb[k][:, :nb], ps_o[k][:, :nb], AF.Copy, [bpo[k]], bofb[k])
                                dma("sp", of_d[hd, :, t0:t0 + nb], ofb[k][:, :nb], [bofb[k]], bof[hd][bi])
                                return
                            O_, G6, G7 = to_[k]
                            bO, bG6, bG7 = bto[k]
                            dma("sp", ofb[k][:, :nb], of_d[hd, :, t0:t0 + nb], [bof[hd][bi]], bofb[k])
                            tt(O_[:, :nb], ps_o[k][:, :nb], ofb[k][:, :nb], ALU.add, [bpo[k], bofb[k]], bO)
                            act(sqo[k][:, :nb], O_[:, :nb], AF.Square, [bO], bsqo[k])
                            mm(ps_q[:, :nb], [(ones_b, sqo[k][:, :nb])], [bsqo[k], bc], bpq)
                            act(G6[:, :nb], ps_q[:, :nb], AF.Ln, [bpq, bc], bG6, scale=1.0 / 128, bias=eps_ap)
                            act(G6[:, :nb], G6[:, :nb], AF.Exp, [bG6], bG6, scale=-0.5)
                            stt(O_[:, :nb], O_[:, :nb], onwt[:, l:l + 1], G6[:, :nb], ALU.mult, ALU.mult, [bO, bG6, bc], bO)
                            mm(ps_f[:, :nb], [(wg[:, kc, cs], hT[:, kc, t0:t0 + nb]) for kc in range(NG)],
                               [bwg, bh[bi]], bpf)
                            act(G7[:, :nb], ps_f[:, :nb], AF.Exp, [bpf], bG7, scale=-1.0)
                            ts(G7[:, :nb], G7[:, :nb], 1.0, ALU.add, [bG7], bG7)
                            S.op("dve", lambda e: e.reciprocal(out=G7[:, :nb], in_=G7[:, :nb]), reads=[bG7], writes=[bG7])
                            tt(G7[:, :nb], G7[:, :nb], ps_f[:, :nb], ALU.mult, [bG7, bpf], bG7)
                            tt(yq[k][:, :nb], O_[:, :nb], G7[:, :nb], ALU.mult, [bO, bG7], byq[k])
                            dma("sp", ya_d[hd, :, t0:t0 + nb], yq[k][:, :nb], [byq[k]], bya[t0 // 256])
                            if nb == 512:
                                bya[t0 // 256 + 1] = bya[t0 // 256]

                        for hp in range(2):
                            front(2 * hp, 0)
                            front(2 * hp + 1, 1)
                            front_b(2 * hp, 0)
                            front_b(2 * hp + 1, 1)
                            for ci, c in enumerate(corder):
                                chain_step(2 * hp, 0, ci, c)
                                chain_step(2 * hp + 1, 1, ci, c)
                            output(2 * hp, 0)
                            output(2 * hp + 1, 1)
                S.barrier()

            def tap_dram(name, src, ngrp, dt, rbufs):
                if tap is None or tap[0] != name or l != tap[2]:
                    return
                with ExitStack() as st:
                    ta = sb(st, "tap_a", [128, T], dt)
                    tb = sb(st, "tap_b", [128, T], F32)
                    bta, btb = Buf(), Buf()
                    for g in range(ngrp):
                        dma("sp", ta[:], src[g], rbufs, bta)
                        act(tb[:], ta[:], AF.Copy, [bta], btb)
                        final_ops.append(dma("sp", tap_out[g], tb[:], [btb], None))
                    S.barrier()

            tap_dram("ya", ya_d, 4, BF16, bya)
            tap_dram("of", of_d, 4, BF16, [])
            if stop == "B":
                LS.close()
                S.barrier()
                break

            byb = [Buf() for _ in range(17)]
            with ExitStack() as st:
                vt = sb(st, "vn", [128, T // 128, 512], BF16)
                bvt = Buf()
                make_vtok(st, vt, bvt, OFF["nv"])
                wqk = sb(st, "wqk", [128, 2, NG, 128], BF16)
                bwqk = Buf()
                qT = sb(st, "qT", [128, T], BF16)
                kT = sb(st, "kT", [128, T], BF16)
                bq_ = [Buf() for _ in range(9)]
                bk_ = [Buf() for _ in range(9)]
                cs_t = [sb(st, "cst%d" % i, [128, 2, 512], F32) for i in range(2)]
                bcs_t = [Buf(), Buf()]
                sqn2 = [sb(st, "sqn%d" % i, [128, 512], BF16) for i in range(2)]
                rs2 = [sb(st, "rsn%d" % i, [128, 512], F32) for i in range(2)]
                qn2 = [sb(st, "qn%d" % i, [128, 512], BF16) for i in range(2)]
                t12 = [sb(st, "t1n%d" % i, [128, 512], F32) for i in range(2)]
                t22 = [sb(st, "t2n%d" % i, [128, 512], F32) for i in range(2)]
                bsqn2, brs2, bqn2, bt12, bt22 = [[Buf(), Buf()] for _ in range(5)]
                bt = sb(st, "bt", [128, 3, 6, 256], F32)
                bbt = Buf()
                tmpS = [sb(st, "tmpS%d" % i, [128, 2, 256], F32) for i in range(2)]
                btmpS = [Buf(), Buf()]
                Pt = [sb(st, "Pt%d" % i, [128, 2, 256], BF16) for i in range(8)]
                bPt = [Buf() for _ in range(8)]
                rsum2 = [sb(st, "rsum%d" % i, [64, 256], F32) for i in range(2)]
                brsum2 = [Buf(), Buf()]
                yo = [sb(st, "yo%d" % i, [64, 256], BF16) for i in range(2)]
                byo = [Buf(), Buf()]
                qbn = [0]
                for gq in range(4):
                    dma("pq", wqk[:, 0], w_in[l, :, OFF["nq"] + gq * 128:OFF["nq"] + (gq + 1) * 128]
                        .rearrange("(kc p) c -> p kc c", p=128), [], bwqk)
                    dma("pq", wqk[:, 1], w_in[l, :, OFF["nk"] + gq * 128:OFF["nk"] + (gq + 1) * 128]
                        .rearrange("(kc p) c -> p kc c", p=128), [], bwqk)
                    SP_ = ExitStack()
                    ps_p2 = [pst(SP_, "ps_p%d" % i, [128, 512]) for i in range(2)]
                    ps_n2 = [pst(SP_, "ps_n%d" % i, [128, 512]) for i in range(2)]
                    ps_r2 = [pst(SP_, "ps_r%d" % i, [128, 512]) for i in range(2)]
                    bpp2, bpn2, bpr2 = [[Buf(), Buf()] for _ in range(3)]
                    for bi, (t0, nb) in enumerate(BLOCKS):
                        ct_ = cs_t[bi % 2]
                        if bi > 0:
                            dma("sp", ct_[:, 0, :], cos_in[:, t0 - 256:t0 + 256], [], bcs_t[bi % 2])
                            dma("sp", ct_[:, 1, :], sin_in[:, t0 - 256:t0 + 256], [], bcs_t[bi % 2])
                        QK = ((qT, bq_), (kT, bk_))
                        for qi in range(2):
                            mm(ps_p2[qi][:, :nb], [(wqk[:, qi, kc, :], hT[:, kc, t0:t0 + nb]) for kc in range(NG)],
                               [bwqk, bh[bi]], bpp2[qi])
                        for qi in range(2):
                            act(sqn2[qi][:, :nb], ps_p2[qi][:, :nb], AF.Square, [bpp2[qi]], bsqn2[qi])
                        for qi in range(2):
                            mm(ps_n2[qi][:, :nb], [(blockones, sqn2[qi][:, :nb])], [bsqn2[qi], bc], bpn2[qi])
                        for qi in range(2):
                            act(rs2[qi][:, :nb], ps_n2[qi][:, :nb], AF.Ln, [bpn2[qi], bc], brs2[qi], scale=1.0 / 64, bias=eps_ap)
                        for qi in range(2):
                            act(rs2[qi][:, :nb], rs2[qi][:, :nb], AF.Exp, [brs2[qi]], brs2[qi], scale=-0.5)
                        if bi == 0:
                            for qi in range(2):
                                dst, bdst = QK[qi]
                                stt(dst[:, t0:t0 + nb], ps_p2[qi][:, :nb], qkwt[:, l, qi:qi + 1], rs2[qi][:, :nb], ALU.mult, ALU.mult,
                                    [bpp2[qi], brs2[qi], bc], bdst[bi])
                        else:
                            for qi in range(2):
                                stt(qn2[qi][:, :nb], ps_p2[qi][:, :nb], qkwt[:, l, qi:qi + 1], rs2[qi][:, :nb], ALU.mult, ALU.mult,
                                    [bpp2[qi], brs2[qi], bc], bqn2[qi])
                            for qi in range(2):
                                mm(ps_r2[qi][:, :nb], [(perm, qn2[qi][:, :nb])], [bqn2[qi], bc], bpr2[qi])
                            for qi in range(2):
                                tt(t12[qi][:, :nb], qn2[qi][:, :nb], ct_[:, 0, :nb], ALU.mult, [bqn2[qi], bcs_t[bi % 2]], bt12[qi])
                            for qi in range(2):
                                dst, bdst = QK[qi]
                                tt(t22[qi][:, :nb], ps_r2[qi][:, :nb], ct_[:, 1, :nb], ALU.mult, [bpr2[qi], bcs_t[bi % 2]], bt22[qi])
                                tt(dst[:, t0:t0 + nb], t12[qi][:, :nb], t22[qi][:, :nb], ALU.add, [bt12[qi], bt22[qi]], bdst[bi])
                    S.barrier()
                    SP_.close()
                    SA_ = ExitStack()
                    ps_s2 = [pst(SA_, "ps_s2%d" % i, [128, 2, 256]) for i in range(4)]
                    bpss = [Buf() for _ in range(4)]
                    pso = [pst(SA_, "pso%d" % i, [64, 2, 256]) for i in range(2)]
                    bpso = [Buf(), Buf()]
                    for par in range(2):
                        h8 = gq * 2 + par
                        prt = slice(par * 64, par * 64 + 64)
                        dma("sp", bt[:].rearrange("p a b c -> p (a b c)"), bias_tab[l, h8], [], bbt)
                        def qk_part(qb):
                            nonlocal_q = {}
                            if qb < 16:
                                q0 = 256 + qb * 256
                                ts_ = min(max(4 * qb - 4, 0), 52)
                                pat = 0 if qb == 0 else (2 if qb == 15 else 1)
                                ktiles = [256 + (ts_ + 2 * j) * 64 for j in range(6)] + [0, 128]
                            else:
                                q0 = 0
                                pat = 0
                                ktiles = [0, 128]
                            npairs = len(ktiles) // 2
                            pb0 = (qbn[0] % 2) * 4
                            ok_ = qbn[0] % 2
                            qbn[0] += 1
                            for jj in range(npairs):
                                pss = ps_s2[jj]
                                for u in range(2):
                                    k0 = ktiles[2 * jj + u]
                                    mm(pss[:, u, :], [(kT[prt, k0:k0 + 128], qT[prt, q0:q0 + 256])], bk_ + bq_, bpss[jj])
                                Pj = Pt[pb0 + jj]
                                if qb < 16 and jj < 3:
                                    tS = tmpS[jj % 2]
                                    tt(tS[:], pss[:], bt[:, pat, 2 * jj:2 * jj + 2, :], ALU.add, [bpss[jj], bbt], btmpS[jj % 2])
                                    act(Pj[:], tS[:], AF.Exp, [btmpS[jj % 2]], bPt[pb0 + jj])
                                else:
                                    act(Pj[:], pss[:], AF.Exp, [bpss[jj]], bPt[pb0 + jj])
                            return (qb, q0, ktiles, npairs, pb0, ok_)

                        def pv_part(info):
                            qb, q0, ktiles, npairs, pb0, ok_ = info
                            pairs_o, pairs_m = [], []
                            for i, k0 in enumerate(ktiles):
                                Pj = Pt[pb0 + i // 2][:, i % 2, :]
                                pairs_o.append((vt[:, k0 // 128, h8 * 64:(h8 + 1) * 64], Pj))
                                pairs_m.append((cst[:, 0, 0:64], Pj))
                            po = pso[ok_]
                            mm(po[:, 0, :], pairs_o, [bvt] + bPt[pb0:pb0 + npairs], bpso[ok_])
                            mm(po[:, 1, :], pairs_m, [bc] + bPt[pb0:pb0 + npairs], bpso[ok_])
                            rsum = rsum2[ok_]
                            S.op("dve", lambda e: e.reciprocal(out=rsum[:], in_=po[:, 1, :]), reads=[bpso[ok_]], writes=[brsum2[ok_]])
                            y_ = yo[qb % 2]
                            tt(y_[:], po[:, 0, :], rsum[:], ALU.mult, [bpso[ok_], brsum2[ok_]], byo[qb % 2])
                            dma("sp", yb_d[gq, prt, q0:q0 + 256], y_[:], [byo[qb % 2]], byb[q0 // 256])

                        prev_info = qk_part(0)
                        for qb in range(1, 17):
                            info = qk_part(qb)
                            pv_part(prev_info)
                            prev_info = info
                        pv_part(prev_info)
                    S.barrier()
                    SA_.close()
                S.barrier()
            tap_dram("yb", yb_d, 4, BF16, byb)
            if stop == "C":
                LS.close()
                S.barrier()
                break

            byc = Buf()
            with ExitStack() as st:
                wc = sb(st, "wc", [128, 3, NG, 128], BF16)
                bwc = Buf()
                ul = sb(st, "ul", [128, TL + 2], F32)
                uc = sb(st, "uc", [128, TC + 2], F32)
                bu = Buf()
                Bs = sb(st, "Bs", [128, T], F32)
                bBs = Buf()
                acc = sb(st, "acc", [128, TL], F32)
                bacc = Buf()
                yout = sb(st, "yout", [128, T], BF16)
                byout = Buf()
                csb = sb(st, "csb", [128, 512], F32)
                bcsb = Buf()
                ps_c = [pst(st, "ps_c%d" % i, [128, 512]) for i in range(3)]
                bpc = [Buf() for _ in range(3)]
                for (u_, n) in ((uc, TC), (ul, TL)):
                    S.op("dve", lambda e: e.memset(u_[:, 0:1], 0.0), writes=[bu])
                    S.op("dve", lambda e: e.memset(u_[:, n + 1:n + 2], 0.0), writes=[bu])
                for gc in range(4):
                    for i, key in enumerate(("cb", "cc", "cx")):
                        dma("pq", wc[:, i], w_in[l, :, OFF[key] + gc * 128:OFF[key] + (gc + 1) * 128]
                            .rearrange("(kc p) c -> p kc c", p=128), [], bwc)
                    for bi, (t0, nb) in enumerate(BLOCKS):
                        for i in range(3):
                            mm(ps_c[i][:, :nb], [(wc[:, i, kc, :], hT[:, kc, t0:t0 + nb]) for kc in range(NG)],
                               [bwc, bh[bi]], bpc[i])
                        act(Bs[:, t0:t0 + nb], ps_c[0][:, :nb], AF.Copy, [bpc[0]], bBs)
                        act(csb[:, :nb], ps_c[1][:, :nb], AF.Copy, [bpc[1]], bcsb)
                        udst = uc[:, 1:1 + nb] if bi == 0 else ul[:, 1 + t0 - 256:1 + t0 - 256 + nb]
                        tt(udst, ps_c[2][:, :nb], csb[:, :nb], ALU.mult, [bpc[2], bcsb], bu)
                    for (u_, n, o0) in ((uc, TC, 0), (ul, TL, 256)):
                        ts(acc[:, :n], u_[:, 1:1 + n], cwt[:, l, 1, gc:gc + 1], ALU.mult, [bu, bc], bacc)
                        stt(acc[:, :n], u_[:, 0:n], cwt[:, l, 0, gc:gc + 1], acc[:, :n], ALU.mult, ALU.add, [bu, bc], bacc)
                        stt(acc[:, :n], u_[:, 2:2 + n], cwt[:, l, 2, gc:gc + 1], acc[:, :n], ALU.mult, ALU.add, [bu, bc], bacc)
                        tt(yout[:, o0:o0 + n], acc[:, :n], Bs[:, o0:o0 + n], ALU.mult, [bacc, bBs], byout)
                    dma("sp", yc_d[gc], yout[:], [byout], byc)
                S.barrier()
            tap_dram("yc", yc_d, 4, BF16, [byc])
            if stop == "D":
                LS.close()
                S.barrier()
                break

            with ExitStack() as st:
                wgt = sb(st, "wgt", [128, NG, 3072], BF16)
                wbr = sb(st, "wbr", [128, 3, 4, 1024], BF16)
                wo = sb(st, "wo", [128, NG, 1024], BF16)
                bwg3 = [Buf() for _ in range(3)]
                bwb3 = [Buf() for _ in range(3)]
                bwo = Buf()
                for br, wsrc in enumerate((w_ba, w_bb, w_bc)):
                    dma("pq", wgt[:, :, br * 1024:(br + 1) * 1024],
                        w_in[l, :, OFF["ga"] + br * 1024:OFF["ga"] + (br + 1) * 1024].rearrange("(kc p) c -> p kc c", p=128), [], bwg3[br])
                    dma("pq", wbr[:, br], wsrc[l].rearrange("(kc p) c -> p kc c", p=128), [], bwb3[br])
                dma("pq", wo[:], w_out[l].rearrange("(kc p) c -> p kc c", p=128), [], bwo)
                yblk = [sb(st, "yblk%d" % i, [128, 3, 4, 256], BF16) for i in range(2)]
                byblk = [Buf(), Buf()]
                xb2 = [sb(st, "xb2%d" % i, [128, NG, 256], F32) for i in range(2)]
                bxb2 = [Buf(), Buf()]
                mixT = sb(st, "mixT", [128, NG, 256], BF16)
                bmix = Buf()
                ea = [sb(st, "ea%d" % i, [128, 256], F32) for i in range(6)]
                bea = [Buf() for _ in range(6)]
                macc = [sb(st, "macc%d" % i, [128, 256], F32) for i in range(2)]
                ctb = [sb(st, "ctb%d" % i, [128, 256], F32) for i in range(2)]
                bmacc = [Buf(), Buf()]
                bctb = [Buf(), Buf()]
                psb = [pst(st, "psb%d" % i, [128, 2, 256]) for i in range(6)]
                bpsb = [Buf() for _ in range(6)]
                ps_w = [pst(st, "ps_w%d" % i, [128, 512]) for i in range(2)]
                bpw = [Buf(), Buf()]
                item = 0
                print("M1 sbuf remaining", nc.sbuf_bytes_remaining)
                for bi, (t0, nb) in enumerate(BLOCKS256):
                    r = 1 if bi == 0 else 0
                    hbi = 0 if bi == 0 else 1 + (bi - 1) // 2
                    yb_ = yblk[bi % 2]
                    x_ = xb2[bi % 2]
                    for i, (src, rb) in enumerate(((ya_d, [bya[bi]]), (yb_d, [byb[bi]]), (yc_d, [byc]))):
                        dma("sp", yb_[:, i], src[:, :, t0:t0 + nb].rearrange("g p t -> p g t"), rb, byblk[bi % 2])
                    dma("sp", x_[:], xs[:, :, t0:t0 + nb].rearrange("g p t -> p g t"), [bxs[bi]], bxb2[bi % 2])
                    for m in range(NG):
                        ma = macc[m % 2]
                        bma = bmacc[m % 2]
                        for br in range(3):
                            k = item % 6
                            item += 1
                            pg = psb[k]
                            e_ = ea[k]
                            mm(pg[:, 0, :], [(wgt[:, kc, br * 1024 + m * 128:br * 1024 + (m + 1) * 128], hT[:, kc, t0:t0 + nb])
                                             for kc in range(NG)], [bwg3[br], bh[hbi]], bpsb[k])
                            mm(pg[:, 1, :], [(wbr[:, br, kc, m * 128:(m + 1) * 128], yb_[:, br, kc, :]) for kc in range(4)],
                               [bwb3[br], byblk[bi % 2]], bpsb[k])
                            act(e_[:], pg[:, 0, :], AF.Exp, [bpsb[k]], bea[k], scale=-1.0)
                            act(e_[:], e_[:], AF.Ln, [bea[k], bc], bea[k], scale=1.0, bias=one_ap)
                            act(e_[:], e_[:], AF.Exp, [bea[k]], bea[k], scale=-1.0)
                            if br == 0:
                                tt(ma[:], pg[:, 1, :], e_[:], ALU.mult, [bpsb[k], bea[k]], bma)
                            else:
                                cb_ = ctb[br % 2]
                                bcb = bctb[br % 2]
                                tt(cb_[:], pg[:, 1, :], e_[:], ALU.mult, [bpsb[k], bea[k]], bcb)
                                if br == 1:
                                    tt(ma[:], ma[:], cb_[:], ALU.add, [bma, bcb], bma)
                                else:
                                    tt(mixT[:, m, :], ma[:], cb_[:], ALU.add, [bma, bcb], bmix)
                    for mo in range(NG):
                        pw = ps_w[mo % 2]
                        mm(pw[:, :nb], [(wo[:, kc, mo * 128:(mo + 1) * 128], mixT[:, kc, :]) for kc in range(NG)],
                           [bwo, bmix], bpw[mo % 2])
                        stt(x_[:, mo, :], pw[:, :nb], mod(2, mo, r), x_[:, mo, :], ALU.mult, ALU.add,
                            [bpw[mo % 2], bmod, bxb2[bi % 2]], bxb2[bi % 2])
                    dma("sp", xs[:, :, t0:t0 + nb].rearrange("g p t -> p g t"), x_[:], [bxb2[bi % 2]], bxs[bi])
                S.barrier()

            LS.close()
            S.barrier()
            tap_dram("x1", xs, NG, F32, bxs)
            if stop == "M":
                break

            with ExitStack() as st:
                w1t = sb(st, "w1t", [128, NG, 4096], BF16)
                w2t = sb(st, "w2t", [128, 32, 1024], BF16)
                bw1 = [Buf() for _ in range(4)]
                bw2 = [Buf() for _ in range(4)]
                for i in range(4):
                    dma("pq", w1t[:, :, i * 1024:(i + 1) * 1024],
                        w1[l, :, i * 1024:(i + 1) * 1024].rearrange("(kc p) c -> p kc c", p=128), [], bw1[i])
                for i in range(4):
                    dma("pq", w2t[:, i * 8:(i + 1) * 8, :],
                        w2[l, i * 1024:(i + 1) * 1024, :].rearrange("(kc p) c -> p kc c", p=128), [], bw2[i])
                xe = [sb(st, "xe%d" % i, [128, NG, 256], F32) for i in range(2)]
                bxe = [Buf(), Buf()]
                sq2 = sb(st, "sq2", [128, NG, 256], BF16)
                rstd2 = sb(st, "rstd2", [128, 256], F32)
                tmp2 = [sb(st, "tmp2%d" % i, [128, 256], F32) for i in range(2)]
                btmp2 = [Buf(), Buf()]
                h2b = [sb(st, "h2%d" % i, [128, NG, 256], BF16) for i in range(2)]
                bh2b = [Buf(), Buf()]
                uT = sb(st, "uT", [128, 32, 256], BF16)
                buT = Buf()
                rl = [sb(st, "rl%d" % i, [128, 256], F32) for i in range(2)]
                brl = [Buf(), Buf()]
                ss2 = pst(st, "ss2", [128, 512])
                ps_u2 = [pst(st, "ps_u2%d" % i, [128, 512]) for i in range(2)]
                ps_w2 = [pst(st, "ps_w2%d" % i, [128, 512]) for i in range(2)]
                bsq2, bss2, brstd2 = Buf(), Buf(), Buf()
                bpu2 = [Buf(), Buf()]
                bpw2 = [Buf(), Buf()]

                def norm_stage(bi):
                    t0, nb = BLOCKS256[bi]
                    r = 1 if bi == 0 else 0
                    x_ = xe[bi % 2]
                    h2 = h2b[bi % 2]
                    dma("sp", x_[:], xs[:, :, t0:t0 + nb].rearrange("g p t -> p g t"), [bxs[bi]], bxe[bi % 2])
                    rms_rstd(st, lambda g: x_[:, g, :], NG, nb, sq2, ss2, bsq2, bss2, rstd2, brstd2, ones_b, 1.0 / D,
                             [bxe[bi % 2]])
                    for g in range(NG):
                        tp_ = tmp2[g % 2]
                        tt(tp_[:], x_[:, g, :], rstd2[:], ALU.mult, [bxe[bi % 2], brstd2], btmp2[g % 2])
                        act(h2[:, g, :], tp_[:], AF.Identity, [btmp2[g % 2], bav, bmod], bh2b[bi % 2],
                            scale=av[:, 1, r, g:g + 1], bias=mod(3, g, r))

                def w1_stage(bi):
                    t0, nb = BLOCKS256[bi]
                    h2 = h2b[bi % 2]
                    for j in range(32):
                        pu = ps_u2[j % 2]
                        mm(pu[:, :nb], [(w1t[:, kc, j * 128:(j + 1) * 128], h2[:, kc, :]) for kc in range(NG)],
                           [bw1[j // 8], bh2b[bi % 2]], bpu2[j % 2])
                        act(rl[j % 2][:], pu[:, :nb], AF.Relu, [bpu2[j % 2]], brl[j % 2])
                        tt(uT[:, j, :], rl[j % 2][:], rl[j % 2][:], ALU.mult, [brl[j % 2]], buT)

                def w2_stage(bi):
                    t0, nb = BLOCKS256[bi]
                    r = 1 if bi == 0 else 0
                    x_ = xe[bi % 2]
                    for mo in range(NG):
                        pw = ps_w2[mo % 2]
                        mm(pw[:, :nb], [(w2t[:, j, mo * 128:(mo + 1) * 128], uT[:, j, :]) for j in range(32)],
                           bw2 + [buT], bpw2[mo % 2])
                        stt(x_[:, mo, :], pw[:, :nb], mod(5, mo, r), x_[:, mo, :], ALU.mult, ALU.add,
                            [bpw2[mo % 2], bmod, bxe[bi % 2]], bxe[bi % 2])
                    dma("sp", xs[:, :, t0:t0 + nb].rearrange("g p t -> p g t"), x_[:], [bxe[bi % 2]], bxs[bi])

                norm_stage(0)
                for bi in range(17):
                    w1_stage(bi)
                    if bi + 1 < 17:
                        norm_stage(bi + 1)
                    w2_stage(bi)
                S.barrier()
            tap_dram("x2", xs, NG, F32, bxs)

        with ExitStack() as st:
            cp = [sb(st, "fcp%d" % i, [128, NG, 512], F32) for i in range(2)]
            bcp = [Buf(), Buf()]
            for bi in range(8):
                t0 = 256 + 512 * bi
                c_ = cp[bi % 2]
                dma("sp", c_[:], xs[:, :, t0:t0 + 512].rearrange("g p t -> p g t"), xs_bufs(t0, 512), bcp[bi % 2])
                final_ops.append(S.op("sp", lambda e: e.dma_start(
                    out=outT[:, :, t0 - 256:t0 + 256].rearrange("g p t -> p g t"), in_=c_[:]), reads=[bcp[bi % 2]]))
        S.finish(final_ops)
    print("instructions:", S.ninst)
    return nc


def _consts():
    c = np.zeros((128, 5, 128), np.float32)
    c[:, 0, :] = 1.0
    c[:, 1, :] = np.eye(128)
    c[:64, 2, :64] = 1.0
    c[64:, 2, 64:] = 1.0
    for m in range(128):
        blk, i = divmod(m, 32)
        pm = blk * 32 + (i + 16) % 32
        c[pm, 3, m] = 1.0
    s = np.arange(128)[:, None]
    t = np.arange(128)[None, :]
    same = (s // 64) == (t // 64)
    c[:, 4, :] = (same & (s <= t)).astype(np.float32)
    mb = (same & (s >= t)).astype(np.float32)
    return c.reshape(128, 640), mb


def _rope_tables():
    tpos = np.arange(TL)
    pos = np.stack([tpos // 64, tpos % 64], -1).astype(np.float32)
    half = 32
    inv = (10000.0 ** (-np.arange(0, half, 2, dtype=np.float32) / half)).astype(np.float32)
    ang = pos[:, :, None] * inv
    cos, sin = np.cos(ang), np.sin(ang)
    cT = np.zeros((128, TL), np.float32)
    sT = np.zeros((128, TL), np.float32)
    for p in range(128):
        d = p % 64
        axis, rem = divmod(d, 32)
        hf, j = divmod(rem, 16)
        cT[p] = cos[:, axis, j]
        sT[p] = sin[:, axis, j] * (-1.0 if hf == 0 else 1.0)
    return cT, sT


def _bias_table(rpb):
    Lr = rpb.shape[0]
    out = np.full((Lr, 8, 128, 3, 6, 256), NEG, np.float32)
    cols = np.arange(64)
    cstart = np.clip(cols - 8, 0, 48)
    for pat, qb in enumerate((0, 1, 15)):
        ts_ = int(np.clip(4 * qb - 4, 0, 52))
        for j in range(6):
            for kr in range(2):
                rk = ts_ + 2 * j + kr
                for qr in range(4):
                    rq = 4 * qb + qr
                    r0 = int(np.clip(rq - 4, 0, 56))
                    if not (r0 <= rk < r0 + 8):
                        continue
                    rix = rk - rq + 7
                    ck = np.arange(64)[:, None]
                    cq = np.arange(64)[None, :]
                    ok = (ck >= cstart[None, :]) & (ck < cstart[None, :] + 16)
                    cix = np.clip(ck - cq + 15, 0, 30)
                    vals = rpb[:, :, rix, :][:, :, cix]
                    blk = np.where(ok[None, None], vals, NEG)
                    out[:, :, kr * 64:(kr + 1) * 64, pat, j, qr * 64:(qr + 1) * 64] = blk
    return out.reshape(Lr, 8, 128, 3 * 6 * 256)


def prep_inputs(x, c, ctx, c_ctx, ada_w, ada_b, norm1_w, norm2_w, w_in, hgrn_lb_logits, hgrn_onorm_w,
                q_norm_w, k_norm_w, natten_rpb, conv_w, w_branch_a, w_branch_b, w_branch_c, w_out,
                mlp_w1, mlp_w2):
    f = lambda a: np.ascontiguousarray(np.asarray(a, np.float32))
    cst, mb = _consts()
    cT_, sT_ = _rope_tables()
    shared = {
        "ada_w": f(ada_w),
        "ada_bT": f(np.asarray(ada_b).reshape(L, 48, 128).transpose(2, 0, 1)),
        "nwT": f(np.stack([np.asarray(norm1_w).reshape(L, NG, 128), np.asarray(norm2_w).reshape(L, NG, 128)], 1)
                 .transpose(3, 0, 1, 2)),
        "w_in": f(w_in),
        "lbl": f(np.asarray(hgrn_lb_logits).reshape(2, L, 4, 128).transpose(3, 0, 1, 2)),
        "onw": f(np.asarray(hgrn_onorm_w).T),
        "qkw": f(np.stack([np.tile(np.asarray(q_norm_w), (1, 2)), np.tile(np.asarray(k_norm_w), (1, 2))], -1)
                 .transpose(1, 0, 2)),
        "bias_tab": f(_bias_table(np.asarray(natten_rpb, np.float32))),
        "cwT": f(np.asarray(conv_w).reshape(L, 3, 4, 128).transpose(3, 0, 1, 2)),
        "w_ba": f(w_branch_a), "w_bb": f(w_branch_b), "w_bc": f(w_branch_c), "w_out": f(w_out),
        "w1": f(mlp_w1), "w2": f(mlp_w2),
        "cosT": f(cT_), "sinT": f(sT_), "cst": f(cst), "cst2": f(mb),
    }
    maps = []
    xn, cn, ctxn, ccn = (np.asarray(a, np.float32) for a in (x, c, ctx, c_ctx))
    for b in range(xn.shape[0]):
        seq = np.concatenate([ctxn[b], xn[b]], 0)
        m = dict(shared)
        m["xT"] = f(seq.T.reshape(NG, 128, T))
        m["cT"] = f(np.stack([cn[b].reshape(NG, 128).T, ccn.reshape(NG, 128).T], -1))
        maps.append(m)
    return maps


def kernel(**inputs):
    maps = prep_inputs(**inputs)
    nc = build()
    res = run_bass_kernel_spmd(nc, maps, core_ids=list(range(8)))
    outs = [r["outT"].reshape(D, TL).T for r in res.results]
    return np.ascontiguousarray(np.stack(outs, 0).astype(np.float32))
```
